# Optimizing a Trainium2 kernel written in Bass

```python
import jax, jax.numpy as jnp
from jax import lax
import numpy as np

D_MODEL = 2048
BATCH = 2
SEQ = 4096
DEPTH = 1
DEC_BATCH = 2
DEC_SEQ = 8192
PAST_LEN = 128

N_META = 16
GRID_W = 64
MAX_KH = 8
KW = 16
D_POOL = D_MODEL // 2
D_ATTN = D_MODEL - D_POOL
N_POOL_GROUPS = 4
POOL_WINDOWS = (2, 4, 8, 16)
POOL_GROUP_DIM = D_POOL // N_POOL_GROUPS
HEAD_DIM = 64
N_HEADS = D_ATTN // HEAD_DIM
D_FF = ((8 * D_MODEL // 3 + 127) // 128) * 128
CONV_W = 3
LN_EPS = 1e-5
RPB_ROWS = 2 * MAX_KH - 1
RPB_COLS = 2 * KW - 1
DEEPNORM_ALPHA = float((2 * DEPTH) ** 0.25)
DEEPNORM_BETA = float((8 * DEPTH) ** -0.25)

kernel_name = "hybrid_pool_natten_encoder"


def layernorm(x, g, b):
    xf = x.astype(jnp.float32)
    mu = jnp.mean(xf, axis=-1, keepdims=True)
    var = jnp.mean(jnp.square(xf - mu), axis=-1, keepdims=True)
    y = (xf - mu) * lax.rsqrt(var + LN_EPS) * g.astype(jnp.float32) + b.astype(jnp.float32)
    return y.astype(x.dtype)


def multi_scale_pool(u, w_pool, pool_scale):
    B, L, _ = u.shape
    uf = u.astype(jnp.float32)
    cs = jnp.concatenate([jnp.zeros((B, 1, D_POOL), jnp.float32), jnp.cumsum(uf, axis=1)], axis=1)
    t = jnp.arange(L)
    diffs = []
    for g, w in enumerate(POOL_WINDOWS):
        sl = slice(g * POOL_GROUP_DIM, (g + 1) * POOL_GROUP_DIM)
        lo = jnp.clip(t - w // 2, 0, L)
        hi = jnp.clip(t - w // 2 + w, 0, L)
        cnt = (hi - lo).astype(jnp.float32)[None, :, None]
        mean = (cs[:, hi, sl] - cs[:, lo, sl]) / cnt
        diffs.append(mean - uf[:, :, sl])
    m = jnp.stack(diffs, axis=2).astype(u.dtype)
    y = jnp.einsum('blgc,gce->blge', m, w_pool).reshape(B, L, D_POOL)
    return y * pool_scale


def _softmax_split(s_loc, s_meta):
    n_loc = s_loc.shape[-1]
    p = jax.nn.softmax(jnp.concatenate([s_loc, s_meta], axis=-1), axis=-1)
    return p[..., :n_loc], p[..., n_loc:]


def neighbourhood_attention(q, k, v, rpb, meta_bias):
    B, L, H, Dh = q.shape
    T = L - N_META
    rows = T // GRID_W
    kh = min(MAX_KH, rows)
    scale = HEAD_DIM ** -0.5
    f32 = jnp.float32
    rpb = rpb.astype(f32)
    meta_b = meta_bias.astype(f32)[None, :, None, :]

    qm, km, vm = q[:, :N_META], k[:, :N_META], v[:, :N_META]
    qg = q[:, N_META:].reshape(B, rows, GRID_W, H, Dh)
    kg = k[:, N_META:].reshape(B, rows, GRID_W, H, Dh)
    vg = v[:, N_META:].reshape(B, rows, GRID_W, H, Dh)

    cols = jnp.arange(GRID_W)
    col_start = jnp.clip(cols - KW // 2, 0, GRID_W - KW)
    col_idx = col_start[:, None] + jnp.arange(KW)[None, :]
    dc_idx = col_idx - cols[:, None] + (KW - 1)

    def row_block(r):
        rs = jnp.clip(r - kh // 2, 0, rows - kh)
        k_rows = lax.dynamic_slice_in_dim(kg, rs, kh, axis=1)
        v_rows = lax.dynamic_slice_in_dim(vg, rs, kh, axis=1)
        k_win = k_rows[:, :, col_idx]
        v_win = v_rows[:, :, col_idx]
        q_row = lax.dynamic_index_in_dim(qg, r, axis=1, keepdims=False)
        dr_idx = rs + jnp.arange(kh) - r + (MAX_KH - 1)
        bias = rpb[:, dr_idx[None, :, None], dc_idx[:, None, :]]
        s_loc = jnp.einsum('bchd,bkcwhd->bhckw', q_row, k_win).astype(f32) * scale + bias[None]
        s_meta = jnp.einsum('bchd,bmhd->bhcm', q_row, km).astype(f32) * scale + meta_b
        p_loc, p_meta = _softmax_split(s_loc.reshape(B, H, GRID_W, kh * KW), s_meta)
        p_loc = p_loc.reshape(B, H, GRID_W, kh, KW).astype(v.dtype)
        out = (jnp.einsum('bhckw,bkcwhd->bchd', p_loc, v_win)
               + jnp.einsum('bhcm,bmhd->bchd', p_meta.astype(v.dtype), vm))
        return out

    y_grid = lax.map(row_block, jnp.arange(rows))
    y_grid = jnp.moveaxis(y_grid, 0, 1).reshape(B, T, H, Dh)

    k0 = kg[:, :kh, :KW]
    v0 = vg[:, :kh, :KW]
    bias0 = rpb[:, (MAX_KH - 1) + jnp.arange(kh)][:, :, (KW - 1) + jnp.arange(KW)]
    s_loc0 = jnp.einsum('bmhd,bkwhd->bhmkw', qm, k0).astype(f32) * scale + bias0[None, :, None]
    s_meta0 = jnp.einsum('bmhd,bnhd->bhmn', qm, km).astype(f32) * scale + meta_b
    p_loc0, p_meta0 = _softmax_split(s_loc0.reshape(B, H, N_META, kh * KW), s_meta0)
    p_loc0 = p_loc0.reshape(B, H, N_META, kh, KW).astype(v.dtype)
    y_meta = (jnp.einsum('bhmkw,bkwhd->bmhd', p_loc0, v0)
              + jnp.einsum('bhmn,bnhd->bmhd', p_meta0.astype(v.dtype), vm))
    return jnp.concatenate([y_meta, y_grid], axis=1)


def conv_gated_ffn(h, w_up, b_up, conv_w, conv_b, w_down):
    z = h @ w_up + b_up
    zp = jnp.pad(z, ((0, 0), (1, 1), (0, 0)))
    z = zp[:, :-2] * conv_w[0] + zp[:, 1:-1] * conv_w[1] + zp[:, 2:] * conv_w[2] + conv_b
    a, g = jnp.split(z, 2, axis=-1)
    return (a * jax.nn.gelu(g, approximate=False)) @ w_down


def encode(x, meta_tokens, ln_in_g, ln_in_b, w_in, w_pool, pool_scale, rpb, meta_bias,
           w_out, ln1_g, ln1_b, w_up, b_up, conv_w, conv_b, w_down, ln2_g, ln2_b):
    B, T, D = x.shape
    meta = jnp.broadcast_to(meta_tokens[None].astype(x.dtype), (B, N_META, D))
    h = layernorm(jnp.concatenate([meta, x], axis=1), ln_in_g, ln_in_b)
    L = h.shape[1]
    for l in range(DEPTH):
        proj = h @ w_in[l]
        u = proj[..., :D_POOL]
        q = proj[..., D_POOL:D_POOL + D_ATTN].reshape(B, L, N_HEADS, HEAD_DIM)
        k = proj[..., D_POOL + D_ATTN:D_POOL + 2 * D_ATTN].reshape(B, L, N_HEADS, HEAD_DIM)
        v = proj[..., D_POOL + 2 * D_ATTN:].reshape(B, L, N_HEADS, HEAD_DIM)
        y_pool = multi_scale_pool(u, w_pool[l], pool_scale[l])
        y_attn = neighbourhood_attention(q, k, v, rpb[l], meta_bias[l]).reshape(B, L, D_ATTN)
        mix = jnp.concatenate([y_pool, y_attn], axis=-1) @ w_out[l]
        h = layernorm(DEEPNORM_ALPHA * h + mix, ln1_g[l], ln1_b[l])
        ffn = conv_gated_ffn(h, w_up[l], b_up[l], conv_w[l], conv_b[l], w_down[l])
        h = layernorm(DEEPNORM_ALPHA * h + ffn, ln2_g[l], ln2_b[l])
    return h[:, N_META:]


def setup_inputs(seed: int = 0) -> dict:
    key = jax.random.key(seed)
    ks = jax.random.split(key, 20)
    f32 = jnp.float32

    def nrm(k, shape, s):
        return jax.random.normal(k, shape, f32) * s

    n_in = D_POOL + 3 * D_ATTN
    col_scale = jnp.concatenate([jnp.ones((D_POOL + 2 * D_ATTN,), f32),
                                 jnp.full((D_ATTN,), DEEPNORM_BETA, f32)])
    return {
        "x_prompt": nrm(ks[0], (BATCH, SEQ, D_MODEL), 1.0),
        "x_sample": nrm(ks[1], (DEC_BATCH, DEC_SEQ, D_MODEL), 1.0),
        "meta_tokens": nrm(ks[2], (N_META, D_MODEL), 1.0),
        "ln_in_g": 1.0 + nrm(ks[3], (D_MODEL,), 0.02),
        "ln_in_b": nrm(ks[4], (D_MODEL,), 0.02),
        "w_in": nrm(ks[5], (DEPTH, D_MODEL, n_in), D_MODEL ** -0.5) * col_scale,
        "w_pool": nrm(ks[6], (DEPTH, N_POOL_GROUPS, POOL_GROUP_DIM, POOL_GROUP_DIM), POOL_GROUP_DIM ** -0.5),
        "pool_scale": 1.0 + nrm(ks[7], (DEPTH, D_POOL), 0.02),
        "rpb": nrm(ks[8], (DEPTH, N_HEADS, RPB_ROWS, RPB_COLS), 0.1),
        "meta_bias": nrm(ks[9], (DEPTH, N_HEADS, N_META), 0.1),
        "w_out": nrm(ks[10], (DEPTH, D_POOL + D_ATTN, D_MODEL), (D_POOL + D_ATTN) ** -0.5 * DEEPNORM_BETA),
        "ln1_g": 1.0 + nrm(ks[11], (DEPTH, D_MODEL), 0.02),
        "ln1_b": nrm(ks[12], (DEPTH, D_MODEL), 0.02),
        "w_up": nrm(ks[13], (DEPTH, D_MODEL, 2 * D_FF), D_MODEL ** -0.5),
        "b_up": nrm(ks[14], (DEPTH, 2 * D_FF), 0.01),
        "conv_w": nrm(ks[15], (DEPTH, CONV_W, 2 * D_FF), CONV_W ** -0.5),
        "conv_b": nrm(ks[16], (DEPTH, 2 * D_FF), 0.01),
        "w_down": nrm(ks[17], (DEPTH, D_FF, D_MODEL), D_FF ** -0.5 * DEEPNORM_BETA),
        "ln2_g": 1.0 + nrm(ks[18], (DEPTH, D_MODEL), 0.02),
        "ln2_b": nrm(ks[19], (DEPTH, D_MODEL), 0.02),
    }


def reference(x_prompt, x_sample, meta_tokens, ln_in_g, ln_in_b, w_in, w_pool, pool_scale, rpb,
              meta_bias, w_out, ln1_g, ln1_b, w_up, b_up, conv_w, conv_b, w_down, ln2_g, ln2_b):
    y_prompt = encode(x_prompt, meta_tokens, ln_in_g, ln_in_b, w_in, w_pool, pool_scale, rpb,
                      meta_bias, w_out, ln1_g, ln1_b, w_up, b_up, conv_w, conv_b, w_down, ln2_g, ln2_b)
    y_sample = encode(x_sample, meta_tokens, ln_in_g, ln_in_b, w_in, w_pool, pool_scale, rpb,
                      meta_bias, w_out, ln1_g, ln1_b, w_up, b_up, conv_w, conv_b, w_down, ln2_g, ln2_b)
    return (y_prompt, y_sample)
```

```python
import contextlib
import os
import numpy as np
import concourse.bass as bass
import concourse.mybir as mybir
from concourse.bass_utils import run_bass_kernel_spmd

F32 = mybir.dt.float32
BF = mybir.dt.bfloat16
AF = mybir.ActivationFunctionType
ALU = mybir.AluOpType

D = 2048
DC = 16
NCORE = 8
NU = 6
TOK = 1168
NTILE = 10
NM = 514
NUC = 530
TAU_M = 383
TAU_U = 375
DFF = 5504
NFC = 43
BIG = 30000.0
ALPHA = float(2.0 ** 0.25)
LN_EPS = 1e-5
N_META = 16

C_LNIN_G, C_LNIN_B, C_LN1_G, C_LN1_B, C_LN2_G, C_LN2_B = 0, 16, 32, 48, 64, 80
C_PSCALE = 96
C_BUP = 104
C_CW = 190
C_CB = 448
C_METAB = 534
C_GH = 550
NCONST = 790
F_BTAB, F_FH, F_FPOST, F_CORR = 0, 36, 276, 277
NFLAG = 309
NHALO = 15

ENGS = ("pe", "act", "dve", "pool", "sp")


class Op:
    __slots__ = ("eng", "fn", "deps", "is_dma", "dsem", "needs_inc", "semval")

    def __init__(self, eng, fn, dsem=None):
        self.eng = eng
        self.fn = fn
        self.deps = ()
        self.is_dma = dsem is not None
        self.dsem = dsem
        self.needs_inc = False
        self.semval = None


class Prog:
    def __init__(self, nc):
        self.nc = nc
        self.ops = {e: [] for e in ENGS}
        self.last_w = {}
        self.readers = {}
        self.reg_of = {}
        self.reg_cur = {}
        self.reg_fence = {}
        self.dma_cnt = {}
        self.excl = {}
        self.pe_tags = []
        self.cur_tag = "setup"
        self.trace = None

    def region(self, reg, names):
        for n in names:
            self.reg_of[n] = reg

    def fence(self, reg):
        cur = self.reg_cur.get(reg)
        if cur:
            self.reg_fence[reg] = dict(cur)
            self.reg_cur[reg] = {}

    @staticmethod
    def _key(o):
        return ("dma", id(o)) if o.is_dma else o.eng

    def op(self, eng, fn, reads=(), writes=(), dsem=None, extra_deps=()):
        o = Op(eng, fn, dsem)
        if eng == "pe":
            self.pe_tags.append(self.cur_tag)
        deps = {id(x): x for x in extra_deps}
        regs = set()
        px = [r for r in tuple(reads) + tuple(writes) if isinstance(r, tuple) and r[0] == "ps"]
        if px:
            reads = [r for r in reads if not (isinstance(r, tuple) and r[0] == "ps")]
            writes = [r for r in writes if not (isinstance(r, tuple) and r[0] == "ps")]
            for r in px:
                prev = self.excl.get(r)
                if prev is not None and prev.eng != eng:
                    deps[id(prev)] = prev
                self.excl[r] = o
        for r in reads:
            w = self.last_w.get(r)
            if w is not None:
                deps[id(w)] = w
            nm = r[0] if isinstance(r, tuple) else r
            rg = self.reg_of.get(nm)
            if rg is not None:
                regs.add(rg)
        for r in writes:
            w = self.last_w.get(r)
            if w is not None:
                deps[id(w)] = w
            rd = self.readers.get(r)
            if rd:
                for x in rd.values():
                    deps[id(x)] = x
            nm = r[0] if isinstance(r, tuple) else r
            rg = self.reg_of.get(nm)
            if rg is not None:
                regs.add(rg)
        for rg in regs:
            f = self.reg_fence.get(rg)
            if f:
                for x in f.values():
                    deps[id(x)] = x
            self.reg_cur.setdefault(rg, {})[self._key(o)] = o
        o.deps = tuple(deps.values())
        k = self._key(o)
        for r in reads:
            self.readers.setdefault(r, {})[k] = o
        for r in writes:
            self.last_w[r] = o
            self.readers[r] = {}
        self.ops[eng].append(o)
        return o

    def emit(self, final_wait_ops=()):
        nc = self.nc
        for o in final_wait_ops:
            if not o.is_dma:
                o.needs_inc = True
        for e in ENGS:
            for o in self.ops[e]:
                for d in o.deps:
                    if d.is_dma:
                        continue
                    if d.eng == o.eng and d.eng == "pe":
                        continue
                    d.needs_inc = True
        for e in ENGS:
            c = 0
            for o in self.ops[e]:
                if o.is_dma:
                    self.dma_cnt[o.dsem] = self.dma_cnt.get(o.dsem, 0) + 16
                    o.semval = self.dma_cnt[o.dsem]
                elif o.needs_inc:
                    c += 1
                    o.semval = c
        with contextlib.ExitStack() as st:
            esem = {e: st.enter_context(nc.semaphore("prog_" + e)) for e in ENGS}
            dsems = {}
            for i, k in enumerate(sorted(self.dma_cnt.keys(), key=str)):
                dsems[k] = st.enter_context(nc.semaphore("dma%d" % i))
            block = st.enter_context(nc.Block())

            def run(ename, eng):
                seen = {}
                for o in self.ops[ename]:
                    need = {}
                    for d in o.deps:
                        if d.is_dma:
                            s = ("d", d.dsem)
                        else:
                            if d.eng == ename and ename == "pe":
                                continue
                            s = ("e", d.eng)
                        if need.get(s, 0) < d.semval:
                            need[s] = d.semval
                    for s, v in need.items():
                        if seen.get(s, 0) >= v:
                            continue
                        seen[s] = v
                        eng.wait_ge(dsems[s[1]] if s[0] == "d" else esem[s[1]], v)
                        if self.trace is not None:
                            self.trace.append((ename, "wait", s, v))
                    ins = o.fn(eng)
                    if self.trace is not None:
                        self.trace.append((ename, "op", getattr(o, "tag", None), (o.dsem if o.is_dma else ("inc" if o.needs_inc else None)), o.semval))
                    if o.is_dma:
                        ins.then_inc(dsems[o.dsem], 16)
                    elif o.needs_inc:
                        ins.then_inc(esem[ename], 1)
                if ename == "sp":
                    for o in final_wait_ops:
                        eng.wait_ge(dsems[o.dsem] if o.is_dma else esem[o.eng], o.semval)

            @block.tensor
            def _(e):
                run("pe", e)

            @block.scalar
            def _(e):
                run("act", e)

            @block.vector
            def _(e):
                run("dve", e)

            @block.gpsimd
            def _(e):
                run("pool", e)

            @block.sync
            def _(e):
                run("sp", e)


def MM(out, lhsT, rhs, start, stop):
    return lambda e: e.matmul(out, lhsT=lhsT, rhs=rhs, start=start, stop=stop)


def TR(out, in_, ident):
    return lambda e: e.transpose(out=out, in_=in_, identity=ident)


def ACTF(out, in_, func, bias=None, scale=None):
    kw = {}
    if bias is not None:
        kw["bias"] = bias
    if scale is not None:
        kw["scale"] = scale
    return lambda e: e.activation(out=out, in_=in_, func=func, **kw)


def STT(out, in0, scalar, in1, op0, op1):
    return lambda e: e.scalar_tensor_tensor(out=out, in0=in0, scalar=scalar, in1=in1, op0=op0, op1=op1)


def TT(out, in0, in1, op):
    return lambda e: e.tensor_tensor(out=out, in0=in0, in1=in1, op=op)


def TS(out, in0, s1, s2, op0, op1=None):
    if op1 is None:
        return lambda e: e.tensor_scalar(out=out, in0=in0, scalar1=s1, scalar2=None, op0=op0)
    return lambda e: e.tensor_scalar(out=out, in0=in0, scalar1=s1, scalar2=s2, op0=op0, op1=op1)


def CP(out, in_):
    return lambda e: e.tensor_copy(out=out, in_=in_)


def ACP(out, in_):
    return lambda e: e.copy(out=out, in_=in_)


def DMA(out, in_):
    return lambda e: e.dma_start(out=out, in_=in_)


ST_I, ST_S, ST_E, ST_NS, ST_NE, ST_NV = 0, 1, 2, 3, 4, 5


def key_status_own(kap, rho):
    d = kap - rho
    if kap == -6:
        return ST_NV
    if 0 <= kap <= 7:
        if -4 <= d <= 3:
            return ST_I
        return ST_S if d >= 4 else ST_E
    if kap < 0:
        return ST_NS if d >= -4 else ST_NV
    return ST_NE if d <= 3 else ST_NV


def pair_plan(j):
    k0 = 2 * j - 6
    per = []
    for rho in range(8):
        st, sb = key_status_own(k0, rho), key_status_own(k0 + 1, rho)
        per.append(None if (st == ST_NV and sb == ST_NV) else st * 6 + sb)
    rows = [r for r in range(8) if per[r] is not None]
    if not rows:
        return None
    ra, rb = rows[0], rows[-1]
    segs = []
    r = ra
    while r <= rb:
        assert per[r] is not None
        r2 = r
        while r2 + 1 <= rb and per[r2 + 1] == per[r]:
            r2 += 1
        segs.append((r, r2, per[r]))
        r = r2 + 1
    return ra, rb, segs


def halo_items():
    items = []
    for j in range(0, 5):
        items.append(("preI", j))
    for j in range(3, 7):
        items.append(("preS", j))
    for j in range(5, 9):
        items.append(("post", j))
    items.append(("metapre", None))
    items.append(("metapost", None))
    assert len(items) == NHALO
    return items


def halo_status(kind, kap):
    if kind == "preI":
        return ST_NS if -5 <= kap <= 2 else ST_NV
    if kind == "preS":
        return ST_S if 0 <= kap <= 7 else ST_NV
    if kind == "post":
        if 4 <= kap <= 7:
            return ST_I
        return ST_NE if 8 <= kap <= 11 else ST_NV
    raise ValueError


def halo_geom(kind, j):
    k0 = 2 * j - 6
    if kind == "preI":
        return -1 - k0, 63
    if kind == "preS":
        return 0 - k0, 0
    return 8 - k0, 0


def build_program(NU=NU, stop_after=None):
    nc = bass.Bass("TRN2", target_bir_lowering=False)
    dt_in = lambda n, s: nc.dram_tensor(n, s, F32, kind="ExternalInput").ap()
    xu = dt_in("xu", [NU, TOK, D])
    w_in = dt_in("w_in", [D, 4096])
    w_out = dt_in("w_out", [D, D])
    w_up = dt_in("w_up", [D, 2 * DFF])
    w_down = dt_in("w_down", [DFF, D])
    w_pool = dt_in("w_pool", [4, 256, 256])
    gp_d = dt_in("gp", [8, 128, 2048])
    cst_d = dt_in("cst", [128, NCONST])
    flg_d = dt_in("flg", [NU, 128, NFLAG])
    y_d = nc.dram_tensor("y", [NU, 512, D], F32, kind="ExternalOutput").ap()
    dbg_d = None
    if stop_after is not None:
        dbg_d = {"A": nc.dram_tensor("dbgA", [128, 9344], F32, kind="ExternalOutput").ap(),
                 "H": nc.dram_tensor("dbgH", [128, DC * NM], F32, kind="ExternalOutput").ap(),
                 "B": nc.dram_tensor("dbgB", [128, 16536], F32, kind="ExternalOutput").ap(),
                 "Y": nc.dram_tensor("dbgY", [128, 6168], F32, kind="ExternalOutput").ap()}
    wb_in = nc.dram_tensor("wb_in", [16, 128, 4096], BF, kind="Internal").ap()
    wb_out = nc.dram_tensor("wb_out", [8, 128, 4096], BF, kind="Internal").ap()
    wb_up = nc.dram_tensor("wb_up", [NFC, 128, 4096], BF, kind="Internal").ap()
    wb_dn = nc.dram_tensor("wb_dn", [32, 128, 22 * 128], BF, kind="Internal").ap()

    with contextlib.ExitStack() as st:
        def sb(n, words):
            return st.enter_context(nc.sbuf_tensor(n, [128, words], F32))

        RA = sb("RA", 9344)
        RH = sb("RH", DC * NM)
        RB = sb("RB", 16536)
        RY = sb("RY", 6168)
        RS = sb("RS", 4 * 2048)
        cst = sb("cst_sb", NCONST)
        flg = sb("flg_sb", 2 * NFLAG)
        ident = sb("ident", 128)
        onesw = sb("onesw", 64)
        wpw = sb("wpw", 1024)
        small = sb("small", 96)
        ps = [st.enter_context(nc.psum_tensor("ps%d" % i, [128, 512], F32)) for i in range(8)]

        def vbf(t, off_w, shape):
            n = int(np.prod(shape))
            a = t.bitcast(BF)[:, 2 * off_w: 2 * off_w + n]
            if len(shape) == 1:
                return a
            names = "abcd"[:len(shape)]
            pat = "p (%s) -> p %s" % (" ".join(names), " ".join(names))
            return a.rearrange(pat, **{names[i]: shape[i] for i in range(len(shape) - 1)})

        def vf(t, off_w, shape):
            n = int(np.prod(shape))
            a = t[:, off_w: off_w + n]
            if len(shape) == 1:
                return a
            names = "abcd"[:len(shape)]
            pat = "p (%s) -> p %s" % (" ".join(names), " ".join(names))
            return a.rearrange(pat, **{names[i]: shape[i] for i in range(len(shape) - 1)})

        h0bf = vbf(RA, 0, [DC, TOK])
        gpb = [vf(RA, 0, [2, 1024]), vf(RA, 2048, [2, 1024])]
        sbt = [vf(RA, 4096, [512]), vf(RA, 4608, [512]), vf(RA, 5120, [512])]
        ptb = [vbf(RA, 5632, [512]), vbf(RA, 5888, [512]), vbf(RA, 6144, [512]), vbf(RA, 6400, [512]),
               vbf(RA, 8100, [512]), vbf(RA, 8356, [512]), vbf(RA, 8612, [512]), vbf(RA, 8868, [512])]
        sbh = vf(RA, 6656, [240])
        pth = vbf(RA, 6896, [240])
        rech = vf(RA, 7016, [16])
        pt1 = vf(RA, 7040, [NUC])
        pt2 = vf(RA, 7040 + NUC, [NUC])
        h1bf = vbf(RA, 0, [DC, NM])
        lxb = [vbf(RA, 4112 + 257 * i, [NM]) for i in range(3)]
        lxq = [vbf(RA, 4112 + 771 + 257 * i, [NM]) for i in range(3)]
        lmean = vf(RA, 5654, [NM])
        lmsq = vf(RA, 5654 + NM, [NM])
        lrstd = vf(RA, 5654 + 2 * NM, [NM])
        ltmp = [vf(RA, 7196 + NM * i, [NM]) for i in range(4)]
        h0f = vf(RH, 0, [DC, NM])
        KT = vbf(RB, 0, [8, 1280])
        Vt = vbf(RB, 5120, [NTILE, 1024])
        QT = vbf(RB, 10240, [8, NM])
        Ut = vf(RB, 12296, [8, NUC])
        actb = vbf(RB, 10240, [22, 512])
        xt = [vf(RY, 0, [D]), vf(RY, 2048, [D]), vf(RY, 4096, [D])]
        mT = vbf(RY, 0, [8, NM])
        ypool = vbf(RY, 2056, [8, NM])
        yattn = vbf(RY, 4112, [8, NM])
        zb = [vf(RY, 0, [2, NM]), vf(RY, 1028, [2, NM])]
        cab = [vf(RY, 2056, [512]), vf(RY, 2568, [512])]
        cgb = [vf(RY, 3080, [512]), vf(RY, 3592, [512])]
        ggb = [vf(RY, 4104, [512]), vf(RY, 4616, [512])]
        otile = [vf(RY, 0, [D]), vf(RY, 2048, [D])]
        slab = [vbf(RS, 2048 * i, [4096]) for i in range(4)]
        onesb = vbf(onesw, 0, [128])
        wp = vbf(wpw, 0, [4, 2, 256])
        stt_l = [vf(small, 32 * i, [4, 6]) for i in range(3)]
        mv_l = [vf(small, 32 * i + 24, [2]) for i in range(3)]
        rstd_l = [vf(small, 32 * i + 26, [1]) for i in range(3)]
        nmr_l = [vf(small, 32 * i + 27, [1]) for i in range(3)]

        P = Prog(nc)
        nc._prog = P
        if "trace" in os.environ.get("KDBG", ""):
            P.trace = []
            nc._ptrace = P.trace
        P.region("A", ["h0bf", "gpb", "sbt", "ptb", "sbh", "pth", "rech", "pt1", "pt2", "h1bf", "lxb", "lxq",
                       "lstat", "ltmp"])
        P.region("B", ["KT", "V", "QT", "U", "act"])
        P.region("Y", ["xt", "m", "ypool", "yattn", "z", "ca", "cg", "gg", "otile"])

        bank_ctr = [0]

        reserved = set()

        def nb():
            while True:
                b = bank_ctr[0] % 8
                bank_ctr[0] += 1
                if b not in reserved:
                    return b

        cc = lambda col: cst[:, col:col + 1]

        P.op("sp", DMA(cst[:], cst_d[:]), writes=["cst"], dsem="cst")
        P.op("dve", lambda e: e.memset(ident[:], 0.0), writes=["ident"])
        P.op("pool", lambda e: e.affine_select(out=ident[:], in_=ident[:], pattern=[[-1, 128]],
                                               compare_op=ALU.not_equal, fill=1.0, base=0, channel_multiplier=1),
             reads=["ident"], writes=["ident"])
        P.op("dve", lambda e: e.memset(onesb, 1.0), writes=["ones"])
        P.op("dve", lambda e: e.memset(RB[:, :], 0.0), writes=[("KT", c) for c in range(8)] + [("V", 9, q) for q in range(4)])

        cv_n = [0]


        dbgflags = os.environ.get("KDBG", "")

        def conv_dma(out, in_, res):
            if "noconv" in dbgflags:
                return
            if "only_" in dbgflags and ("only_" + res[0]) not in dbgflags:
                return
            n = cv_n[0]
            cv_n[0] += 1
            P.op("pool", DMA(out, in_), writes=[res, ("cvslot", n % 5)], dsem=("cv", n % 5))

        w_in_v = w_in.rearrange("(kc p) n -> p kc n", p=128)
        w_out_v = w_out.rearrange("(kc p) n -> p kc n", p=128)
        w_up_v = w_up.rearrange("(kc p) n -> p kc n", p=128)
        w_dn_v = w_down.rearrange("(kc p) n -> p kc n", p=128)
        for s in range(16):
            conv_dma(wb_in[s].rearrange("p (kc n) -> p kc n", kc=16), w_in_v[:, :, 256 * s:256 * s + 256], ("wb_in", s))
        for g in range(4):
            if "nowp" in dbgflags:
                continue
            P.op("pool", DMA(wp[:, g], w_pool[g].rearrange("(kc p) e -> p kc e", p=128)), writes=[("wp", g)],
                 dsem=("wpl", g))
        for s in range(8):
            conv_dma(wb_out[s].rearrange("p (kc n) -> p kc n", kc=16), w_out_v[:, :, 256 * s:256 * s + 256], ("wb_out", s))
        def conv_up(j):
            ov = wb_up[j].rearrange("p (kc n) -> p kc n", kc=16)
            conv_dma(ov[:, :, 0:128], w_up_v[:, :, 128 * j:128 * j + 128], ("wb_up", j, 0))
            conv_dma(ov[:, :, 128:256], w_up_v[:, :, DFF + 128 * j:DFF + 128 * j + 128], ("wb_up", j, 1))

        def conv_dn(oc, part):
            k0, nk = (0, 22) if part == 0 else (22, 21)
            ov = wb_dn[2 * oc + part].rearrange("p (kc n) -> p kc n", kc=22)
            conv_dma(ov[:, 0:nk, :], w_dn_v[:, k0:k0 + nk, 128 * oc:128 * oc + 128], ("wb_dn", 2 * oc + part))

        for part in range(2):
            for j in (range(0, 22) if part == 0 else range(22, NFC)):
                conv_up(j)
            for oc in range(16):
                conv_dn(oc, part)

        slab_list = []
        for u in range(NU):
            for s in range(16):
                slab_list.append((wb_in[s], 4096, [("wb_in", s)]))
            for s in range(8):
                slab_list.append((wb_out[s], 4096, [("wb_out", s)]))
            for part in range(2):
                for j in (range(0, 22) if part == 0 else range(22, NFC)):
                    slab_list.append((wb_up[j], 4096, [("wb_up", j, 0), ("wb_up", j, 1)]))
                nk = 22 if part == 0 else 21
                for oc in range(DC):
                    slab_list.append((wb_dn[2 * oc + part][:, 0:nk * 128], nk * 128, [("wb_dn", 2 * oc + part)]))
        slab_issued = [0]
        slab_cur = [0]

        def prefetch_slabs():
            i = slab_cur[0]
            while slab_issued[0] < min(len(slab_list), i + 4):
                n = slab_issued[0]
                src, ncol, res = slab_list[n]
                P.op("sp", DMA(slab[n % 4][:, 0:ncol], src), reads=res, writes=[("slab", n % 4)],
                     dsem=("slab", n % 4))
                slab_issued[0] += 1

        def next_slab(ahead=4):
            i = slab_cur[0]
            slab_cur[0] += 1
            while slab_issued[0] < min(len(slab_list), i + ahead):
                n = slab_issued[0]
                src, ncol, res = slab_list[n]
                P.op("sp", DMA(slab[n % 4][:, 0:ncol], src), reads=res, writes=[("slab", n % 4)],
                     dsem=("slab", n % 4))
                slab_issued[0] += 1
            return i % 4, ("slab", i % 4)

        def ln_begin(pieces):
            banks = [(nb(), nb()) for _ in pieces]
            for b1, b2 in banks:
                reserved.add(b1)
                reserved.add(b2)
            return {"pieces": pieces, "banks": banks}

        def ln_chunk(stt, c):
            r = c % 3
            P.op("act", ACP(lxb[r][:, 0:NM], h0f[:, c, :]), reads=[("h0f", c)], writes=[("lxb", r)])
            P.op("act", ACTF(lxq[r][:, 0:NM], h0f[:, c, :], AF.Square), reads=[("h0f", c)], writes=[("lxq", r)])
            for pi, (c0, n) in enumerate(stt["pieces"]):
                b1, b2 = stt["banks"][pi]
                P.op("pe", MM(ps[b1][:, 0:n], onesb, lxb[r][:, c0:c0 + n], c == 0, c == DC - 1),
                     reads=[("lxb", r), "ones"], writes=[("ps", b1)])
                P.op("pe", MM(ps[b2][:, 0:n], onesb, lxq[r][:, c0:c0 + n], c == 0, c == DC - 1),
                     reads=[("lxq", r), "ones"], writes=[("ps", b2)])

        def ln_finish(stt, gcol, bcol, out_bf):
            pieces, banks = stt["pieces"], stt["banks"]
            for pi, (c0, n) in enumerate(pieces):
                b1, b2 = banks[pi]
                sl = slice(c0, c0 + n)
                P.op("dve", TS(lmean[:, sl], ps[b1][:, 0:n], 1.0 / D, None, ALU.mult), reads=[("ps", b1)],
                     writes=[("lstat", "mean", pi)])
                P.op("dve", TT(lmsq[:, sl], lmean[:, sl], lmean[:, sl], ALU.mult), reads=[("lstat", "mean", pi)],
                     writes=[("lstat", "msq", pi)])
                P.op("dve", STT(lrstd[:, sl], ps[b2][:, 0:n], 1.0 / D, lmsq[:, sl], ALU.mult, ALU.subtract),
                     reads=[("ps", b2), ("lstat", "msq", pi)], writes=[("lstat", "rstd", pi)])
                P.op("act", ACTF(lrstd[:, sl], lrstd[:, sl], AF.Sqrt, bias=LN_EPS),
                     reads=[("lstat", "rstd", pi)], writes=[("lstat", "rstd", pi)])
                P.op("dve", lambda e, sl=sl: e.reciprocal(out=lrstd[:, sl], in_=lrstd[:, sl]),
                     reads=[("lstat", "rstd", pi)], writes=[("lstat", "rstd", pi)])
                reserved.discard(b1)
                reserved.discard(b2)
            lo = pieces[0][0]
            hi = pieces[-1][0] + pieces[-1][1]
            sl = slice(lo, hi)
            srd = [("lstat", "mean", pi) for pi in range(len(pieces))] + [("lstat", "rstd", pi) for pi in range(len(pieces))]
            for c in range(DC):
                eng_ = "dve"
                r = c % 4
                P.op(eng_, TT(ltmp[r][:, sl], h0f[:, c, sl], lmean[:, sl], ALU.subtract),
                     reads=[("h0f", c)] + srd, writes=[("ltmp", r)])
                P.op(eng_, TT(ltmp[r][:, sl], ltmp[r][:, sl], lrstd[:, sl], ALU.mult),
                     reads=[("ltmp", r)] + srd, writes=[("ltmp", r)])
                P.op("act", ACTF(h0f[:, c, sl], ltmp[r][:, sl], AF.Identity, bias=cc(bcol + c), scale=cc(gcol + c)),
                     reads=[("ltmp", r), "cst"], writes=[("h0f", c)])
                if out_bf:
                    P.op("act", ACTF(h1bf[:, c, sl], ltmp[r][:, sl], AF.Identity, bias=cc(bcol + c), scale=cc(gcol + c)),
                         reads=[("ltmp", r), "cst"], writes=[("h1bf", c)])

        items_h = halo_items()
        plans = [pair_plan(j) for j in range(9)]
        out_ops = []

        class _Stop(Exception):
            pass

        def check_stop(name):
            if stop_after != name:
                return
            lasts = [P.ops[e][-1] for e in ("pe", "act", "dve") if P.ops[e]]
            for nm, t in (("A", RA), ("H", RH), ("B", RB), ("Y", RY)):
                o = P.op("sp", DMA(dbg_d[nm][:, :], t[:, :]), dsem=("dbg", nm), extra_deps=lasts)
                out_ops.append(o)
            raise _Stop()

        CONT = (1, 3, 4, 5)
        for u in range(NU):
          try:
            check_stop("setup")
            fb = (u % 2) * NFLAG
            fcol = lambda col: flg[:, fb + col: fb + col + 1]
            P.op("sp", DMA(flg[:, fb:fb + NFLAG], flg_d[u]), writes=[("flg", u % 2)], dsem=("flg", u % 2))
            FL = ("flg", u % 2)

            P.cur_tag = "P1a"
            def stage_a(t):
                    p = 128 if t < 9 else 16
                    xb_ = t % 3
                    stt_t, mv, rstd_s, nmr_s = stt_l[xb_], mv_l[xb_], rstd_l[xb_], nmr_l[xb_]
                    tok0 = t * 128
                    P.op("sp", DMA(xt[xb_][0:p, :], xu[u, tok0:tok0 + p, :]), writes=[("xt", xb_)], dsem=("x", xb_))
                    for q in range(4):
                        P.op("dve", lambda e, q=q, p=p, xb_=xb_, stt_t=stt_t: e.bn_stats(out=stt_t[0:p, q, :], in_=xt[xb_][0:p, 512 * q:512 * q + 512]),
                             reads=[("xt", xb_)], writes=[("bnst", xb_, q)])
                    P.op("dve", lambda e, p=p, stt_t=stt_t, mv=mv: e.bn_aggr(out=mv[0:p, :], in_=stt_t[0:p].rearrange("p a b -> p (a b)")),
                         reads=[("bnst", xb_, q) for q in range(4)], writes=[("mv", xb_)])
                    P.op("act", ACTF(rstd_s[0:p, :], mv[0:p, 1:2], AF.Sqrt, bias=LN_EPS), reads=[("mv", xb_)], writes=[("rstd_s", xb_)])
                    P.op("dve", lambda e, p=p, rstd_s=rstd_s: e.reciprocal(out=rstd_s[0:p, :], in_=rstd_s[0:p, :]),
                         reads=[("rstd_s", xb_)], writes=[("rstd_s", xb_)])
                    P.op("dve", TS(nmr_s[0:p, :], mv[0:p, 0:1], rstd_s[0:p, 0:1], -1.0, ALU.mult, ALU.mult),
                         reads=[("mv", xb_), ("rstd_s", xb_)], writes=[("nmr_s", xb_)])
                    P.op("act", ACTF(xt[xb_][0:p, :], xt[xb_][0:p, :], AF.Identity, bias=nmr_s[0:p, 0:1], scale=rstd_s[0:p, 0:1]),
                         reads=[("xt", xb_), ("rstd_s", xb_), ("nmr_s", xb_)], writes=[("xt", xb_)])

            def stage_b(t):
                    p = 128 if t < 9 else 16
                    xb_ = t % 3
                    tok0 = t * 128
                    ja = max(0, tok0 - TAU_M)
                    jb = min(NM, tok0 + p - TAU_M)
                    for g4 in range(4):
                        b = nb()
                        eng_ = "act" if (g4 + t) % 2 == 0 else "dve"
                        for k in range(4):
                            c = 4 * g4 + k
                            P.op("pe", TR(ps[b][:, k * 128:k * 128 + p], xt[xb_][0:p, c * 128:(c + 1) * 128], ident[0:p, 0:p]),
                                 reads=[("xt", xb_), "ident"], writes=[("ps", b)])
                        for k in range(4):
                            c = 4 * g4 + k
                            src = ps[b][:, k * 128:k * 128 + p]
                            if eng_ == "act":
                                P.op("act", ACTF(h0bf[:, c, tok0:tok0 + p], src, AF.Identity,
                                                 bias=cc(C_LNIN_B + c), scale=cc(C_LNIN_G + c)),
                                     reads=[("ps", b), "cst"], writes=[("h0bf", c, t)])
                            else:
                                P.op("dve", TS(h0bf[:, c, tok0:tok0 + p], src, cc(C_LNIN_G + c), cc(C_LNIN_B + c), ALU.mult, ALU.add),
                                     reads=[("ps", b), "cst"], writes=[("h0bf", c, t)])
                            if jb > ja:
                                o0 = TAU_M + ja - tok0
                                src2 = ps[b][:, k * 128 + o0:k * 128 + o0 + (jb - ja)]
                                if eng_ == "act":
                                    P.op("act", ACTF(h0f[:, c, ja:jb], src2, AF.Identity, bias=cc(C_LNIN_B + c), scale=cc(C_LNIN_G + c)),
                                         reads=[("ps", b), "cst"], writes=[("h0f", c)])
                                else:
                                    P.op("dve", TS(h0f[:, c, ja:jb], src2, cc(C_LNIN_G + c), cc(C_LNIN_B + c), ALU.mult, ALU.add),
                                         reads=[("ps", b), "cst"], writes=[("h0f", c)])

            cont = u in CONT
            tiles = list(range(2, 9)) if cont else list(range(NTILE))
            for i_ in range(len(tiles) + 1):
                if i_ < len(tiles):
                    stage_a(tiles[i_])
                if i_ >= 1:
                    stage_b(tiles[i_ - 1])
            P.fence("Y")
            h0all = [("h0bf", c, t) for c in range(DC) for t in tiles]
            check_stop("P1a")

            P.cur_tag = "P1b"
            ev = [0]

            def evac(out, in_, reads, writes):
                ev[0] += 1
                if ev[0] % 2:
                    P.op("act", ACP(out, in_), reads=reads, writes=writes)
                else:
                    P.op("dve", CP(out, in_), reads=reads, writes=writes)

            if cont:
                for tt in range(5):
                    P.op("pool", CP(KT[:, :, 128 * tt:128 * tt + 128], KT[:, :, 128 * (tt + 4):128 * (tt + 4) + 128]),
                         reads=[("KT", c) for c in range(8)], writes=[("KT", c) for c in range(8)])
                    P.op("pool", CP(Vt[:, tt, :], Vt[:, tt + 4, :]), reads=[("V", tt + 4, q) for q in range(4)],
                         writes=[("V", tt, q) for q in range(4)])
            for s in range(16):
                sl_, sres = next_slab()
                sv = slab[sl_].rearrange("p (kc n) -> p kc n", kc=16)
                kind = s // 4
                for e_ in range(2):
                    oc = 2 * (s % 4) + e_
                    if kind == 0:
                        pcs = [(TAU_U, 0, 265), (TAU_U + 265, 265, 265)]
                    elif kind == 1:
                        pcs = [(TAU_M, 0, 257), (TAU_M + 257, 257, 257)]
                    elif kind == 2:
                        pcs = [(640, 640, 512)] if cont else [(0, 0, 512), (512, 512, 512), (1024, 1024, 144)]
                    else:
                        pcs = []
                    for (tau0, o0, n) in pcs:
                        b = nb()
                        for kc in range(DC):
                            P.op("pe", MM(ps[b][:, 0:n], sv[:, kc, e_ * 128:(e_ + 1) * 128], h0bf[:, kc, tau0:tau0 + n],
                                          kc == 0, kc == DC - 1),
                                 reads=[sres] + (h0all if kc == 0 else []), writes=[("ps", b)])
                        if kind == 0:
                            evac(Ut[:, oc, o0:o0 + n], ps[b][:, 0:n], [("ps", b)], [("U", oc)])
                        elif kind == 1:
                            evac(QT[:, oc, o0:o0 + n], ps[b][:, 0:n], [("ps", b)], [("QT", oc)])
                        else:
                            evac(KT[:, oc, o0:o0 + n], ps[b][:, 0:n], [("ps", b)], [("KT", oc)])
                if kind == 3:
                    f0 = 256 * (s % 4)
                    for t2 in ((5, 7) if cont else range(0, NTILE, 2)):
                        b = nb()
                        for tt in (t2, t2 + 1):
                            p = 128 if tt < 9 else 16
                            off = 256 * (tt - t2)
                            for kc in range(DC):
                                P.op("pe", MM(ps[b][0:p, off:off + 256], h0bf[:, kc, tt * 128:tt * 128 + p], sv[:, kc, :],
                                              kc == 0, kc == DC - 1),
                                     reads=[sres] + (h0all if kc == 0 else []), writes=[("ps", b)])
                        evac(Vt[:, t2, f0:f0 + 256], ps[b][:, 0:256], [("ps", b)], [("V", t2, s % 4)])
                        p = 128 if t2 + 1 < 9 else 16
                        evac(Vt[0:p, t2 + 1, f0:f0 + 256], ps[b][0:p, 256:512], [("ps", b)], [("V", t2 + 1, s % 4)])
            check_stop("P1b")
            P.fence("A")
            Vall = [("V", t, q) for t in range(NTILE) for q in range(4)]

            P.cur_tag = "P2a"
            P.op("dve", TS(Ut[:, :, 521:530], Ut[:, :, 521:530], fcol(F_FPOST), None, ALU.mult),
                 reads=[("U", c) for c in range(8)] + [FL], writes=[("U", c) for c in range(8)])
            for g in range(4):
                w = 2 ** (g + 1)
                for kc2 in range(2):
                    c = 2 * g + kc2
                    U = Ut[:, c, :]
                    P.op("dve", TT(pt1[:, 1:530], U[:, 0:529], U[:, 1:530], ALU.add), reads=[("U", c)], writes=["pt1"])
                    cur, curname = pt1, "pt1"
                    if g >= 1:
                        P.op("dve", TT(pt2[:, 2:529], pt1[:, 1:528], pt1[:, 3:530], ALU.add), reads=["pt1"], writes=["pt2"])
                        cur, curname = pt2, "pt2"
                    if g >= 2:
                        P.op("dve", TT(pt1[:, 4:527], pt2[:, 2:525], pt2[:, 6:529], ALU.add), reads=["pt2"], writes=["pt1"])
                        cur, curname = pt1, "pt1"
                    if g >= 3:
                        P.op("dve", TT(pt2[:, 8:523], pt1[:, 4:519], pt1[:, 12:527], ALU.add), reads=["pt1"], writes=["pt2"])
                        cur, curname = pt2, "pt2"
                    P.op("dve", TT(cur[:, 513:521], cur[:, 513:521], flg[:, fb + F_CORR + 8 * g: fb + F_CORR + 8 * g + 8], ALU.mult),
                         reads=[curname, FL], writes=[curname])
                    P.op("dve", STT(mT[:, c, :], cur[:, 8:8 + NM], 1.0 / w, U[:, 8:8 + NM], ALU.mult, ALU.subtract),
                         reads=[curname, ("U", c)], writes=[("m", c)])
                for e_ in range(2):
                    oc = 2 * g + e_
                    for half in range(2):
                        b = nb()
                        for kc2 in range(2):
                            P.op("pe", MM(ps[b][:, 0:257], wp[:, g, kc2, e_ * 128:(e_ + 1) * 128],
                                          mT[:, 2 * g + kc2, 257 * half:257 * half + 257], kc2 == 0, kc2 == 1),
                                 reads=[("m", 2 * g + kc2), ("wp", g)], writes=[("ps", b)])
                        P.op("act", ACTF(ypool[:, oc, 257 * half:257 * half + 257], ps[b][:, 0:257], AF.Identity,
                                         scale=cc(C_PSCALE + oc)),
                             reads=[("ps", b), "cst"], writes=[("ypool", oc)])

            check_stop("P2a")
            prefetch_slabs()
            P.cur_tag = "P2b"
            first = True
            for h in range(16):
                hp, pi = h // 2, h % 2
                pr = slice(64 * pi, 64 * pi + 64)
                for it, (kind, j) in enumerate(items_h):
                    col = h * NHALO + it
                    qcol = NM - 1 if kind in ("post", "metapost") else 0
                    if j is None:
                        lhs = KT[pr, hp, 1152:1280]
                        out = ps[7][:, col:col + 1]
                    else:
                        lhs = KT[pr, hp, 128 * j:128 * j + 128]
                        out = ps[7][:, col:col + 1]
                    P.op("pe", MM(out, lhs, QT[pr, hp, qcol:qcol + 1], first, True),
                         reads=[("KT", hp), ("QT", hp)], writes=[("ps", 7)])
                    first = False
            P.op("dve", STT(sbh[:, :], ps[7][:, 0:240], 0.125, cst[:, C_GH:C_GH + 240], ALU.mult, ALU.add),
                 reads=[("ps", 7), "cst"], writes=["sbh"])
            P.op("dve", TT(sbh[:, :], sbh[:, :], flg[:, fb + F_FH: fb + F_FH + 240], ALU.add), reads=["sbh", FL], writes=["sbh"])
            P.op("act", ACTF(pth[:, :], sbh[:, :], AF.Exp), reads=["sbh"], writes=["pth"])

            def load_gp(hp_):
                P.op("sp", DMA(gpb[hp_ % 2].rearrange("p a b -> p (a b)"), gp_d[hp_]), writes=[("gpb", hp_ % 2)],
                     dsem=("gp", hp_ % 2))

            load_gp(0)
            steps = [None] + [j for j in range(9) if plans[j] is not None]
            nst = len(steps)
            sbank = [0, 1, 2]

            def emit_S(hp, i, pi):
                j = steps[i]
                n = 2 * (hp * nst + i) + pi
                h = 2 * hp + pi
                pr = slice(64 * pi, 64 * pi + 64)
                b = sbank[n % 3]
                if j is None:
                    P.op("pe", MM(ps[b][:, 0:512], KT[pr, hp, 1152:1280], QT[pr, hp, 1:513], True, True),
                         reads=[("KT", hp), ("QT", hp)], writes=[("ps", b)])
                    P.op("act", ACTF(ptb[n % 8][:, :], ps[b][:, 0:512], AF.Exp, bias=cc(C_METAB + h), scale=0.125),
                         reads=[("ps", b), "cst"], writes=[("ptb", n % 8)])
                else:
                    ra, rb, segs = plans[j]
                    qa, qb = 64 * ra, 64 * rb + 64
                    n_ = qb - qa
                    k0 = 2 * j - 6
                    gofs = (ra - k0 + 7) * 64
                    P.op("pe", MM(ps[b][:, 0:n_], KT[pr, hp, 128 * j:128 * j + 128], QT[pr, hp, 1 + qa:1 + qb], True, True),
                         reads=[("KT", hp), ("QT", hp)], writes=[("ps", b)])
                    P.op("dve", STT(sbt[n % 3][:, 0:n_], ps[b][:, 0:n_], 0.125, gpb[hp % 2][:, pi, gofs:gofs + n_],
                                    ALU.mult, ALU.add),
                         reads=[("ps", b), ("gpb", hp % 2)], writes=[("sbt", n % 3)])
                    for (r0, r1, combo) in segs:
                        a0, a1 = 64 * r0 - qa, 64 * r1 + 64 - qa
                        P.op("act", ACTF(ptb[n % 8][:, a0:a1], sbt[n % 3][:, a0:a1], AF.Exp, bias=fcol(F_BTAB + combo)),
                             reads=[("sbt", n % 3), FL], writes=[("ptb", n % 8)])

            def emit_PV(hp, i, pi):
                j = steps[i]
                n = 2 * (hp * nst + i) + pi
                h = 2 * hp + pi
                bV, bO = 3 + hp % 2, 5 + hp % 2
                pr = slice(64 * pi, 64 * pi + 64)
                last = i == nst - 1
                if j is None:
                    P.op("pe", MM(ps[bV][pr, 0:512], Vt[:, 9, 64 * h:64 * h + 64], ptb[n % 8][:, :], True, False),
                         reads=[("ptb", n % 8)] + Vall, writes=[("ps", bV)])
                    P.op("pe", MM(ps[bO][pr, 0:512], onesb[:, 0:64], ptb[n % 8][:, :], True, False),
                         reads=[("ptb", n % 8), "ones"], writes=[("ps", bO)])
                else:
                    ra, rb, segs = plans[j]
                    qa, qb = 64 * ra, 64 * rb + 64
                    n_ = qb - qa
                    P.op("pe", MM(ps[bV][pr, qa:qb], Vt[:, j, 64 * h:64 * h + 64], ptb[n % 8][:, 0:n_], False, last),
                         reads=[("ptb", n % 8)], writes=[("ps", bV)])
                    P.op("pe", MM(ps[bO][pr, qa:qb], onesb[:, 0:64], ptb[n % 8][:, 0:n_], False, last),
                         reads=[("ptb", n % 8), "ones"], writes=[("ps", bO)])

            def emit_norm(hp):
                bV, bO = 3 + hp % 2, 5 + hp % 2
                P.op("dve", lambda e, bO=bO: e.reciprocal(out=pt1[:, 0:512], in_=ps[bO][:, 0:512]), reads=[("ps", bO)],
                     writes=["pt1"])
                P.op("dve", TT(yattn[:, hp, 1:513], ps[bV][:, 0:512], pt1[:, 0:512], ALU.mult),
                     reads=[("ps", bV), "pt1"], writes=[("yattn", hp)])

            allsteps = [(hp, i) for hp in range(8) for i in range(nst)]
            LAG = 2
            for idx in range(len(allsteps) + LAG):
                if idx < len(allsteps):
                    hp, i = allsteps[idx]
                    if i == 0 and hp + 1 < 8:
                        load_gp(hp + 1)
                    emit_S(hp, i, 0)
                    emit_S(hp, i, 1)
                if idx >= LAG:
                    hp2, i2 = allsteps[idx - LAG]
                    emit_PV(hp2, i2, 0)
                    emit_PV(hp2, i2, 1)
                    if i2 == nst - 1:
                        emit_norm(hp2)
            for h in range(16):
                hp, pi = h // 2, h % 2
                pr = slice(64 * pi, 64 * pi + 64)
                for which in range(2):
                    its = [(it, kj) for it, kj in enumerate(items_h)
                           if (kj[0] in ("post", "metapost")) == (which == 1)]
                    cV = 256 + which * 8 + hp
                    cO = 288 + which * 8 + hp
                    for ii, (it, (kind, j)) in enumerate(its):
                        col = h * NHALO + it
                        lastf = ii == len(its) - 1
                        if j is None:
                            P.op("pe", MM(ps[7][pr, cV:cV + 1], Vt[:, 9, 64 * h:64 * h + 64], pth[:, col:col + 1], False, lastf),
                                 reads=["pth"] + Vall, writes=[("ps", 7)])
                            P.op("pe", MM(ps[7][pr, cO:cO + 1], onesb[:, 0:64], pth[:, col:col + 1], False, lastf),
                                 reads=["pth", "ones"], writes=[("ps", 7)])
                        else:
                            P.op("pe", MM(ps[7][pr, cV:cV + 1], Vt[:, j, 64 * h:64 * h + 64], pth[:, col:col + 1], False, lastf),
                                 reads=["pth"], writes=[("ps", 7)])
                            P.op("pe", MM(ps[7][pr, cO:cO + 1], onesb[:, 0:64], pth[:, col:col + 1], False, lastf),
                                 reads=["pth", "ones"], writes=[("ps", 7)])
            P.op("dve", lambda e: e.reciprocal(out=rech[:, :], in_=ps[7][:, 288:304]), reads=[("ps", 7)], writes=["rech"])
            P.op("dve", TT(yattn[:, :, 0], ps[7][:, 256:264], rech[:, 0:8], ALU.mult), reads=[("ps", 7), "rech"],
                 writes=[("yattn", c) for c in range(8)])
            P.op("dve", TT(yattn[:, :, NM - 1], ps[7][:, 264:272], rech[:, 8:16], ALU.mult), reads=[("ps", 7), "rech"],
                 writes=[("yattn", c) for c in range(8)])
            check_stop("P2b")
            P.fence("A")
            P.fence("B")

            P.cur_tag = "P3"
            ln1 = ln_begin([(0, 257), (257, 257)])
            for s in range(8):
                sl_, sres = next_slab()
                sv = slab[sl_].rearrange("p (kc n) -> p kc n", kc=16)
                for e_ in range(2):
                    oc = 2 * s + e_
                    if oc >= 2:
                        ln_chunk(ln1, oc - 2)
                    for half in range(2):
                        b = nb()
                        hs = slice(257 * half, 257 * half + 257)
                        for kc in range(DC):
                            src = ypool[:, kc, hs] if kc < 8 else yattn[:, kc - 8, hs]
                            rd = ("ypool", kc) if kc < 8 else ("yattn", kc - 8)
                            P.op("pe", MM(ps[b][:, 0:257], sv[:, kc, e_ * 128:(e_ + 1) * 128], src, kc == 0, kc == DC - 1),
                                 reads=[sres, rd], writes=[("ps", b)])
                        P.op("dve", STT(h0f[:, oc, hs], h0f[:, oc, hs], ALPHA, ps[b][:, 0:257], ALU.mult, ALU.add),
                             reads=[("ps", b), ("h0f", oc)], writes=[("h0f", oc)])
            check_stop("P3a")
            P.cur_tag = "LN1"
            ln_chunk(ln1, DC - 2)
            ln_chunk(ln1, DC - 1)
            ln_finish(ln1, C_LN1_G, C_LN1_B, True)
            check_stop("P3")
            P.fence("Y")

            P.cur_tag = "P4up"
            for part in range(2):
                P.cur_tag = "P4up"
                j0 = 0 if part == 0 else 22
                jr = range(0, 22) if part == 0 else range(22, NFC)
                def up_post(j, banks4):
                    zi = j % 2
                    for ag in range(2):
                        for half in range(2):
                            b = banks4[2 * ag + half]
                            hs = slice(257 * half, 257 * half + 257)
                            P.op("act", ACTF(zb[zi][:, ag, hs], ps[b][:, 0:257], AF.Identity, bias=cc(C_BUP + ag * NFC + j)),
                                 reads=[("ps", b), "cst"], writes=[("z", zi)])
                    P.op("dve", TS(zb[zi][:, :, NM - 1:NM], zb[zi][:, :, NM - 1:NM], fcol(F_FPOST), None, ALU.mult),
                         reads=[("z", zi), FL], writes=[("z", zi)])
                    outs = [(cab, "ca"), (cgb, "cg")]
                    for ag in range(2):
                        ob, on = outs[ag]
                        cj = ag * NFC + j
                        z = zb[zi][:, ag, :]
                        P.op("act", ACTF(ob[zi][:, :], z[:, 1:513], AF.Identity, bias=cc(C_CB + cj), scale=cc(C_CW + 86 + cj)),
                             reads=[("z", zi), "cst"], writes=[(on, zi)])
                        P.op("dve", STT(ob[zi][:, :], z[:, 0:512], cc(C_CW + cj), ob[zi][:, :], ALU.mult, ALU.add),
                             reads=[("z", zi), (on, zi), "cst"], writes=[(on, zi)])
                        P.op("dve", STT(ob[zi][:, :], z[:, 2:514], cc(C_CW + 172 + cj), ob[zi][:, :], ALU.mult, ALU.add),
                             reads=[("z", zi), (on, zi), "cst"], writes=[(on, zi)])
                    P.op("act", ACTF(ggb[zi][:, :], cgb[zi][:, :], AF.Gelu), reads=[("cg", zi)], writes=[("gg", zi)])
                    P.op("dve", TT(actb[:, j - j0, :], cab[zi][:, :], ggb[zi][:, :], ALU.mult), reads=[("ca", zi), ("gg", zi)],
                         writes=[("act", j - j0)])

                jlist = list(jr)
                if part == 0:
                    grp = []
                    for j in jlist[:2]:
                        sl_, sres = next_slab(4 if j == jlist[0] else 3)
                        sv = slab[sl_].rearrange("p (kc n) -> p kc n", kc=16)
                        grp.append((j, sres, sv, [nb() for _ in range(4)]))
                    for kc in range(DC):
                        for (j, sres, sv, banks4) in grp:
                            for ag in range(2):
                                for half in range(2):
                                    b = banks4[2 * ag + half]
                                    hs = slice(257 * half, 257 * half + 257)
                                    P.op("pe", MM(ps[b][:, 0:257], sv[:, kc, ag * 128:(ag + 1) * 128], h1bf[:, kc, hs], kc == 0, kc == DC - 1),
                                         reads=[sres, ("h1bf", kc)], writes=[("ps", b)])
                    for (j, sres, sv, banks4) in grp:
                        up_post(j, banks4)
                    jlist = jlist[2:]
                for j in jlist:
                    sl_, sres = next_slab()
                    sv = slab[sl_].rearrange("p (kc n) -> p kc n", kc=16)
                    banks4 = []
                    for ag in range(2):
                        for half in range(2):
                            b = nb()
                            banks4.append(b)
                            hs = slice(257 * half, 257 * half + 257)
                            for kc in range(DC):
                                P.op("pe", MM(ps[b][:, 0:257], sv[:, kc, ag * 128:(ag + 1) * 128], h1bf[:, kc, hs], kc == 0, kc == DC - 1),
                                     reads=[sres, ("h1bf", kc)], writes=[("ps", b)])
                    up_post(j, banks4)
                P.cur_tag = "P4dn"
                k0, nk = (0, 22) if part == 0 else (22, 21)
                if part == 1:
                    ln2 = ln_begin([(1, 512)])
                for oc in range(DC):
                    if part == 1 and oc >= 2:
                        ln_chunk(ln2, oc - 2)
                    b = nb()
                    sl_, sres = next_slab()
                    sv = slab[sl_][:, 0:nk * 128].rearrange("p (kc n) -> p kc n", kc=nk)
                    for kk in range(nk):
                        P.op("pe", MM(ps[b][:, 0:512], sv[:, kk, :], actb[:, kk, :], kk == 0, kk == nk - 1),
                             reads=[sres, ("act", kk)], writes=[("ps", b)])
                    P.op("dve", STT(h0f[:, oc, 1:513], h0f[:, oc, 1:513], ALPHA if part == 0 else 1.0, ps[b][:, 0:512], ALU.mult, ALU.add),
                         reads=[("ps", b), ("h0f", oc)], writes=[("h0f", oc)])
            check_stop("P4")
            P.fence("Y")
            P.fence("B")

            P.cur_tag = "P5"
            ln_chunk(ln2, DC - 2)
            ln_chunk(ln2, DC - 1)
            ln_finish(ln2, C_LN2_G, C_LN2_B, False)
            check_stop("P5a")
            for tt in range(4):
                ob_ = tt % 2
                for g4 in range(4):
                    b = nb()
                    for k in range(4):
                        c = 4 * g4 + k
                        P.op("pe", TR(ps[b][:, k * 128:(k + 1) * 128], h0f[:, c, 1 + 128 * tt:1 + 128 * tt + 128], ident[:, :]),
                             reads=[("h0f", c), "ident"], writes=[("ps", b)])
                    evac(otile[ob_][:, 512 * g4:512 * g4 + 512], ps[b][:, :], [("ps", b)], [("otile", ob_, g4)])
                o = P.op("pool", DMA(y_d[u, 128 * tt:128 * tt + 128, :], otile[ob_][:, :]),
                         reads=[("otile", ob_, g4) for g4 in range(4)], dsem=("out", ob_))
                out_ops.append(o)
            P.fence("Y")
            P.fence("A")
          except _Stop:
            break

        finals = {}
        for o in out_ops:
            finals[o.dsem] = o
        P.emit(final_wait_ops=list(finals.values()))
    return nc


def _unit_table():
    units = []
    for core in range(NCORE):
        ps_, run = core // 4, core % 4
        for k in range(2):
            units.append((ps_, 2 * run + k, 8))
        for k in range(4):
            units.append((2 + ps_, 4 * run + k, 16))
    return units


def _status_value(stt, start, end):
    if stt == ST_I:
        return 0.0
    if stt == ST_S:
        return 0.0 if start else -BIG
    if stt == ST_E:
        return 0.0 if end else -BIG
    if stt == ST_NS:
        return -BIG if start else 0.0
    if stt == ST_NE:
        return -BIG if end else 0.0
    return -BIG


def _build_gp(rpb):
    H = rpb.shape[0]
    gp = np.zeros((H, 2, 64, 16, 64), np.float32)
    c = np.arange(64)
    cs = np.clip(c - 8, 0, 48)
    key = np.arange(64)
    ok = (key[:, None] >= cs[None, :]) & (key[:, None] < cs[None, :] + 16)
    dc = np.clip(key[:, None] - c[None, :] + 15, 0, 30)
    for rp in range(2):
        for di in range(16):
            dr = rp - (di - 7) + 7
            if 0 <= dr <= 14:
                vals = rpb[:, dr, :][:, dc]
                gp[:, rp, :, di, :] = np.where(ok[None], vals, np.float32(-BIG))
            else:
                gp[:, rp, :, di, :] = np.where(ok, np.float32(0.0), np.float32(-BIG))[None]
    return gp.reshape(H, 128, 1024)


def _prepare(x_prompt, x_sample, meta_tokens, ln_in_g, ln_in_b, w_in, w_pool, pool_scale, rpb, meta_bias,
             w_out, ln1_g, ln1_b, w_up, b_up, conv_w, conv_b, w_down, ln2_g, ln2_b, cores=None):
    f32 = np.float32
    xs = [np.asarray(x_prompt[0], f32), np.asarray(x_prompt[1], f32), np.asarray(x_sample[0], f32),
          np.asarray(x_sample[1], f32)]
    meta = np.asarray(meta_tokens, f32)
    rpb0 = np.asarray(rpb, f32)[0]
    mb0 = np.asarray(meta_bias, f32)[0]
    units = _unit_table()
    assert len(units) == NCORE * NU

    gp = _build_gp(rpb0)
    gp_pairs = np.ascontiguousarray(gp.reshape(8, 2, 128, 1024).transpose(0, 2, 1, 3).reshape(8, 128, 2048))
    cst = np.zeros((128, NCONST), f32)
    col = lambda v: np.asarray(v, f32).reshape(-1, 128).T
    cst[:, C_LNIN_G:C_LNIN_G + 16] = col(ln_in_g)
    cst[:, C_LNIN_B:C_LNIN_B + 16] = col(ln_in_b)
    cst[:, C_LN1_G:C_LN1_G + 16] = col(ln1_g)
    cst[:, C_LN1_B:C_LN1_B + 16] = col(ln1_b)
    cst[:, C_LN2_G:C_LN2_G + 16] = col(ln2_g)
    cst[:, C_LN2_B:C_LN2_B + 16] = col(ln2_b)
    cst[:, C_PSCALE:C_PSCALE + 8] = col(pool_scale)
    cst[:, C_BUP:C_BUP + 86] = col(b_up)
    cw = np.asarray(conv_w, f32)[0]
    for k in range(3):
        cst[:, C_CW + 86 * k:C_CW + 86 * k + 86] = col(cw[k])
    cst[:, C_CB:C_CB + 86] = col(conv_b)
    cst[:, C_METAB:C_METAB + 16] = -BIG
    cst[0:16, C_METAB:C_METAB + 16] = mb0.T
    items = halo_items()
    for h in range(16):
        for it, (kind, j) in enumerate(items):
            c_ = C_GH + h * NHALO + it
            if j is None:
                cst[:, c_] = -BIG
                cst[0:16, c_] = mb0[h]
            else:
                dl, cq = halo_geom(kind, j)
                cst[:, c_] = gp[h][:, (dl + 7) * 64 + cq]

    def unit_flags(start, end):
        fl = np.zeros((128, NFLAG), f32)
        for stt in range(6):
            for sbb in range(6):
                fl[0:64, F_BTAB + stt * 6 + sbb] = _status_value(stt, start, end)
                fl[64:128, F_BTAB + stt * 6 + sbb] = _status_value(sbb, start, end)
        for h in range(16):
            for it, (kind, j) in enumerate(items):
                if j is None:
                    continue
                k0 = 2 * j - 6
                c_ = F_FH + h * NHALO + it
                fl[0:64, c_] = _status_value(halo_status(kind, k0), start, end)
                fl[64:128, c_] = _status_value(halo_status(kind, k0 + 1), start, end)
        fl[:, F_FPOST] = 0.0 if end else 1.0
        for g in range(4):
            w = 2 ** (g + 1)
            half = w // 2
            for i in range(8):
                tau_own = 504 + i
                cnt = w
                if end and tau_own + half > 512:
                    cnt = 512 - tau_own + half
                fl[:, F_CORR + 8 * g + i] = f32(w) / f32(cnt)
        return fl

    in_maps = []
    for core in (range(NCORE) if cores is None else cores):
        xu = np.zeros((NU, TOK, D), f32)
        flg = np.zeros((NU, 128, NFLAG), f32)
        for k in range(NU):
            s, uu, nun = units[core * NU + k]
            X = xs[s]
            nrows = nun * 8
            r0 = 8 * uu
            for row in range(-6, 12):
                gr = r0 + row
                if 0 <= gr < nrows:
                    xu[k, (row + 6) * 64:(row + 7) * 64] = X[gr * 64:(gr + 1) * 64]
            if uu == 0:
                xu[k, 384 - 16:384] = meta
            xu[k, 1152:1168] = meta
            flg[k] = unit_flags(uu == 0, uu == nun - 1)
        in_maps.append({
            "xu": xu, "w_in": np.ascontiguousarray(np.asarray(w_in, f32)[0]),
            "w_out": np.ascontiguousarray(np.asarray(w_out, f32)[0]),
            "w_up": np.ascontiguousarray(np.asarray(w_up, f32)[0]),
            "w_down": np.ascontiguousarray(np.asarray(w_down, f32)[0]),
            "w_pool": np.ascontiguousarray(np.asarray(w_pool, f32)[0]),
            "gp": gp_pairs, "cst": cst, "flg": flg,
        })

    return in_maps, units


def kernel(**inputs):
    in_maps, units = _prepare(**inputs)
    nc = build_program()
    res = run_bass_kernel_spmd(nc, in_maps, core_ids=list(range(NCORE)))
    return _assemble(res, units)


def _assemble(res, units):
    f32 = np.float32
    yp = np.zeros((2, 4096, D), f32)
    ysm = np.zeros((2, 8192, D), f32)
    outs = [yp[0], yp[1], ysm[0], ysm[1]]
    for core in range(NCORE):
        y = np.asarray(res.results[core]["y"], f32)
        for k in range(NU):
            s, uu, nun = units[core * NU + k]
            outs[s][512 * uu:512 * uu + 512] = y[k]
    return (yp, ysm)
```

```python
import contextlib
import os
import numpy as np
import concourse.bass as bass
import concourse.mybir as mybir
from concourse.bass_utils import run_bass_kernel_spmd

F32 = mybir.dt.float32
BF = mybir.dt.bfloat16
AF = mybir.ActivationFunctionType
ALU = mybir.AluOpType

D = 2048
DC = 16
NCORE = 8
NU = 6
TOK = 1168
NTILE = 10
NM = 514
NUC = 530
TAU_M = 383
TAU_U = 375
DFF = 5504
NFC = 43
BIG = 30000.0
ALPHA = float(2.0 ** 0.25)
LN_EPS = 1e-5
N_META = 16

C_LNIN_G, C_LNIN_B, C_LN1_G, C_LN1_B, C_LN2_G, C_LN2_B = 0, 16, 32, 48, 64, 80
C_PSCALE = 96
C_BUP = 104
C_CW = 190
C_CB = 448
C_METAB = 534
C_GH = 550
NCONST = 790
F_BTAB, F_FH, F_FPOST, F_CORR = 0, 36, 276, 277
NFLAG = 309
NHALO = 15

ENGS = ("pe", "act", "dve", "pool", "sp")


class Op:
    __slots__ = ("eng", "fn", "deps", "is_dma", "dsem", "needs_inc", "semval")

    def __init__(self, eng, fn, dsem=None):
        self.eng = eng
        self.fn = fn
        self.deps = ()
        self.is_dma = dsem is not None
        self.dsem = dsem
        self.needs_inc = False
        self.semval = None


class Prog:
    def __init__(self, nc):
        self.nc = nc
        self.ops = {e: [] for e in ENGS}
        self.last_w = {}
        self.readers = {}
        self.reg_of = {}
        self.reg_cur = {}
        self.reg_fence = {}
        self.dma_cnt = {}
        self.excl = {}
        self.pe_tags = []
        self.cur_tag = "setup"
        self.trace = None

    def region(self, reg, names):
        for n in names:
            self.reg_of[n] = reg

    def fence(self, reg):
        cur = self.reg_cur.get(reg)
        if cur:
            self.reg_fence[reg] = dict(cur)
            self.reg_cur[reg] = {}

    @staticmethod
    def _key(o):
        return ("dma", id(o)) if o.is_dma else o.eng

    def op(self, eng, fn, reads=(), writes=(), dsem=None, extra_deps=()):
        o = Op(eng, fn, dsem)
        if eng == "pe":
            self.pe_tags.append(self.cur_tag)
        deps = {id(x): x for x in extra_deps}
        regs = set()
        px = [r for r in tuple(reads) + tuple(writes) if isinstance(r, tuple) and r[0] == "ps"]
        if px:
            reads = [r for r in reads if not (isinstance(r, tuple) and r[0] == "ps")]
            writes = [r for r in writes if not (isinstance(r, tuple) and r[0] == "ps")]
            for r in px:
                prev = self.excl.get(r)
                if prev is not None and prev.eng != eng:
                    deps[id(prev)] = prev
                self.excl[r] = o
        for r in reads:
            w = self.last_w.get(r)
            if w is not None:
                deps[id(w)] = w
            nm = r[0] if isinstance(r, tuple) else r
            rg = self.reg_of.get(nm)
            if rg is not None:
                regs.add(rg)
        for r in writes:
            w = self.last_w.get(r)
            if w is not None:
                deps[id(w)] = w
            rd = self.readers.get(r)
            if rd:
                for x in rd.values():
                    deps[id(x)] = x
            nm = r[0] if isinstance(r, tuple) else r
            rg = self.reg_of.get(nm)
            if rg is not None:
                regs.add(rg)
        for rg in regs:
            f = self.reg_fence.get(rg)
            if f:
                for x in f.values():
                    deps[id(x)] = x
            self.reg_cur.setdefault(rg, {})[self._key(o)] = o
        o.deps = tuple(deps.values())
        k = self._key(o)
        for r in reads:
            self.readers.setdefault(r, {})[k] = o
        for r in writes:
            self.last_w[r] = o
            self.readers[r] = {}
        self.ops[eng].append(o)
        return o

    def emit(self, final_wait_ops=()):
        nc = self.nc
        for o in final_wait_ops:
            if not o.is_dma:
                o.needs_inc = True
        for e in ENGS:
            for o in self.ops[e]:
                for d in o.deps:
                    if d.is_dma:
                        continue
                    if d.eng == o.eng and d.eng == "pe":
                        continue
                    d.needs_inc = True
        for e in ENGS:
            c = 0
            for o in self.ops[e]:
                if o.is_dma:
                    self.dma_cnt[o.dsem] = self.dma_cnt.get(o.dsem, 0) + 16
                    o.semval = self.dma_cnt[o.dsem]
                elif o.needs_inc:
                    c += 1
                    o.semval = c
        with contextlib.ExitStack() as st:
            esem = {e: st.enter_context(nc.semaphore("prog_" + e)) for e in ENGS}
            dsems = {}
            for i, k in enumerate(sorted(self.dma_cnt.keys(), key=str)):
                dsems[k] = st.enter_context(nc.semaphore("dma%d" % i))
            block = st.enter_context(nc.Block())

            def run(ename, eng):
                seen = {}
                for o in self.ops[ename]:
                    need = {}
                    for d in o.deps:
                        if d.is_dma:
                            s = ("d", d.dsem)
                        else:
                            if d.eng == ename and ename == "pe":
                                continue
                            s = ("e", d.eng)
                        if need.get(s, 0) < d.semval:
                            need[s] = d.semval
                    for s, v in need.items():
                        if seen.get(s, 0) >= v:
                            continue
                        seen[s] = v
                        eng.wait_ge(dsems[s[1]] if s[0] == "d" else esem[s[1]], v)
                        if self.trace is not None:
                            self.trace.append((ename, "wait", s, v))
                    ins = o.fn(eng)
                    if self.trace is not None:
                        self.trace.append((ename, "op", getattr(o, "tag", None), (o.dsem if o.is_dma else ("inc" if o.needs_inc else None)), o.semval))
                    if o.is_dma:
                        ins.then_inc(dsems[o.dsem], 16)
                    elif o.needs_inc:
                        ins.then_inc(esem[ename], 1)
                if ename == "sp":
                    for o in final_wait_ops:
                        eng.wait_ge(dsems[o.dsem] if o.is_dma else esem[o.eng], o.semval)

            @block.tensor
            def _(e):
                run("pe", e)

            @block.scalar
            def _(e):
                run("act", e)

            @block.vector
            def _(e):
                run("dve", e)

            @block.gpsimd
            def _(e):
                run("pool", e)

            @block.sync
            def _(e):
                run("sp", e)


def MM(out, lhsT, rhs, start, stop):
    return lambda e: e.matmul(out, lhsT=lhsT, rhs=rhs, start=start, stop=stop)


def TR(out, in_, ident):
    return lambda e: e.transpose(out=out, in_=in_, identity=ident)


def ACTF(out, in_, func, bias=None, scale=None):
    kw = {}
    if bias is not None:
        kw["bias"] = bias
    if scale is not None:
        kw["scale"] = scale
    return lambda e: e.activation(out=out, in_=in_, func=func, **kw)


def STT(out, in0, scalar, in1, op0, op1):
    return lambda e: e.scalar_tensor_tensor(out=out, in0=in0, scalar=scalar, in1=in1, op0=op0, op1=op1)


def TT(out, in0, in1, op):
    return lambda e: e.tensor_tensor(out=out, in0=in0, in1=in1, op=op)


def TS(out, in0, s1, s2, op0, op1=None):
    if op1 is None:
        return lambda e: e.tensor_scalar(out=out, in0=in0, scalar1=s1, scalar2=None, op0=op0)
    return lambda e: e.tensor_scalar(out=out, in0=in0, scalar1=s1, scalar2=s2, op0=op0, op1=op1)


def CP(out, in_):
    return lambda e: e.tensor_copy(out=out, in_=in_)


def ACP(out, in_):
    return lambda e: e.copy(out=out, in_=in_)


def DMA(out, in_):
    return lambda e: e.dma_start(out=out, in_=in_)


ST_I, ST_S, ST_E, ST_NS, ST_NE, ST_NV = 0, 1, 2, 3, 4, 5


def key_status_own(kap, rho):
    d = kap - rho
    if kap == -6:
        return ST_NV
    if 0 <= kap <= 7:
        if -4 <= d <= 3:
            return ST_I
        return ST_S if d >= 4 else ST_E
    if kap < 0:
        return ST_NS if d >= -4 else ST_NV
    return ST_NE if d <= 3 else ST_NV


def pair_plan(j):
    k0 = 2 * j - 6
    per = []
    for rho in range(8):
        st, sb = key_status_own(k0, rho), key_status_own(k0 + 1, rho)
        per.append(None if (st == ST_NV and sb == ST_NV) else st * 6 + sb)
    rows = [r for r in range(8) if per[r] is not None]
    if not rows:
        return None
    ra, rb = rows[0], rows[-1]
    segs = []
    r = ra
    while r <= rb:
        assert per[r] is not None
        r2 = r
        while r2 + 1 <= rb and per[r2 + 1] == per[r]:
            r2 += 1
        segs.append((r, r2, per[r]))
        r = r2 + 1
    return ra, rb, segs


def halo_items():
    items = []
    for j in range(0, 5):
        items.append(("preI", j))
    for j in range(3, 7):
        items.append(("preS", j))
    for j in range(5, 9):
        items.append(("post", j))
    items.append(("metapre", None))
    items.append(("metapost", None))
    assert len(items) == NHALO
    return items


def halo_status(kind, kap):
    if kind == "preI":
        return ST_NS if -5 <= kap <= 2 else ST_NV
    if kind == "preS":
        return ST_S if 0 <= kap <= 7 else ST_NV
    if kind == "post":
        if 4 <= kap <= 7:
            return ST_I
        return ST_NE if 8 <= kap <= 11 else ST_NV
    raise ValueError


def halo_geom(kind, j):
    k0 = 2 * j - 6
    if kind == "preI":
        return -1 - k0, 63
    if kind == "preS":
        return 0 - k0, 0
    return 8 - k0, 0


def build_program(NU=NU, stop_after=None):
    nc = bass.Bass("TRN2", target_bir_lowering=False)
    dt_in = lambda n, s: nc.dram_tensor(n, s, F32, kind="ExternalInput").ap()
    xu = dt_in("xu", [NU, TOK, D])
    w_in = dt_in("w_in", [D, 4096])
    w_out = dt_in("w_out", [D, D])
    w_up = dt_in("w_up", [D, 2 * DFF])
    w_down = dt_in("w_down", [DFF, D])
    w_pool = dt_in("w_pool", [4, 256, 256])
    gp_d = dt_in("gp", [8, 128, 2048])
    cst_d = dt_in("cst", [128, NCONST])
    flg_d = dt_in("flg", [NU, 128, NFLAG])
    y_d = nc.dram_tensor("y", [NU, 512, D], F32, kind="ExternalOutput").ap()
    dbg_d = None
    if stop_after is not None:
        dbg_d = {"A": nc.dram_tensor("dbgA", [128, 9344], F32, kind="ExternalOutput").ap(),
                 "H": nc.dram_tensor("dbgH", [128, DC * NM], F32, kind="ExternalOutput").ap(),
                 "B": nc.dram_tensor("dbgB", [128, 16536], F32, kind="ExternalOutput").ap(),
                 "Y": nc.dram_tensor("dbgY", [128, 6168], F32, kind="ExternalOutput").ap()}
    wb_in = nc.dram_tensor("wb_in", [16, 128, 4096], BF, kind="Internal").ap()
    wb_out = nc.dram_tensor("wb_out", [8, 128, 4096], BF, kind="Internal").ap()
    wb_up = nc.dram_tensor("wb_up", [NFC, 128, 4096], BF, kind="Internal").ap()
    wb_dn = nc.dram_tensor("wb_dn", [32, 128, 22 * 128], BF, kind="Internal").ap()

    with contextlib.ExitStack() as st:
        def sb(n, words):
            return st.enter_context(nc.sbuf_tensor(n, [128, words], F32))

        RA = sb("RA", 9344)
        RH = sb("RH", DC * NM)
        RB = sb("RB", 16536)
        RY = sb("RY", 6168)
        RS = sb("RS", 4 * 2048)
        cst = sb("cst_sb", NCONST)
        flg = sb("flg_sb", 2 * NFLAG)
        ident = sb("ident", 128)
        onesw = sb("onesw", 64)
        wpw = sb("wpw", 1024)
        small = sb("small", 96)
        ps = [st.enter_context(nc.psum_tensor("ps%d" % i, [128, 512], F32)) for i in range(8)]

        def vbf(t, off_w, shape):
            n = int(np.prod(shape))
            a = t.bitcast(BF)[:, 2 * off_w: 2 * off_w + n]
            if len(shape) == 1:
                return a
            names = "abcd"[:len(shape)]
            pat = "p (%s) -> p %s" % (" ".join(names), " ".join(names))
            return a.rearrange(pat, **{names[i]: shape[i] for i in range(len(shape) - 1)})

        def vf(t, off_w, shape):
            n = int(np.prod(shape))
            a = t[:, off_w: off_w + n]
            if len(shape) == 1:
                return a
            names = "abcd"[:len(shape)]
            pat = "p (%s) -> p %s" % (" ".join(names), " ".join(names))
            return a.rearrange(pat, **{names[i]: shape[i] for i in range(len(shape) - 1)})

        h0bf = vbf(RA, 0, [DC, TOK])
        gpb = [vf(RA, 0, [2, 1024]), vf(RA, 2048, [2, 1024])]
        sbt = [vf(RA, 4096, [512]), vf(RA, 4608, [512]), vf(RA, 5120, [512])]
        ptb = [vbf(RA, 5632, [512]), vbf(RA, 5888, [512]), vbf(RA, 6144, [512]), vbf(RA, 6400, [512]),
               vbf(RA, 8100, [512]), vbf(RA, 8356, [512]), vbf(RA, 8612, [512]), vbf(RA, 8868, [512])]
        sbh = vf(RA, 6656, [240])
        pth = vbf(RA, 6896, [240])
        rech = vf(RA, 7016, [16])
        pt1 = vf(RA, 7040, [NUC])
        pt2 = vf(RA, 7040 + NUC, [NUC])
        h1bf = vbf(RA, 0, [DC, NM])
        lxb = [vbf(RA, 4112 + 257 * i, [NM]) for i in range(3)]
        lxq = [vbf(RA, 4112 + 771 + 257 * i, [NM]) for i in range(3)]
        lmean = vf(RA, 5654, [NM])
        lmsq = vf(RA, 5654 + NM, [NM])
        lrstd = vf(RA, 5654 + 2 * NM, [NM])
        ltmp = [vf(RA, 7196 + NM * i, [NM]) for i in range(4)]
        h0f = vf(RH, 0, [DC, NM])
        KT = vbf(RB, 0, [8, 1280])
        Vt = vbf(RB, 5120, [NTILE, 1024])
        QT = vbf(RB, 10240, [8, NM])
        Ut = vf(RB, 12296, [8, NUC])
        actb = vbf(RB, 10240, [22, 512])
        xt = [vf(RY, 0, [D]), vf(RY, 2048, [D]), vf(RY, 4096, [D])]
        mT = vbf(RY, 0, [8, NM])
        ypool = vbf(RY, 2056, [8, NM])
        yattn = vbf(RY, 4112, [8, NM])
        zb = [vf(RY, 0, [2, NM]), vf(RY, 1028, [2, NM])]
        cab = [vf(RY, 2056, [512]), vf(RY, 2568, [512])]
        cgb = [vf(RY, 3080, [512]), vf(RY, 3592, [512])]
        ggb = [vf(RY, 4104, [512]), vf(RY, 4616, [512])]
        otile = [vf(RY, 0, [D]), vf(RY, 2048, [D])]
        slab = [vbf(RS, 2048 * i, [4096]) for i in range(4)]
        onesb = vbf(onesw, 0, [128])
        wp = vbf(wpw, 0, [4, 2, 256])
        stt_l = [vf(small, 32 * i, [4, 6]) for i in range(3)]
        mv_l = [vf(small, 32 * i + 24, [2]) for i in range(3)]
        rstd_l = [vf(small, 32 * i + 26, [1]) for i in range(3)]
        nmr_l = [vf(small, 32 * i + 27, [1]) for i in range(3)]

        P = Prog(nc)
        nc._prog = P
        if "trace" in os.environ.get("KDBG", ""):
            P.trace = []
            nc._ptrace = P.trace
        P.region("A", ["h0bf", "gpb", "sbt", "ptb", "sbh", "pth", "rech", "pt1", "pt2", "h1bf", "lxb", "lxq",
                       "lstat", "ltmp"])
        P.region("B", ["KT", "V", "QT", "U", "act"])
        P.region("Y", ["xt", "m", "ypool", "yattn", "z", "ca", "cg", "gg", "otile"])

        bank_ctr = [0]

        reserved = set()

        def nb():
            while True:
                b = bank_ctr[0] % 8
                bank_ctr[0] += 1
                if b not in reserved:
                    return b

        cc = lambda col: cst[:, col:col + 1]

        P.op("sp", DMA(cst[:], cst_d[:]), writes=["cst"], dsem="cst")
        P.op("dve", lambda e: e.memset(ident[:], 0.0), writes=["ident"])
        P.op("pool", lambda e: e.affine_select(out=ident[:], in_=ident[:], pattern=[[-1, 128]],
                                               compare_op=ALU.not_equal, fill=1.0, base=0, channel_multiplier=1),
             reads=["ident"], writes=["ident"])
        P.op("dve", lambda e: e.memset(onesb, 1.0), writes=["ones"])
        P.op("dve", lambda e: e.memset(RB[:, :], 0.0), writes=[("KT", c) for c in range(8)] + [("V", 9, q) for q in range(4)])

        cv_n = [0]


        dbgflags = os.environ.get("KDBG", "")

        def conv_dma(out, in_, res):
            if "noconv" in dbgflags:
                return
            if "only_" in dbgflags and ("only_" + res[0]) not in dbgflags:
                return
            n = cv_n[0]
            cv_n[0] += 1
            P.op("pool", DMA(out, in_), writes=[res, ("cvslot", n % 5)], dsem=("cv", n % 5))

        w_in_v = w_in.rearrange("(kc p) n -> p kc n", p=128)
        w_out_v = w_out.rearrange("(kc p) n -> p kc n", p=128)
        w_up_v = w_up.rearrange("(kc p) n -> p kc n", p=128)
        w_dn_v = w_down.rearrange("(kc p) n -> p kc n", p=128)
        for s in range(16):
            conv_dma(wb_in[s].rearrange("p (kc n) -> p kc n", kc=16), w_in_v[:, :, 256 * s:256 * s + 256], ("wb_in", s))
        for g in range(4):
            if "nowp" in dbgflags:
                continue
            P.op("pool", DMA(wp[:, g], w_pool[g].rearrange("(kc p) e -> p kc e", p=128)), writes=[("wp", g)],
                 dsem=("wpl", g))
        for s in range(8):
            conv_dma(wb_out[s].rearrange("p (kc n) -> p kc n", kc=16), w_out_v[:, :, 256 * s:256 * s + 256], ("wb_out", s))
        def conv_up(j):
            ov = wb_up[j].rearrange("p (kc n) -> p kc n", kc=16)
            conv_dma(ov[:, :, 0:128], w_up_v[:, :, 128 * j:128 * j + 128], ("wb_up", j, 0))
            conv_dma(ov[:, :, 128:256], w_up_v[:, :, DFF + 128 * j:DFF + 128 * j + 128], ("wb_up", j, 1))

        def conv_dn(oc, part):
            k0, nk = (0, 22) if part == 0 else (22, 21)
            ov = wb_dn[2 * oc + part].rearrange("p (kc n) -> p kc n", kc=22)
            conv_dma(ov[:, 0:nk, :], w_dn_v[:, k0:k0 + nk, 128 * oc:128 * oc + 128], ("wb_dn", 2 * oc + part))

        for part in range(2):
            for j in (range(0, 22) if part == 0 else range(22, NFC)):
                conv_up(j)
            for oc in range(16):
                conv_dn(oc, part)

        slab_list = []
        for u in range(NU):
            for s in range(16):
                slab_list.append((wb_in[s], 4096, [("wb_in", s)]))
            for s in range(8):
                slab_list.append((wb_out[s], 4096, [("wb_out", s)]))
            for part in range(2):
                for j in (range(0, 22) if part == 0 else range(22, NFC)):
                    slab_list.append((wb_up[j], 4096, [("wb_up", j, 0), ("wb_up", j, 1)]))
                nk = 22 if part == 0 else 21
                for oc in range(DC):
                    slab_list.append((wb_dn[2 * oc + part][:, 0:nk * 128], nk * 128, [("wb_dn", 2 * oc + part)]))
        slab_issued = [0]
        slab_cur = [0]

        def prefetch_slabs():
            i = slab_cur[0]
            while slab_issued[0] < min(len(slab_list), i + 4):
                n = slab_issued[0]
                src, ncol, res = slab_list[n]
                P.op("sp", DMA(slab[n % 4][:, 0:ncol], src), reads=res, writes=[("slab", n % 4)],
                     dsem=("slab", n % 4))
                slab_issued[0] += 1

        def next_slab(ahead=4):
            i = slab_cur[0]
            slab_cur[0] += 1
            while slab_issued[0] < min(len(slab_list), i + ahead):
                n = slab_issued[0]
                src, ncol, res = slab_list[n]
                P.op("sp", DMA(slab[n % 4][:, 0:ncol], src), reads=res, writes=[("slab", n % 4)],
                     dsem=("slab", n % 4))
                slab_issued[0] += 1
            return i % 4, ("slab", i % 4)

        def ln_begin(pieces):
            banks = [(nb(), nb()) for _ in pieces]
            for b1, b2 in banks:
                reserved.add(b1)
                reserved.add(b2)
            return {"pieces": pieces, "banks": banks}

        def ln_chunk(stt, c):
            r = c % 3
            P.op("act", ACP(lxb[r][:, 0:NM], h0f[:, c, :]), reads=[("h0f", c)], writes=[("lxb", r)])
            P.op("act", ACTF(lxq[r][:, 0:NM], h0f[:, c, :], AF.Square), reads=[("h0f", c)], writes=[("lxq", r)])
            for pi, (c0, n) in enumerate(stt["pieces"]):
                b1, b2 = stt["banks"][pi]
                P.op("pe", MM(ps[b1][:, 0:n], onesb, lxb[r][:, c0:c0 + n], c == 0, c == DC - 1),
                     reads=[("lxb", r), "ones"], writes=[("ps", b1)])
                P.op("pe", MM(ps[b2][:, 0:n], onesb, lxq[r][:, c0:c0 + n], c == 0, c == DC - 1),
                     reads=[("lxq", r), "ones"], writes=[("ps", b2)])

        def ln_finish(stt, gcol, bcol, out_bf):
            pieces, banks = stt["pieces"], stt["banks"]
            for pi, (c0, n) in enumerate(pieces):
                b1, b2 = banks[pi]
                sl = slice(c0, c0 + n)
                P.op("dve", TS(lmean[:, sl], ps[b1][:, 0:n], 1.0 / D, None, ALU.mult), reads=[("ps", b1)],
                     writes=[("lstat", "mean", pi)])
                P.op("dve", TT(lmsq[:, sl], lmean[:, sl], lmean[:, sl], ALU.mult), reads=[("lstat", "mean", pi)],
                     writes=[("lstat", "msq", pi)])
                P.op("dve", STT(lrstd[:, sl], ps[b2][:, 0:n], 1.0 / D, lmsq[:, sl], ALU.mult, ALU.subtract),
                     reads=[("ps", b2), ("lstat", "msq", pi)], writes=[("lstat", "rstd", pi)])
                P.op("act", ACTF(lrstd[:, sl], lrstd[:, sl], AF.Sqrt, bias=LN_EPS),
                     reads=[("lstat", "rstd", pi)], writes=[("lstat", "rstd", pi)])
                P.op("dve", lambda e, sl=sl: e.reciprocal(out=lrstd[:, sl], in_=lrstd[:, sl]),
                     reads=[("lstat", "rstd", pi)], writes=[("lstat", "rstd", pi)])
                reserved.discard(b1)
                reserved.discard(b2)
            lo = pieces[0][0]
            hi = pieces[-1][0] + pieces[-1][1]
            sl = slice(lo, hi)
            srd = [("lstat", "mean", pi) for pi in range(len(pieces))] + [("lstat", "rstd", pi) for pi in range(len(pieces))]
            for c in range(DC):
                eng_ = "dve"
                r = c % 4
                P.op(eng_, TT(ltmp[r][:, sl], h0f[:, c, sl], lmean[:, sl], ALU.subtract),
                     reads=[("h0f", c)] + srd, writes=[("ltmp", r)])
                P.op(eng_, TT(ltmp[r][:, sl], ltmp[r][:, sl], lrstd[:, sl], ALU.mult),
                     reads=[("ltmp", r)] + srd, writes=[("ltmp", r)])
                P.op("act", ACTF(h0f[:, c, sl], ltmp[r][:, sl], AF.Identity, bias=cc(bcol + c), scale=cc(gcol + c)),
                     reads=[("ltmp", r), "cst"], writes=[("h0f", c)])
                if out_bf:
                    P.op("act", ACTF(h1bf[:, c, sl], ltmp[r][:, sl], AF.Identity, bias=cc(bcol + c), scale=cc(gcol + c)),
                         reads=[("ltmp", r), "cst"], writes=[("h1bf", c)])

        items_h = halo_items()
        plans = [pair_plan(j) for j in range(9)]
        out_ops = []

        class _Stop(Exception):
            pass

        def check_stop(name):
            if stop_after != name:
                return
            lasts = [P.ops[e][-1] for e in ("pe", "act", "dve") if P.ops[e]]
            for nm, t in (("A", RA), ("H", RH), ("B", RB), ("Y", RY)):
                o = P.op("sp", DMA(dbg_d[nm][:, :], t[:, :]), dsem=("dbg", nm), extra_deps=lasts)
                out_ops.append(o)
            raise _Stop()

        CONT = (1, 3, 4, 5)
        for u in range(NU):
          try:
            check_stop("setup")
            fb = (u % 2) * NFLAG
            fcol = lambda col: flg[:, fb + col: fb + col + 1]
            P.op("sp", DMA(flg[:, fb:fb + NFLAG], flg_d[u]), writes=[("flg", u % 2)], dsem=("flg", u % 2))
            FL = ("flg", u % 2)

            P.cur_tag = "P1a"
            def stage_a(t):
                    p = 128 if t < 9 else 16
                    xb_ = t % 3
                    stt_t, mv, rstd_s, nmr_s = stt_l[xb_], mv_l[xb_], rstd_l[xb_], nmr_l[xb_]
                    tok0 = t * 128
                    P.op("sp", DMA(xt[xb_][0:p, :], xu[u, tok0:tok0 + p, :]), writes=[("xt", xb_)], dsem=("x", xb_))
                    for q in range(4):
                        P.op("dve", lambda e, q=q, p=p, xb_=xb_, stt_t=stt_t: e.bn_stats(out=stt_t[0:p, q, :], in_=xt[xb_][0:p, 512 * q:512 * q + 512]),
                             reads=[("xt", xb_)], writes=[("bnst", xb_, q)])
                    P.op("dve", lambda e, p=p, stt_t=stt_t, mv=mv: e.bn_aggr(out=mv[0:p, :], in_=stt_t[0:p].rearrange("p a b -> p (a b)")),
                         reads=[("bnst", xb_, q) for q in range(4)], writes=[("mv", xb_)])
                    P.op("act", ACTF(rstd_s[0:p, :], mv[0:p, 1:2], AF.Sqrt, bias=LN_EPS), reads=[("mv", xb_)], writes=[("rstd_s", xb_)])
                    P.op("dve", lambda e, p=p, rstd_s=rstd_s: e.reciprocal(out=rstd_s[0:p, :], in_=rstd_s[0:p, :]),
                         reads=[("rstd_s", xb_)], writes=[("rstd_s", xb_)])
                    P.op("dve", TS(nmr_s[0:p, :], mv[0:p, 0:1], rstd_s[0:p, 0:1], -1.0, ALU.mult, ALU.mult),
                         reads=[("mv", xb_), ("rstd_s", xb_)], writes=[("nmr_s", xb_)])
                    P.op("act", ACTF(xt[xb_][0:p, :], xt[xb_][0:p, :], AF.Identity, bias=nmr_s[0:p, 0:1], scale=rstd_s[0:p, 0:1]),
                         reads=[("xt", xb_), ("rstd_s", xb_), ("nmr_s", xb_)], writes=[("xt", xb_)])

            def stage_b(t):
                    p = 128 if t < 9 else 16
                    xb_ = t % 3
                    tok0 = t * 128
                    ja = max(0, tok0 - TAU_M)
                    jb = min(NM, tok0 + p - TAU_M)
                    for g4 in range(4):
                        b = nb()
                        eng_ = "act" if (g4 + t) % 2 == 0 else "dve"
                        for k in range(4):
                            c = 4 * g4 + k
                            P.op("pe", TR(ps[b][:, k * 128:k * 128 + p], xt[xb_][0:p, c * 128:(c + 1) * 128], ident[0:p, 0:p]),
                                 reads=[("xt", xb_), "ident"], writes=[("ps", b)])
                        for k in range(4):
                            c = 4 * g4 + k
                            src = ps[b][:, k * 128:k * 128 + p]
                            if eng_ == "act":
                                P.op("act", ACTF(h0bf[:, c, tok0:tok0 + p], src, AF.Identity,
                                                 bias=cc(C_LNIN_B + c), scale=cc(C_LNIN_G + c)),
                                     reads=[("ps", b), "cst"], writes=[("h0bf", c, t)])
                            else:
                                P.op("dve", TS(h0bf[:, c, tok0:tok0 + p], src, cc(C_LNIN_G + c), cc(C_LNIN_B + c), ALU.mult, ALU.add),
                                     reads=[("ps", b), "cst"], writes=[("h0bf", c, t)])
                            if jb > ja:
                                o0 = TAU_M + ja - tok0
                                src2 = ps[b][:, k * 128 + o0:k * 128 + o0 + (jb - ja)]
                                if eng_ == "act":
                                    P.op("act", ACTF(h0f[:, c, ja:jb], src2, AF.Identity, bias=cc(C_LNIN_B + c), scale=cc(C_LNIN_G + c)),
                                         reads=[("ps", b), "cst"], writes=[("h0f", c)])
                                else:
                                    P.op("dve", TS(h0f[:, c, ja:jb], src2, cc(C_LNIN_G + c), cc(C_LNIN_B + c), ALU.mult, ALU.add),
                                         reads=[("ps", b), "cst"], writes=[("h0f", c)])

            cont = u in CONT
            tiles = list(range(2, 9)) if cont else list(range(NTILE))
            for i_ in range(len(tiles) + 1):
                if i_ < len(tiles):
                    stage_a(tiles[i_])
                if i_ >= 1:
                    stage_b(tiles[i_ - 1])
            P.fence("Y")
            h0all = [("h0bf", c, t) for c in range(DC) for t in tiles]
            check_stop("P1a")

            P.cur_tag = "P1b"
            ev = [0]

            def evac(out, in_, reads, writes):
                ev[0] += 1
                if ev[0] % 2:
                    P.op("act", ACP(out, in_), reads=reads, writes=writes)
                else:
                    P.op("dve", CP(out, in_), reads=reads, writes=writes)

            if cont:
                for tt in range(5):
                    P.op("pool", CP(KT[:, :, 128 * tt:128 * tt + 128], KT[:, :, 128 * (tt + 4):128 * (tt + 4) + 128]),
                         reads=[("KT", c) for c in range(8)], writes=[("KT", c) for c in range(8)])
                    P.op("pool", CP(Vt[:, tt, :], Vt[:, tt + 4, :]), reads=[("V", tt + 4, q) for q in range(4)],
                         writes=[("V", tt, q) for q in range(4)])
            for s in range(16):
                sl_, sres = next_slab()
                sv = slab[sl_].rearrange("p (kc n) -> p kc n", kc=16)
                kind = s // 4
                for e_ in range(2):
                    oc = 2 * (s % 4) + e_
                    if kind == 0:
                        pcs = [(TAU_U, 0, 265), (TAU_U + 265, 265, 265)]
                    elif kind == 1:
                        pcs = [(TAU_M, 0, 257), (TAU_M + 257, 257, 257)]
                    elif kind == 2:
                        pcs = [(640, 640, 512)] if cont else [(0, 0, 512), (512, 512, 512), (1024, 1024, 144)]
                    else:
                        pcs = []
                    for (tau0, o0, n) in pcs:
                        b = nb()
                        for kc in range(DC):
                            P.op("pe", MM(ps[b][:, 0:n], sv[:, kc, e_ * 128:(e_ + 1) * 128], h0bf[:, kc, tau0:tau0 + n],
                                          kc == 0, kc == DC - 1),
                                 reads=[sres] + (h0all if kc == 0 else []), writes=[("ps", b)])
                        if kind == 0:
                            evac(Ut[:, oc, o0:o0 + n], ps[b][:, 0:n], [("ps", b)], [("U", oc)])
                        elif kind == 1:
                            evac(QT[:, oc, o0:o0 + n], ps[b][:, 0:n], [("ps", b)], [("QT", oc)])
                        else:
                            evac(KT[:, oc, o0:o0 + n], ps[b][:, 0:n], [("ps", b)], [("KT", oc)])
                if kind == 3:
                    f0 = 256 * (s % 4)
                    for t2 in ((5, 7) if cont else range(0, NTILE, 2)):
                        b = nb()
                        for tt in (t2, t2 + 1):
                            p = 128 if tt < 9 else 16
                            off = 256 * (tt - t2)
                            for kc in range(DC):
                                P.op("pe", MM(ps[b][0:p, off:off + 256], h0bf[:, kc, tt * 128:tt * 128 + p], sv[:, kc, :],
                                              kc == 0, kc == DC - 1),
                                     reads=[sres] + (h0all if kc == 0 else []), writes=[("ps", b)])
                        evac(Vt[:, t2, f0:f0 + 256], ps[b][:, 0:256], [("ps", b)], [("V", t2, s % 4)])
                        p = 128 if t2 + 1 < 9 else 16
                        evac(Vt[0:p, t2 + 1, f0:f0 + 256], ps[b][0:p, 256:512], [("ps", b)], [("V", t2 + 1, s % 4)])
            check_stop("P1b")
            P.fence("A")
            Vall = [("V", t, q) for t in range(NTILE) for q in range(4)]

            P.cur_tag = "P2a"
            P.op("dve", TS(Ut[:, :, 521:530], Ut[:, :, 521:530], fcol(F_FPOST), None, ALU.mult),
                 reads=[("U", c) for c in range(8)] + [FL], writes=[("U", c) for c in range(8)])
            for g in range(4):
                w = 2 ** (g + 1)
                for kc2 in range(2):
                    c = 2 * g + kc2
                    U = Ut[:, c, :]
                    P.op("dve", TT(pt1[:, 1:530], U[:, 0:529], U[:, 1:530], ALU.add), reads=[("U", c)], writes=["pt1"])
                    cur, curname = pt1, "pt1"
                    if g >= 1:
                        P.op("dve", TT(pt2[:, 2:529], pt1[:, 1:528], pt1[:, 3:530], ALU.add), reads=["pt1"], writes=["pt2"])
                        cur, curname = pt2, "pt2"
                    if g >= 2:
                        P.op("dve", TT(pt1[:, 4:527], pt2[:, 2:525], pt2[:, 6:529], ALU.add), reads=["pt2"], writes=["pt1"])
                        cur, curname = pt1, "pt1"
                    if g >= 3:
                        P.op("dve", TT(pt2[:, 8:523], pt1[:, 4:519], pt1[:, 12:527], ALU.add), reads=["pt1"], writes=["pt2"])
                        cur, curname = pt2, "pt2"
                    P.op("dve", TT(cur[:, 513:521], cur[:, 513:521], flg[:, fb + F_CORR + 8 * g: fb + F_CORR + 8 * g + 8], ALU.mult),
                         reads=[curname, FL], writes=[curname])
                    P.op("dve", STT(mT[:, c, :], cur[:, 8:8 + NM], 1.0 / w, U[:, 8:8 + NM], ALU.mult, ALU.subtract),
                         reads=[curname, ("U", c)], writes=[("m", c)])
                for e_ in range(2):
                    oc = 2 * g + e_
                    for half in range(2):
                        b = nb()
                        for kc2 in range(2):
                            P.op("pe", MM(ps[b][:, 0:257], wp[:, g, kc2, e_ * 128:(e_ + 1) * 128],
                                          mT[:, 2 * g + kc2, 257 * half:257 * half + 257], kc2 == 0, kc2 == 1),
                                 reads=[("m", 2 * g + kc2), ("wp", g)], writes=[("ps", b)])
                        P.op("act", ACTF(ypool[:, oc, 257 * half:257 * half + 257], ps[b][:, 0:257], AF.Identity,
                                         scale=cc(C_PSCALE + oc)),
                             reads=[("ps", b), "cst"], writes=[("ypool", oc)])

            check_stop("P2a")
            prefetch_slabs()
            P.cur_tag = "P2b"
            first = True
            for h in range(16):
                hp, pi = h // 2, h % 2
                pr = slice(64 * pi, 64 * pi + 64)
                for it, (kind, j) in enumerate(items_h):
                    col = h * NHALO + it
                    qcol = NM - 1 if kind in ("post", "metapost") else 0
                    if j is None:
                        lhs = KT[pr, hp, 1152:1280]
                        out = ps[7][:, col:col + 1]
                    else:
                        lhs = KT[pr, hp, 128 * j:128 * j + 128]
                        out = ps[7][:, col:col + 1]
                    P.op("pe", MM(out, lhs, QT[pr, hp, qcol:qcol + 1], True, True),
                         reads=[("KT", hp), ("QT", hp)], writes=[("ps", 7)])
            P.op("dve", STT(sbh[:, :], ps[7][:, 0:240], 0.125, cst[:, C_GH:C_GH + 240], ALU.mult, ALU.add),
                 reads=[("ps", 7), "cst"], writes=["sbh"])
            P.op("dve", TT(sbh[:, :], sbh[:, :], flg[:, fb + F_FH: fb + F_FH + 240], ALU.add), reads=["sbh", FL], writes=["sbh"])
            P.op("act", ACTF(pth[:, :], sbh[:, :], AF.Exp), reads=["sbh"], writes=["pth"])

            def load_gp(hp_):
                P.op("sp", DMA(gpb[hp_ % 2].rearrange("p a b -> p (a b)"), gp_d[hp_]), writes=[("gpb", hp_ % 2)],
                     dsem=("gp", hp_ % 2))

            load_gp(0)
            steps = [None] + [j for j in range(9) if plans[j] is not None]
            nst = len(steps)
            sbank = [0, 1, 2]

            def emit_S(hp, i, pi):
                j = steps[i]
                n = 2 * (hp * nst + i) + pi
                h = 2 * hp + pi
                pr = slice(64 * pi, 64 * pi + 64)
                b = sbank[n % 3]
                if j is None:
                    P.op("pe", MM(ps[b][:, 0:512], KT[pr, hp, 1152:1280], QT[pr, hp, 1:513], True, True),
                         reads=[("KT", hp), ("QT", hp)], writes=[("ps", b)])
                    P.op("act", ACTF(ptb[n % 8][:, :], ps[b][:, 0:512], AF.Exp, bias=cc(C_METAB + h), scale=0.125),
                         reads=[("ps", b), "cst"], writes=[("ptb", n % 8)])
                else:
                    ra, rb, segs = plans[j]
                    qa, qb = 64 * ra, 64 * rb + 64
                    n_ = qb - qa
                    k0 = 2 * j - 6
                    gofs = (ra - k0 + 7) * 64
                    P.op("pe", MM(ps[b][:, 0:n_], KT[pr, hp, 128 * j:128 * j + 128], QT[pr, hp, 1 + qa:1 + qb], True, True),
                         reads=[("KT", hp), ("QT", hp)], writes=[("ps", b)])
                    P.op("dve", STT(sbt[n % 3][:, 0:n_], ps[b][:, 0:n_], 0.125, gpb[hp % 2][:, pi, gofs:gofs + n_],
                                    ALU.mult, ALU.add),
                         reads=[("ps", b), ("gpb", hp % 2)], writes=[("sbt", n % 3)])
                    for (r0, r1, combo) in segs:
                        a0, a1 = 64 * r0 - qa, 64 * r1 + 64 - qa
                        P.op("act", ACTF(ptb[n % 8][:, a0:a1], sbt[n % 3][:, a0:a1], AF.Exp, bias=fcol(F_BTAB + combo)),
                             reads=[("sbt", n % 3), FL], writes=[("ptb", n % 8)])

            def emit_PV(hp, i, pi):
                j = steps[i]
                n = 2 * (hp * nst + i) + pi
                h = 2 * hp + pi
                bV, bO = 3 + hp % 2, 5 + hp % 2
                pr = slice(64 * pi, 64 * pi + 64)
                last = i == nst - 1
                if j is None:
                    P.op("pe", MM(ps[bV][pr, 0:512], Vt[:, 9, 64 * h:64 * h + 64], ptb[n % 8][:, :], True, False),
                         reads=[("ptb", n % 8)] + Vall, writes=[("ps", bV)])
                    P.op("pe", MM(ps[bO][pr, 0:512], onesb[:, 0:64], ptb[n % 8][:, :], True, False),
                         reads=[("ptb", n % 8), "ones"], writes=[("ps", bO)])
                else:
                    ra, rb, segs = plans[j]
                    qa, qb = 64 * ra, 64 * rb + 64
                    n_ = qb - qa
                    P.op("pe", MM(ps[bV][pr, qa:qb], Vt[:, j, 64 * h:64 * h + 64], ptb[n % 8][:, 0:n_], False, last),
                         reads=[("ptb", n % 8)], writes=[("ps", bV)])
                    P.op("pe", MM(ps[bO][pr, qa:qb], onesb[:, 0:64], ptb[n % 8][:, 0:n_], False, last),
                         reads=[("ptb", n % 8), "ones"], writes=[("ps", bO)])

            def emit_norm(hp):
                bV, bO = 3 + hp % 2, 5 + hp % 2
                P.op("dve", lambda e, bO=bO: e.reciprocal(out=pt1[:, 0:512], in_=ps[bO][:, 0:512]), reads=[("ps", bO)],
                     writes=["pt1"])
                P.op("dve", TT(yattn[:, hp, 1:513], ps[bV][:, 0:512], pt1[:, 0:512], ALU.mult),
                     reads=[("ps", bV), "pt1"], writes=[("yattn", hp)])

            allsteps = [(hp, i) for hp in range(8) for i in range(nst)]
            LAG = 2
            for idx in range(len(allsteps) + LAG):
                if idx < len(allsteps):
                    hp, i = allsteps[idx]
                    if i == 0 and hp + 1 < 8:
                        load_gp(hp + 1)
                    emit_S(hp, i, 0)
                    emit_S(hp, i, 1)
                if idx >= LAG:
                    hp2, i2 = allsteps[idx - LAG]
                    emit_PV(hp2, i2, 0)
                    emit_PV(hp2, i2, 1)
                    if i2 == nst - 1:
                        emit_norm(hp2)
            for h in range(16):
                hp, pi = h // 2, h % 2
                pr = slice(64 * pi, 64 * pi + 64)
                for which in range(2):
                    its = [(it, kj) for it, kj in enumerate(items_h)
                           if (kj[0] in ("post", "metapost")) == (which == 1)]
                    cV = 256 + which * 8 + hp
                    cO = 288 + which * 8 + hp
                    for pass_ in range(2):
                        for ii, (it, (kind, j)) in enumerate(its):
                            col = h * NHALO + it
                            firstf, lastf = ii == 0, ii == len(its) - 1
                            tj = 9 if j is None else j
                            if pass_ == 0:
                                P.op("pe", MM(ps[7][pr, cV:cV + 1], Vt[:, tj, 64 * h:64 * h + 64], pth[:, col:col + 1], firstf, lastf),
                                     reads=["pth"] + (Vall if ii == 0 else []), writes=[("ps", 7)])
                            else:
                                P.op("pe", MM(ps[7][pr, cO:cO + 1], onesb[:, 0:64], pth[:, col:col + 1], firstf, lastf),
                                     reads=["pth", "ones"], writes=[("ps", 7)])
            P.op("dve", lambda e: e.reciprocal(out=rech[:, :], in_=ps[7][:, 288:304]), reads=[("ps", 7)], writes=["rech"])
            P.op("dve", TT(yattn[:, :, 0], ps[7][:, 256:264], rech[:, 0:8], ALU.mult), reads=[("ps", 7), "rech"],
                 writes=[("yattn", c) for c in range(8)])
            P.op("dve", TT(yattn[:, :, NM - 1], ps[7][:, 264:272], rech[:, 8:16], ALU.mult), reads=[("ps", 7), "rech"],
                 writes=[("yattn", c) for c in range(8)])
            check_stop("P2b")
            P.fence("A")
            P.fence("B")

            P.cur_tag = "P3"
            ln1 = ln_begin([(0, 257), (257, 257)])
            for s in range(8):
                sl_, sres = next_slab()
                sv = slab[sl_].rearrange("p (kc n) -> p kc n", kc=16)
                for e_ in range(2):
                    oc = 2 * s + e_
                    if oc >= 2:
                        ln_chunk(ln1, oc - 2)
                    for half in range(2):
                        b = nb()
                        hs = slice(257 * half, 257 * half + 257)
                        for kc in range(DC):
                            src = ypool[:, kc, hs] if kc < 8 else yattn[:, kc - 8, hs]
                            rd = ("ypool", kc) if kc < 8 else ("yattn", kc - 8)
                            P.op("pe", MM(ps[b][:, 0:257], sv[:, kc, e_ * 128:(e_ + 1) * 128], src, kc == 0, kc == DC - 1),
                                 reads=[sres, rd], writes=[("ps", b)])
                        P.op("dve", STT(h0f[:, oc, hs], h0f[:, oc, hs], ALPHA, ps[b][:, 0:257], ALU.mult, ALU.add),
                             reads=[("ps", b), ("h0f", oc)], writes=[("h0f", oc)])
            check_stop("P3a")
            P.cur_tag = "LN1"
            ln_chunk(ln1, DC - 2)
            ln_chunk(ln1, DC - 1)
            ln_finish(ln1, C_LN1_G, C_LN1_B, True)
            check_stop("P3")
            P.fence("Y")

            P.cur_tag = "P4up"
            for part in range(2):
                P.cur_tag = "P4up"
                j0 = 0 if part == 0 else 22
                jr = range(0, 22) if part == 0 else range(22, NFC)
                def up_post(j, banks4):
                    zi = j % 2
                    for ag in range(2):
                        for half in range(2):
                            b = banks4[2 * ag + half]
                            hs = slice(257 * half, 257 * half + 257)
                            P.op("act", ACTF(zb[zi][:, ag, hs], ps[b][:, 0:257], AF.Identity, bias=cc(C_BUP + ag * NFC + j)),
                                 reads=[("ps", b), "cst"], writes=[("z", zi)])
                    P.op("dve", TS(zb[zi][:, :, NM - 1:NM], zb[zi][:, :, NM - 1:NM], fcol(F_FPOST), None, ALU.mult),
                         reads=[("z", zi), FL], writes=[("z", zi)])
                    outs = [(cab, "ca"), (cgb, "cg")]
                    for ag in range(2):
                        ob, on = outs[ag]
                        cj = ag * NFC + j
                        z = zb[zi][:, ag, :]
                        P.op("act", ACTF(ob[zi][:, :], z[:, 1:513], AF.Identity, bias=cc(C_CB + cj), scale=cc(C_CW + 86 + cj)),
                             reads=[("z", zi), "cst"], writes=[(on, zi)])
                        P.op("dve", STT(ob[zi][:, :], z[:, 0:512], cc(C_CW + cj), ob[zi][:, :], ALU.mult, ALU.add),
                             reads=[("z", zi), (on, zi), "cst"], writes=[(on, zi)])
                        P.op("dve", STT(ob[zi][:, :], z[:, 2:514], cc(C_CW + 172 + cj), ob[zi][:, :], ALU.mult, ALU.add),
                             reads=[("z", zi), (on, zi), "cst"], writes=[(on, zi)])
                    P.op("act", ACTF(ggb[zi][:, :], cgb[zi][:, :], AF.Gelu), reads=[("cg", zi)], writes=[("gg", zi)])
                    P.op("dve", TT(actb[:, j - j0, :], cab[zi][:, :], ggb[zi][:, :], ALU.mult), reads=[("ca", zi), ("gg", zi)],
                         writes=[("act", j - j0)])

                jlist = list(jr)
                if part == 0:
                    grp = []
                    for j in jlist[:2]:
                        sl_, sres = next_slab(4 if j == jlist[0] else 3)
                        sv = slab[sl_].rearrange("p (kc n) -> p kc n", kc=16)
                        grp.append((j, sres, sv, [nb() for _ in range(4)]))
                    for kc in range(DC):
                        for (j, sres, sv, banks4) in grp:
                            for ag in range(2):
                                for half in range(2):
                                    b = banks4[2 * ag + half]
                                    hs = slice(257 * half, 257 * half + 257)
                                    P.op("pe", MM(ps[b][:, 0:257], sv[:, kc, ag * 128:(ag + 1) * 128], h1bf[:, kc, hs], kc == 0, kc == DC - 1),
                                         reads=[sres, ("h1bf", kc)], writes=[("ps", b)])
                    for (j, sres, sv, banks4) in grp:
                        up_post(j, banks4)
                    jlist = jlist[2:]
                for j in jlist:
                    sl_, sres = next_slab()
                    sv = slab[sl_].rearrange("p (kc n) -> p kc n", kc=16)
                    banks4 = []
                    for ag in range(2):
                        for half in range(2):
                            b = nb()
                            banks4.append(b)
                            hs = slice(257 * half, 257 * half + 257)
                            for kc in range(DC):
                                P.op("pe", MM(ps[b][:, 0:257], sv[:, kc, ag * 128:(ag + 1) * 128], h1bf[:, kc, hs], kc == 0, kc == DC - 1),
                                     reads=[sres, ("h1bf", kc)], writes=[("ps", b)])
                    up_post(j, banks4)
                P.cur_tag = "P4dn"
                k0, nk = (0, 22) if part == 0 else (22, 21)
                if part == 1:
                    ln2 = ln_begin([(1, 512)])
                for oc in range(DC):
                    if part == 1 and oc >= 2:
                        ln_chunk(ln2, oc - 2)
                    b = nb()
                    sl_, sres = next_slab()
                    sv = slab[sl_][:, 0:nk * 128].rearrange("p (kc n) -> p kc n", kc=nk)
                    for kk in range(nk):
                        P.op("pe", MM(ps[b][:, 0:512], sv[:, kk, :], actb[:, kk, :], kk == 0, kk == nk - 1),
                             reads=[sres, ("act", kk)], writes=[("ps", b)])
                    P.op("dve", STT(h0f[:, oc, 1:513], h0f[:, oc, 1:513], ALPHA if part == 0 else 1.0, ps[b][:, 0:512], ALU.mult, ALU.add),
                         reads=[("ps", b), ("h0f", oc)], writes=[("h0f", oc)])
            check_stop("P4")
            P.fence("Y")
            P.fence("B")

            P.cur_tag = "P5"
            ln_chunk(ln2, DC - 2)
            ln_chunk(ln2, DC - 1)
            ln_finish(ln2, C_LN2_G, C_LN2_B, False)
            check_stop("P5a")
            for tt in range(4):
                ob_ = tt % 2
                for g4 in range(4):
                    b = nb()
                    for k in range(4):
                        c = 4 * g4 + k
                        P.op("pe", TR(ps[b][:, k * 128:(k + 1) * 128], h0f[:, c, 1 + 128 * tt:1 + 128 * tt + 128], ident[:, :]),
                             reads=[("h0f", c), "ident"], writes=[("ps", b)])
                    evac(otile[ob_][:, 512 * g4:512 * g4 + 512], ps[b][:, :], [("ps", b)], [("otile", ob_, g4)])
                o = P.op("pool", DMA(y_d[u, 128 * tt:128 * tt + 128, :], otile[ob_][:, :]),
                         reads=[("otile", ob_, g4) for g4 in range(4)], dsem=("out", ob_))
                out_ops.append(o)
            P.fence("Y")
            P.fence("A")
          except _Stop:
            break

        finals = {}
        for o in out_ops:
            finals[o.dsem] = o
        P.emit(final_wait_ops=list(finals.values()))
    return nc


def _unit_table():
    units = []
    for core in range(NCORE):
        ps_, run = core // 4, core % 4
        for k in range(2):
            units.append((ps_, 2 * run + k, 8))
        for k in range(4):
            units.append((2 + ps_, 4 * run + k, 16))
    return units


def _status_value(stt, start, end):
    if stt == ST_I:
        return 0.0
    if stt == ST_S:
        return 0.0 if start else -BIG
    if stt == ST_E:
        return 0.0 if end else -BIG
    if stt == ST_NS:
        return -BIG if start else 0.0
    if stt == ST_NE:
        return -BIG if end else 0.0
    return -BIG


def _build_gp(rpb):
    H = rpb.shape[0]
    gp = np.zeros((H, 2, 64, 16, 64), np.float32)
    c = np.arange(64)
    cs = np.clip(c - 8, 0, 48)
    key = np.arange(64)
    ok = (key[:, None] >= cs[None, :]) & (key[:, None] < cs[None, :] + 16)
    dc = np.clip(key[:, None] - c[None, :] + 15, 0, 30)
    for rp in range(2):
        for di in range(16):
            dr = rp - (di - 7) + 7
            if 0 <= dr <= 14:
                vals = rpb[:, dr, :][:, dc]
                gp[:, rp, :, di, :] = np.where(ok[None], vals, np.float32(-BIG))
            else:
                gp[:, rp, :, di, :] = np.where(ok, np.float32(0.0), np.float32(-BIG))[None]
    return gp.reshape(H, 128, 1024)


def _prepare(x_prompt, x_sample, meta_tokens, ln_in_g, ln_in_b, w_in, w_pool, pool_scale, rpb, meta_bias,
             w_out, ln1_g, ln1_b, w_up, b_up, conv_w, conv_b, w_down, ln2_g, ln2_b, cores=None):
    f32 = np.float32
    xs = [np.asarray(x_prompt[0], f32), np.asarray(x_prompt[1], f32), np.asarray(x_sample[0], f32),
          np.asarray(x_sample[1], f32)]
    meta = np.asarray(meta_tokens, f32)
    rpb0 = np.asarray(rpb, f32)[0]
    mb0 = np.asarray(meta_bias, f32)[0]
    units = _unit_table()
    assert len(units) == NCORE * NU

    gp = _build_gp(rpb0)
    gp_pairs = np.ascontiguousarray(gp.reshape(8, 2, 128, 1024).transpose(0, 2, 1, 3).reshape(8, 128, 2048))
    cst = np.zeros((128, NCONST), f32)
    col = lambda v: np.asarray(v, f32).reshape(-1, 128).T
    cst[:, C_LNIN_G:C_LNIN_G + 16] = col(ln_in_g)
    cst[:, C_LNIN_B:C_LNIN_B + 16] = col(ln_in_b)
    cst[:, C_LN1_G:C_LN1_G + 16] = col(ln1_g)
    cst[:, C_LN1_B:C_LN1_B + 16] = col(ln1_b)
    cst[:, C_LN2_G:C_LN2_G + 16] = col(ln2_g)
    cst[:, C_LN2_B:C_LN2_B + 16] = col(ln2_b)
    cst[:, C_PSCALE:C_PSCALE + 8] = col(pool_scale)
    cst[:, C_BUP:C_BUP + 86] = col(b_up)
    cw = np.asarray(conv_w, f32)[0]
    for k in range(3):
        cst[:, C_CW + 86 * k:C_CW + 86 * k + 86] = col(cw[k])
    cst[:, C_CB:C_CB + 86] = col(conv_b)
    cst[:, C_METAB:C_METAB + 16] = -BIG
    cst[0:16, C_METAB:C_METAB + 16] = mb0.T
    items = halo_items()
    for h in range(16):
        for it, (kind, j) in enumerate(items):
            c_ = C_GH + h * NHALO + it
            if j is None:
                cst[:, c_] = -BIG
                cst[0:16, c_] = mb0[h]
            else:
                dl, cq = halo_geom(kind, j)
                cst[:, c_] = gp[h][:, (dl + 7) * 64 + cq]

    def unit_flags(start, end):
        fl = np.zeros((128, NFLAG), f32)
        for stt in range(6):
            for sbb in range(6):
                fl[0:64, F_BTAB + stt * 6 + sbb] = _status_value(stt, start, end)
                fl[64:128, F_BTAB + stt * 6 + sbb] = _status_value(sbb, start, end)
        for h in range(16):
            for it, (kind, j) in enumerate(items):
                if j is None:
                    continue
                k0 = 2 * j - 6
                c_ = F_FH + h * NHALO + it
                fl[0:64, c_] = _status_value(halo_status(kind, k0), start, end)
                fl[64:128, c_] = _status_value(halo_status(kind, k0 + 1), start, end)
        fl[:, F_FPOST] = 0.0 if end else 1.0
        for g in range(4):
            w = 2 ** (g + 1)
            half = w // 2
            for i in range(8):
                tau_own = 504 + i
                cnt = w
                if end and tau_own + half > 512:
                    cnt = 512 - tau_own + half
                fl[:, F_CORR + 8 * g + i] = f32(w) / f32(cnt)
        return fl

    in_maps = []
    for core in (range(NCORE) if cores is None else cores):
        xu = np.zeros((NU, TOK, D), f32)
        flg = np.zeros((NU, 128, NFLAG), f32)
        for k in range(NU):
            s, uu, nun = units[core * NU + k]
            X = xs[s]
            nrows = nun * 8
            r0 = 8 * uu
            for row in range(-6, 12):
                gr = r0 + row
                if 0 <= gr < nrows:
                    xu[k, (row + 6) * 64:(row + 7) * 64] = X[gr * 64:(gr + 1) * 64]
            if uu == 0:
                xu[k, 384 - 16:384] = meta
            xu[k, 1152:1168] = meta
            flg[k] = unit_flags(uu == 0, uu == nun - 1)
        in_maps.append({
            "xu": xu, "w_in": np.ascontiguousarray(np.asarray(w_in, f32)[0]),
            "w_out": np.ascontiguousarray(np.asarray(w_out, f32)[0]),
            "w_up": np.ascontiguousarray(np.asarray(w_up, f32)[0]),
            "w_down": np.ascontiguousarray(np.asarray(w_down, f32)[0]),
            "w_pool": np.ascontiguousarray(np.asarray(w_pool, f32)[0]),
            "gp": gp_pairs, "cst": cst, "flg": flg,
        })

    return in_maps, units


def kernel(**inputs):
    in_maps, units = _prepare(**inputs)
    nc = build_program()
    res = run_bass_kernel_spmd(nc, in_maps, core_ids=list(range(NCORE)))
    return _assemble(res, units)


def _assemble(res, units):
    f32 = np.float32
    yp = np.zeros((2, 4096, D), f32)
    ysm = np.zeros((2, 8192, D), f32)
    outs = [yp[0], yp[1], ysm[0], ysm[1]]
    for core in range(NCORE):
        y = np.asarray(res.results[core]["y"], f32)
        for k in range(NU):
            s, uu, nun = units[core * NU + k]
            outs[s][512 * uu:512 * uu + 512] = y[k]
    return (yp, ysm)
```

```python
import contextlib
import os
import numpy as np
import concourse.bass as bass
import concourse.mybir as mybir
from concourse.bass_utils import run_bass_kernel_spmd

F32 = mybir.dt.float32
BF = mybir.dt.bfloat16
AF = mybir.ActivationFunctionType
ALU = mybir.AluOpType

D = 2048
DC = 16
NCORE = 8
NU = 6
TOK = 1168
NTILE = 10
NM = 514
NUC = 530
TAU_M = 383
TAU_U = 375
DFF = 5504
NFC = 43
BIG = 30000.0
ALPHA = float(2.0 ** 0.25)
LN_EPS = 1e-5
N_META = 16

C_LNIN_G, C_LNIN_B, C_LN1_G, C_LN1_B, C_LN2_G, C_LN2_B = 0, 16, 32, 48, 64, 80
C_PSCALE = 96
C_BUP = 104
C_CW = 190
C_CB = 448
C_METAB = 534
C_GH = 550
NCONST = 790
F_BTAB, F_FH, F_FPOST, F_CORR = 0, 36, 276, 277
NFLAG = 309
NHALO = 15
NSLAB = 5

ENGS = ("pe", "act", "dve", "pool", "sp")


class Op:
    __slots__ = ("eng", "fn", "deps", "is_dma", "dsem", "needs_inc", "semval")

    def __init__(self, eng, fn, dsem=None):
        self.eng = eng
        self.fn = fn
        self.deps = ()
        self.is_dma = dsem is not None
        self.dsem = dsem
        self.needs_inc = False
        self.semval = None


class Prog:
    def __init__(self, nc):
        self.nc = nc
        self.ops = {e: [] for e in ENGS}
        self.last_w = {}
        self.readers = {}
        self.reg_of = {}
        self.reg_cur = {}
        self.reg_fence = {}
        self.dma_cnt = {}
        self.excl = {}
        self.pe_tags = []
        self.cur_tag = "setup"
        self.trace = None

    def region(self, reg, names):
        for n in names:
            self.reg_of[n] = reg

    def fence(self, reg):
        cur = self.reg_cur.get(reg)
        if cur:
            self.reg_fence[reg] = dict(cur)
            self.reg_cur[reg] = {}

    @staticmethod
    def _key(o):
        return ("dma", id(o)) if o.is_dma else o.eng

    def op(self, eng, fn, reads=(), writes=(), dsem=None, extra_deps=()):
        o = Op(eng, fn, dsem)
        if eng == "pe":
            self.pe_tags.append(self.cur_tag)
        deps = {id(x): x for x in extra_deps}
        regs = set()
        px = [r for r in tuple(reads) + tuple(writes) if isinstance(r, tuple) and r[0] == "ps"]
        if px:
            reads = [r for r in reads if not (isinstance(r, tuple) and r[0] == "ps")]
            writes = [r for r in writes if not (isinstance(r, tuple) and r[0] == "ps")]
            for r in px:
                prev = self.excl.get(r)
                if prev is not None and prev.eng != eng:
                    deps[id(prev)] = prev
                self.excl[r] = o
        for r in reads:
            w = self.last_w.get(r)
            if w is not None:
                deps[id(w)] = w
            nm = r[0] if isinstance(r, tuple) else r
            rg = self.reg_of.get(nm)
            if rg is not None:
                regs.add(rg)
        for r in writes:
            w = self.last_w.get(r)
            if w is not None:
                deps[id(w)] = w
            rd = self.readers.get(r)
            if rd:
                for x in rd.values():
                    deps[id(x)] = x
            nm = r[0] if isinstance(r, tuple) else r
            rg = self.reg_of.get(nm)
            if rg is not None:
                regs.add(rg)
        for rg in regs:
            f = self.reg_fence.get(rg)
            if f:
                for x in f.values():
                    deps[id(x)] = x
            self.reg_cur.setdefault(rg, {})[self._key(o)] = o
        o.deps = tuple(deps.values())
        k = self._key(o)
        for r in reads:
            self.readers.setdefault(r, {})[k] = o
        for r in writes:
            self.last_w[r] = o
            self.readers[r] = {}
        self.ops[eng].append(o)
        return o

    def emit(self, final_wait_ops=()):
        nc = self.nc
        for o in final_wait_ops:
            if not o.is_dma:
                o.needs_inc = True
        for e in ENGS:
            for o in self.ops[e]:
                for d in o.deps:
                    if d.is_dma:
                        continue
                    if d.eng == o.eng and d.eng == "pe":
                        continue
                    d.needs_inc = True
        for e in ENGS:
            c = 0
            for o in self.ops[e]:
                if o.is_dma:
                    self.dma_cnt[o.dsem] = self.dma_cnt.get(o.dsem, 0) + 16
                    o.semval = self.dma_cnt[o.dsem]
                elif o.needs_inc:
                    c += 1
                    o.semval = c
        with contextlib.ExitStack() as st:
            esem = {e: st.enter_context(nc.semaphore("prog_" + e)) for e in ENGS}
            dsems = {}
            for i, k in enumerate(sorted(self.dma_cnt.keys(), key=str)):
                dsems[k] = st.enter_context(nc.semaphore("dma%d" % i))
            block = st.enter_context(nc.Block())

            def run(ename, eng):
                seen = {}
                for o in self.ops[ename]:
                    need = {}
                    for d in o.deps:
                        if d.is_dma:
                            s = ("d", d.dsem)
                        else:
                            if d.eng == ename and ename == "pe":
                                continue
                            s = ("e", d.eng)
                        if need.get(s, 0) < d.semval:
                            need[s] = d.semval
                    for s, v in need.items():
                        if seen.get(s, 0) >= v:
                            continue
                        seen[s] = v
                        eng.wait_ge(dsems[s[1]] if s[0] == "d" else esem[s[1]], v)
                        if self.trace is not None:
                            self.trace.append((ename, "wait", s, v))
                    ins = o.fn(eng)
                    if self.trace is not None:
                        self.trace.append((ename, "op", getattr(o, "tag", None), (o.dsem if o.is_dma else ("inc" if o.needs_inc else None)), o.semval))
                    if o.is_dma:
                        ins.then_inc(dsems[o.dsem], 16)
                    elif o.needs_inc:
                        ins.then_inc(esem[ename], 1)
                if ename == "sp":
                    for o in final_wait_ops:
                        eng.wait_ge(dsems[o.dsem] if o.is_dma else esem[o.eng], o.semval)

            @block.tensor
            def _(e):
                run("pe", e)

            @block.scalar
            def _(e):
                run("act", e)

            @block.vector
            def _(e):
                run("dve", e)

            @block.gpsimd
            def _(e):
                run("pool", e)

            @block.sync
            def _(e):
                run("sp", e)


def MM(out, lhsT, rhs, start, stop):
    return lambda e: e.matmul(out, lhsT=lhsT, rhs=rhs, start=start, stop=stop)


def TR(out, in_, ident):
    return lambda e: e.transpose(out=out, in_=in_, identity=ident)


def ACTF(out, in_, func, bias=None, scale=None):
    kw = {}
    if bias is not None:
        kw["bias"] = bias
    if scale is not None:
        kw["scale"] = scale
    return lambda e: e.activation(out=out, in_=in_, func=func, **kw)


def STT(out, in0, scalar, in1, op0, op1):
    return lambda e: e.scalar_tensor_tensor(out=out, in0=in0, scalar=scalar, in1=in1, op0=op0, op1=op1)


def TT(out, in0, in1, op):
    return lambda e: e.tensor_tensor(out=out, in0=in0, in1=in1, op=op)


def TS(out, in0, s1, s2, op0, op1=None):
    if op1 is None:
        return lambda e: e.tensor_scalar(out=out, in0=in0, scalar1=s1, scalar2=None, op0=op0)
    return lambda e: e.tensor_scalar(out=out, in0=in0, scalar1=s1, scalar2=s2, op0=op0, op1=op1)


def CP(out, in_):
    return lambda e: e.tensor_copy(out=out, in_=in_)


def ACP(out, in_):
    return lambda e: e.copy(out=out, in_=in_)


def DMA(out, in_):
    return lambda e: e.dma_start(out=out, in_=in_)


ST_I, ST_S, ST_E, ST_NS, ST_NE, ST_NV = 0, 1, 2, 3, 4, 5


def key_status_own(kap, rho):
    d = kap - rho
    if kap == -6:
        return ST_NV
    if 0 <= kap <= 7:
        if -4 <= d <= 3:
            return ST_I
        return ST_S if d >= 4 else ST_E
    if kap < 0:
        return ST_NS if d >= -4 else ST_NV
    return ST_NE if d <= 3 else ST_NV


def pair_plan(j):
    k0 = 2 * j - 6
    per = []
    for rho in range(8):
        st, sb = key_status_own(k0, rho), key_status_own(k0 + 1, rho)
        per.append(None if (st == ST_NV and sb == ST_NV) else st * 6 + sb)
    rows = [r for r in range(8) if per[r] is not None]
    if not rows:
        return None
    ra, rb = rows[0], rows[-1]
    segs = []
    r = ra
    while r <= rb:
        assert per[r] is not None
        r2 = r
        while r2 + 1 <= rb and per[r2 + 1] == per[r]:
            r2 += 1
        segs.append((r, r2, per[r]))
        r = r2 + 1
    return ra, rb, segs


def halo_items():
    items = []
    for j in range(0, 5):
        items.append(("preI", j))
    for j in range(3, 7):
        items.append(("preS", j))
    for j in range(5, 9):
        items.append(("post", j))
    items.append(("metapre", None))
    items.append(("metapost", None))
    assert len(items) == NHALO
    return items


def halo_status(kind, kap):
    if kind == "preI":
        return ST_NS if -5 <= kap <= 2 else ST_NV
    if kind == "preS":
        return ST_S if 0 <= kap <= 7 else ST_NV
    if kind == "post":
        if 4 <= kap <= 7:
            return ST_I
        return ST_NE if 8 <= kap <= 11 else ST_NV
    raise ValueError


def halo_geom(kind, j):
    k0 = 2 * j - 6
    if kind == "preI":
        return -1 - k0, 63
    if kind == "preS":
        return 0 - k0, 0
    return 8 - k0, 0


def build_program(NU=NU, stop_after=None):
    nc = bass.Bass("TRN2", target_bir_lowering=False)
    dt_in = lambda n, s: nc.dram_tensor(n, s, F32, kind="ExternalInput").ap()
    xu = dt_in("xu", [NU, TOK, D])
    w_in = dt_in("w_in", [D, 4096])
    w_out = dt_in("w_out", [D, D])
    w_up = dt_in("w_up", [D, 2 * DFF])
    w_down = dt_in("w_down", [DFF, D])
    w_pool = dt_in("w_pool", [4, 256, 256])
    gp_d = dt_in("gp", [8, 128, 2048])
    cst_d = dt_in("cst", [128, NCONST])
    flg_d = dt_in("flg", [NU, 128, NFLAG])
    y_d = nc.dram_tensor("y", [NU, 512, D], F32, kind="ExternalOutput").ap()
    dbg_d = None
    if stop_after is not None:
        dbg_d = {"A": nc.dram_tensor("dbgA", [128, 9344], F32, kind="ExternalOutput").ap(),
                 "H": nc.dram_tensor("dbgH", [128, DC * NM], F32, kind="ExternalOutput").ap(),
                 "B": nc.dram_tensor("dbgB", [128, 16536], F32, kind="ExternalOutput").ap(),
                 "Y": nc.dram_tensor("dbgY", [128, 6168], F32, kind="ExternalOutput").ap()}
    wb_in = nc.dram_tensor("wb_in", [16, 128, 4096], BF, kind="Internal").ap()
    wb_out = nc.dram_tensor("wb_out", [8, 128, 4096], BF, kind="Internal").ap()
    wb_up = nc.dram_tensor("wb_up", [NFC, 128, 4096], BF, kind="Internal").ap()
    wb_dn = nc.dram_tensor("wb_dn", [32, 128, 22 * 128], BF, kind="Internal").ap()

    with contextlib.ExitStack() as st:
        def sb(n, words):
            return st.enter_context(nc.sbuf_tensor(n, [128, words], F32))

        RA = sb("RA", 9344)
        RH = sb("RH", DC * NM)
        RB = sb("RB", 16536)
        RY = sb("RY", 6168)
        RS = sb("RS", NSLAB * 2048)
        cst = sb("cst_sb", NCONST)
        flg = sb("flg_sb", NFLAG)
        ident = sb("ident", 128)
        onesw = sb("onesw", 64)
        wpw = sb("wpw", 1024)
        small = sb("small", 96)
        ps = [st.enter_context(nc.psum_tensor("ps%d" % i, [128, 512], F32)) for i in range(8)]

        def vbf(t, off_w, shape):
            n = int(np.prod(shape))
            a = t.bitcast(BF)[:, 2 * off_w: 2 * off_w + n]
            if len(shape) == 1:
                return a
            names = "abcd"[:len(shape)]
            pat = "p (%s) -> p %s" % (" ".join(names), " ".join(names))
            return a.rearrange(pat, **{names[i]: shape[i] for i in range(len(shape) - 1)})

        def vf(t, off_w, shape):
            n = int(np.prod(shape))
            a = t[:, off_w: off_w + n]
            if len(shape) == 1:
                return a
            names = "abcd"[:len(shape)]
            pat = "p (%s) -> p %s" % (" ".join(names), " ".join(names))
            return a.rearrange(pat, **{names[i]: shape[i] for i in range(len(shape) - 1)})

        h0bf = vbf(RA, 0, [DC, TOK])
        gpb = [vf(RA, 0, [2, 1024]), vf(RA, 2048, [2, 1024])]
        sbt = [vf(RA, 4096, [512]), vf(RA, 4608, [512]), vf(RA, 5120, [512])]
        ptb = [vbf(RA, 5632, [512]), vbf(RA, 5888, [512]), vbf(RA, 6144, [512]), vbf(RA, 6400, [512]),
               vbf(RA, 8100, [512]), vbf(RA, 8356, [512]), vbf(RA, 8612, [512]), vbf(RA, 8868, [512])]
        sbh = vf(RA, 6656, [240])
        pth = vbf(RA, 6896, [240])
        rech = vf(RA, 7016, [16])
        pt1 = vf(RA, 7040, [NUC])
        pt2 = vf(RA, 7040 + NUC, [NUC])
        h1bf = vbf(RA, 0, [DC, NM])
        lxb = [vbf(RA, 4112 + 257 * i, [NM]) for i in range(3)]
        lxq = [vbf(RA, 4112 + 771 + 257 * i, [NM]) for i in range(3)]
        lmean = vf(RA, 5654, [NM])
        lmsq = vf(RA, 5654 + NM, [NM])
        lrstd = vf(RA, 5654 + 2 * NM, [NM])
        ltmp = [vf(RA, 7196 + NM * i, [NM]) for i in range(4)]
        h0f = vf(RH, 0, [DC, NM])
        KT = vbf(RB, 0, [8, 1280])
        Vt = vbf(RB, 5120, [NTILE, 1024])
        QT = vbf(RB, 10240, [8, NM])
        Ut = vf(RB, 12296, [8, NUC])
        actb = vbf(RB, 10240, [22, 512])
        xt = [vf(RY, 0, [D]), vf(RY, 2048, [D]), vf(RY, 4096, [D])]
        mT = vbf(RY, 0, [8, NM])
        ypool = vbf(RY, 2056, [8, NM])
        yattn = vbf(RY, 4112, [8, NM])
        zb = [vf(RY, 0, [2, NM]), vf(RY, 1028, [2, NM])]
        cab = [vf(RY, 2056, [512]), vf(RY, 2568, [512])]
        cgb = [vf(RY, 3080, [512]), vf(RY, 3592, [512])]
        ggb = [vf(RY, 4104, [512]), vf(RY, 4616, [512])]
        otile = [vf(RY, 0, [D]), vf(RY, 2048, [D])]
        slab = [vbf(RS, 2048 * i, [4096]) for i in range(NSLAB)]
        onesb = vbf(onesw, 0, [128])
        wp = vbf(wpw, 0, [4, 2, 256])
        stt_l = [vf(small, 32 * i, [4, 6]) for i in range(3)]
        mv_l = [vf(small, 32 * i + 24, [2]) for i in range(3)]
        rstd_l = [vf(small, 32 * i + 26, [1]) for i in range(3)]
        nmr_l = [vf(small, 32 * i + 27, [1]) for i in range(3)]

        P = Prog(nc)
        nc._prog = P
        if "trace" in os.environ.get("KDBG", ""):
            P.trace = []
            nc._ptrace = P.trace
        P.region("A", ["h0bf", "gpb", "sbt", "ptb", "sbh", "pth", "rech", "pt1", "pt2", "h1bf", "lxb", "lxq",
                       "lstat", "ltmp"])
        P.region("B", ["KT", "V", "QT", "U", "act"])
        P.region("Y", ["xt", "m", "ypool", "yattn", "z", "ca", "cg", "gg", "otile"])

        bank_ctr = [0]

        reserved = set()

        def nb():
            while True:
                b = bank_ctr[0] % 8
                bank_ctr[0] += 1
                if b not in reserved:
                    return b

        cc = lambda col: cst[:, col:col + 1]

        P.op("sp", DMA(cst[:], cst_d[:]), writes=["cst"], dsem="cst")
        P.op("dve", lambda e: e.memset(ident[:], 0.0), writes=["ident"])
        P.op("pool", lambda e: e.affine_select(out=ident[:], in_=ident[:], pattern=[[-1, 128]],
                                               compare_op=ALU.not_equal, fill=1.0, base=0, channel_multiplier=1),
             reads=["ident"], writes=["ident"])
        P.op("dve", lambda e: e.memset(onesb, 1.0), writes=["ones"])
        P.op("dve", lambda e: e.memset(RB[:, :], 0.0), writes=[("KT", c) for c in range(8)] + [("V", 9, q) for q in range(4)])

        cv_n = [0]


        dbgflags = os.environ.get("KDBG", "")

        def conv_dma(out, in_, res):
            if "noconv" in dbgflags:
                return
            if "only_" in dbgflags and ("only_" + res[0]) not in dbgflags:
                return
            n = cv_n[0]
            cv_n[0] += 1
            P.op("pool", DMA(out, in_), writes=[res, ("cvslot", n % 5)], dsem=("cv", n % 5))

        w_in_v = w_in.rearrange("(kc p) n -> p kc n", p=128)
        w_out_v = w_out.rearrange("(kc p) n -> p kc n", p=128)
        w_up_v = w_up.rearrange("(kc p) n -> p kc n", p=128)
        w_dn_v = w_down.rearrange("(kc p) n -> p kc n", p=128)
        for s in range(16):
            conv_dma(wb_in[s].rearrange("p (kc n) -> p kc n", kc=16), w_in_v[:, :, 256 * s:256 * s + 256], ("wb_in", s))
        for g in range(4):
            if "nowp" in dbgflags:
                continue
            P.op("pool", DMA(wp[:, g], w_pool[g].rearrange("(kc p) e -> p kc e", p=128)), writes=[("wp", g)],
                 dsem=("wpl", g))
        for s in range(8):
            conv_dma(wb_out[s].rearrange("p (kc n) -> p kc n", kc=16), w_out_v[:, :, 256 * s:256 * s + 256], ("wb_out", s))
        def conv_up(j):
            ov = wb_up[j].rearrange("p (kc n) -> p kc n", kc=16)
            conv_dma(ov[:, :, 0:128], w_up_v[:, :, 128 * j:128 * j + 128], ("wb_up", j, 0))
            conv_dma(ov[:, :, 128:256], w_up_v[:, :, DFF + 128 * j:DFF + 128 * j + 128], ("wb_up", j, 1))

        def conv_dn(oc, part):
            k0, nk = (0, 22) if part == 0 else (22, 21)
            ov = wb_dn[2 * oc + part].rearrange("p (kc n) -> p kc n", kc=22)
            conv_dma(ov[:, 0:nk, :], w_dn_v[:, k0:k0 + nk, 128 * oc:128 * oc + 128], ("wb_dn", 2 * oc + part))

        for part in range(2):
            for j in (range(0, 22) if part == 0 else range(22, NFC)):
                conv_up(j)
            for oc in range(16):
                conv_dn(oc, part)

        slab_list = []
        for u in range(NU):
            for s in range(16):
                slab_list.append((wb_in[s], 4096, [("wb_in", s)]))
            for s in range(8):
                slab_list.append((wb_out[s], 4096, [("wb_out", s)]))
            for part in range(2):
                for j in (range(0, 22) if part == 0 else range(22, NFC)):
                    slab_list.append((wb_up[j], 4096, [("wb_up", j, 0), ("wb_up", j, 1)]))
                nk = 22 if part == 0 else 21
                for oc in range(DC):
                    slab_list.append((wb_dn[2 * oc + part][:, 0:nk * 128], nk * 128, [("wb_dn", 2 * oc + part)]))
        slab_issued = [0]
        slab_cur = [0]

        def prefetch_slabs():
            i = slab_cur[0]
            while slab_issued[0] < min(len(slab_list), i + NSLAB):
                n = slab_issued[0]
                src, ncol, res = slab_list[n]
                P.op("sp", DMA(slab[n % NSLAB][:, 0:ncol], src), reads=res, writes=[("slab", n % NSLAB)],
                     dsem=("slab", n % NSLAB))
                slab_issued[0] += 1

        def next_slab(ahead=NSLAB):
            i = slab_cur[0]
            slab_cur[0] += 1
            while slab_issued[0] < min(len(slab_list), i + ahead):
                n = slab_issued[0]
                src, ncol, res = slab_list[n]
                P.op("sp", DMA(slab[n % NSLAB][:, 0:ncol], src), reads=res, writes=[("slab", n % NSLAB)],
                     dsem=("slab", n % NSLAB))
                slab_issued[0] += 1
            return i % NSLAB, ("slab", i % NSLAB)

        def ln_begin(pieces):
            banks = [(nb(), nb()) for _ in pieces]
            for b1, b2 in banks:
                reserved.add(b1)
                reserved.add(b2)
            return {"pieces": pieces, "banks": banks}

        def ln_chunk(stt, c):
            r = c % 3
            P.op("act", ACP(lxb[r][:, 0:NM], h0f[:, c, :]), reads=[("h0f", c)], writes=[("lxb", r)])
            P.op("act", ACTF(lxq[r][:, 0:NM], h0f[:, c, :], AF.Square), reads=[("h0f", c)], writes=[("lxq", r)])
            for pi, (c0, n) in enumerate(stt["pieces"]):
                b1, b2 = stt["banks"][pi]
                P.op("pe", MM(ps[b1][:, 0:n], onesb, lxb[r][:, c0:c0 + n], c == 0, c == DC - 1),
                     reads=[("lxb", r), "ones"], writes=[("ps", b1)])
                P.op("pe", MM(ps[b2][:, 0:n], onesb, lxq[r][:, c0:c0 + n], c == 0, c == DC - 1),
                     reads=[("lxq", r), "ones"], writes=[("ps", b2)])

        def ln_finish(stt, gcol, bcol, out_bf):
            pieces, banks = stt["pieces"], stt["banks"]
            for pi, (c0, n) in enumerate(pieces):
                b1, b2 = banks[pi]
                sl = slice(c0, c0 + n)
                P.op("dve", TS(lmean[:, sl], ps[b1][:, 0:n], 1.0 / D, None, ALU.mult), reads=[("ps", b1)],
                     writes=[("lstat", "mean", pi)])
                P.op("dve", TT(lmsq[:, sl], lmean[:, sl], lmean[:, sl], ALU.mult), reads=[("lstat", "mean", pi)],
                     writes=[("lstat", "msq", pi)])
                P.op("dve", STT(lrstd[:, sl], ps[b2][:, 0:n], 1.0 / D, lmsq[:, sl], ALU.mult, ALU.subtract),
                     reads=[("ps", b2), ("lstat", "msq", pi)], writes=[("lstat", "rstd", pi)])
                P.op("act", ACTF(lrstd[:, sl], lrstd[:, sl], AF.Sqrt, bias=LN_EPS),
                     reads=[("lstat", "rstd", pi)], writes=[("lstat", "rstd", pi)])
                P.op("dve", lambda e, sl=sl: e.reciprocal(out=lrstd[:, sl], in_=lrstd[:, sl]),
                     reads=[("lstat", "rstd", pi)], writes=[("lstat", "rstd", pi)])
                reserved.discard(b1)
                reserved.discard(b2)
            lo = pieces[0][0]
            hi = pieces[-1][0] + pieces[-1][1]
            sl = slice(lo, hi)
            srd = [("lstat", "mean", pi) for pi in range(len(pieces))] + [("lstat", "rstd", pi) for pi in range(len(pieces))]
            for c in range(DC):
                eng_ = "dve"
                r = c % 4
                P.op(eng_, TT(ltmp[r][:, sl], h0f[:, c, sl], lmean[:, sl], ALU.subtract),
                     reads=[("h0f", c)] + srd, writes=[("ltmp", r)])
                P.op(eng_, TT(ltmp[r][:, sl], ltmp[r][:, sl], lrstd[:, sl], ALU.mult),
                     reads=[("ltmp", r)] + srd, writes=[("ltmp", r)])
                P.op("act", ACTF(h0f[:, c, sl], ltmp[r][:, sl], AF.Identity, bias=cc(bcol + c), scale=cc(gcol + c)),
                     reads=[("ltmp", r), "cst"], writes=[("h0f", c)])
                if out_bf:
                    P.op("act", ACTF(h1bf[:, c, sl], ltmp[r][:, sl], AF.Identity, bias=cc(bcol + c), scale=cc(gcol + c)),
                         reads=[("ltmp", r), "cst"], writes=[("h1bf", c)])

        items_h = halo_items()
        plans = [pair_plan(j) for j in range(9)]
        out_ops = []

        class _Stop(Exception):
            pass

        def check_stop(name):
            if stop_after != name:
                return
            lasts = [P.ops[e][-1] for e in ("pe", "act", "dve") if P.ops[e]]
            for nm, t in (("A", RA), ("H", RH), ("B", RB), ("Y", RY)):
                o = P.op("sp", DMA(dbg_d[nm][:, :], t[:, :]), dsem=("dbg", nm), extra_deps=lasts)
                out_ops.append(o)
            raise _Stop()

        CONT = (1, 3, 4, 5)
        for u in range(NU):
          try:
            check_stop("setup")
            fb = 0
            fcol = lambda col: flg[:, fb + col: fb + col + 1]
            P.op("sp", DMA(flg[:, fb:fb + NFLAG], flg_d[u]), writes=[("flg", 0)], dsem=("flg", 0))
            FL = ("flg", 0)

            P.cur_tag = "P1a"
            def stage_a(t):
                    p = 128 if t < 9 else 16
                    xb_ = t % 3
                    stt_t, mv, rstd_s, nmr_s = stt_l[xb_], mv_l[xb_], rstd_l[xb_], nmr_l[xb_]
                    tok0 = t * 128
                    P.op("sp", DMA(xt[xb_][0:p, :], xu[u, tok0:tok0 + p, :]), writes=[("xt", xb_)], dsem=("x", xb_))
                    for q in range(4):
                        P.op("dve", lambda e, q=q, p=p, xb_=xb_, stt_t=stt_t: e.bn_stats(out=stt_t[0:p, q, :], in_=xt[xb_][0:p, 512 * q:512 * q + 512]),
                             reads=[("xt", xb_)], writes=[("bnst", xb_, q)])
                    P.op("dve", lambda e, p=p, stt_t=stt_t, mv=mv: e.bn_aggr(out=mv[0:p, :], in_=stt_t[0:p].rearrange("p a b -> p (a b)")),
                         reads=[("bnst", xb_, q) for q in range(4)], writes=[("mv", xb_)])
                    P.op("act", ACTF(rstd_s[0:p, :], mv[0:p, 1:2], AF.Sqrt, bias=LN_EPS), reads=[("mv", xb_)], writes=[("rstd_s", xb_)])
                    P.op("dve", lambda e, p=p, rstd_s=rstd_s: e.reciprocal(out=rstd_s[0:p, :], in_=rstd_s[0:p, :]),
                         reads=[("rstd_s", xb_)], writes=[("rstd_s", xb_)])
                    P.op("dve", TS(nmr_s[0:p, :], mv[0:p, 0:1], rstd_s[0:p, 0:1], -1.0, ALU.mult, ALU.mult),
                         reads=[("mv", xb_), ("rstd_s", xb_)], writes=[("nmr_s", xb_)])
                    P.op("act", ACTF(xt[xb_][0:p, :], xt[xb_][0:p, :], AF.Identity, bias=nmr_s[0:p, 0:1], scale=rstd_s[0:p, 0:1]),
                         reads=[("xt", xb_), ("rstd_s", xb_), ("nmr_s", xb_)], writes=[("xt", xb_)])

            def stage_b(t):
                    p = 128 if t < 9 else 16
                    xb_ = t % 3
                    tok0 = t * 128
                    ja = max(0, tok0 - TAU_M)
                    jb = min(NM, tok0 + p - TAU_M)
                    for g4 in range(4):
                        b = nb()
                        eng_ = "act" if (g4 + t) % 2 == 0 else "dve"
                        for k in range(4):
                            c = 4 * g4 + k
                            P.op("pe", TR(ps[b][:, k * 128:k * 128 + p], xt[xb_][0:p, c * 128:(c + 1) * 128], ident[0:p, 0:p]),
                                 reads=[("xt", xb_), "ident"], writes=[("ps", b)])
                        for k in range(4):
                            c = 4 * g4 + k
                            src = ps[b][:, k * 128:k * 128 + p]
                            if eng_ == "act":
                                P.op("act", ACTF(h0bf[:, c, tok0:tok0 + p], src, AF.Identity,
                                                 bias=cc(C_LNIN_B + c), scale=cc(C_LNIN_G + c)),
                                     reads=[("ps", b), "cst"], writes=[("h0bf", c, t)])
                            else:
                                P.op("dve", TS(h0bf[:, c, tok0:tok0 + p], src, cc(C_LNIN_G + c), cc(C_LNIN_B + c), ALU.mult, ALU.add),
                                     reads=[("ps", b), "cst"], writes=[("h0bf", c, t)])
                            if jb > ja:
                                o0 = TAU_M + ja - tok0
                                src2 = ps[b][:, k * 128 + o0:k * 128 + o0 + (jb - ja)]
                                if eng_ == "act":
                                    P.op("act", ACTF(h0f[:, c, ja:jb], src2, AF.Identity, bias=cc(C_LNIN_B + c), scale=cc(C_LNIN_G + c)),
                                         reads=[("ps", b), "cst"], writes=[("h0f", c)])
                                else:
                                    P.op("dve", TS(h0f[:, c, ja:jb], src2, cc(C_LNIN_G + c), cc(C_LNIN_B + c), ALU.mult, ALU.add),
                                         reads=[("ps", b), "cst"], writes=[("h0f", c)])

            cont = u in CONT
            tiles = list(range(2, 9)) if cont else list(range(NTILE))
            for i_ in range(len(tiles) + 1):
                if i_ < len(tiles):
                    stage_a(tiles[i_])
                if i_ >= 1:
                    stage_b(tiles[i_ - 1])
            P.fence("Y")
            h0all = [("h0bf", c, t) for c in range(DC) for t in tiles]
            check_stop("P1a")

            P.cur_tag = "P1b"
            ev = [0]

            def evac(out, in_, reads, writes):
                ev[0] += 1
                if ev[0] % 2:
                    P.op("act", ACP(out, in_), reads=reads, writes=writes)
                else:
                    P.op("dve", CP(out, in_), reads=reads, writes=writes)

            if cont:
                for tt in range(5):
                    P.op("pool", CP(KT[:, :, 128 * tt:128 * tt + 128], KT[:, :, 128 * (tt + 4):128 * (tt + 4) + 128]),
                         reads=[("KT", c) for c in range(8)], writes=[("KT", c) for c in range(8)])
                    P.op("pool", CP(Vt[:, tt, :], Vt[:, tt + 4, :]), reads=[("V", tt + 4, q) for q in range(4)],
                         writes=[("V", tt, q) for q in range(4)])
            for s in range(16):
                sl_, sres = next_slab()
                sv = slab[sl_].rearrange("p (kc n) -> p kc n", kc=16)
                kind = s // 4
                for e_ in range(2):
                    oc = 2 * (s % 4) + e_
                    if kind == 0:
                        pcs = [(TAU_U, 0, 265), (TAU_U + 265, 265, 265)]
                    elif kind == 1:
                        pcs = [(TAU_M, 0, 257), (TAU_M + 257, 257, 257)]
                    elif kind == 2:
                        pcs = [(640, 640, 512)] if cont else [(0, 0, 512), (512, 512, 512), (1024, 1024, 144)]
                    else:
                        pcs = []
                    for (tau0, o0, n) in pcs:
                        b = nb()
                        for kc in range(DC):
                            P.op("pe", MM(ps[b][:, 0:n], sv[:, kc, e_ * 128:(e_ + 1) * 128], h0bf[:, kc, tau0:tau0 + n],
                                          kc == 0, kc == DC - 1),
                                 reads=[sres] + (h0all if kc == 0 else []), writes=[("ps", b)])
                        if kind == 0:
                            evac(Ut[:, oc, o0:o0 + n], ps[b][:, 0:n], [("ps", b)], [("U", oc)])
                        elif kind == 1:
                            evac(QT[:, oc, o0:o0 + n], ps[b][:, 0:n], [("ps", b)], [("QT", oc)])
                        else:
                            evac(KT[:, oc, o0:o0 + n], ps[b][:, 0:n], [("ps", b)], [("KT", oc)])
                if kind == 3:
                    f0 = 256 * (s % 4)
                    for t2 in ((5, 7) if cont else range(0, NTILE, 2)):
                        b = nb()
                        for tt in (t2, t2 + 1):
                            p = 128 if tt < 9 else 16
                            off = 256 * (tt - t2)
                            for kc in range(DC):
                                P.op("pe", MM(ps[b][0:p, off:off + 256], h0bf[:, kc, tt * 128:tt * 128 + p], sv[:, kc, :],
                                              kc == 0, kc == DC - 1),
                                     reads=[sres] + (h0all if kc == 0 else []), writes=[("ps", b)])
                        evac(Vt[:, t2, f0:f0 + 256], ps[b][:, 0:256], [("ps", b)], [("V", t2, s % 4)])
                        p = 128 if t2 + 1 < 9 else 16
                        evac(Vt[0:p, t2 + 1, f0:f0 + 256], ps[b][0:p, 256:512], [("ps", b)], [("V", t2 + 1, s % 4)])
            check_stop("P1b")
            P.fence("A")
            Vall = [("V", t, q) for t in range(NTILE) for q in range(4)]

            P.cur_tag = "P2a"
            P.op("dve", TS(Ut[:, :, 521:530], Ut[:, :, 521:530], fcol(F_FPOST), None, ALU.mult),
                 reads=[("U", c) for c in range(8)] + [FL], writes=[("U", c) for c in range(8)])
            for g in range(4):
                w = 2 ** (g + 1)
                for kc2 in range(2):
                    c = 2 * g + kc2
                    U = Ut[:, c, :]
                    P.op("dve", TT(pt1[:, 1:530], U[:, 0:529], U[:, 1:530], ALU.add), reads=[("U", c)], writes=["pt1"])
                    cur, curname = pt1, "pt1"
                    if g >= 1:
                        P.op("dve", TT(pt2[:, 2:529], pt1[:, 1:528], pt1[:, 3:530], ALU.add), reads=["pt1"], writes=["pt2"])
                        cur, curname = pt2, "pt2"
                    if g >= 2:
                        P.op("dve", TT(pt1[:, 4:527], pt2[:, 2:525], pt2[:, 6:529], ALU.add), reads=["pt2"], writes=["pt1"])
                        cur, curname = pt1, "pt1"
                    if g >= 3:
                        P.op("dve", TT(pt2[:, 8:523], pt1[:, 4:519], pt1[:, 12:527], ALU.add), reads=["pt1"], writes=["pt2"])
                        cur, curname = pt2, "pt2"
                    P.op("dve", TT(cur[:, 513:521], cur[:, 513:521], flg[:, fb + F_CORR + 8 * g: fb + F_CORR + 8 * g + 8], ALU.mult),
                         reads=[curname, FL], writes=[curname])
                    P.op("dve", STT(mT[:, c, :], cur[:, 8:8 + NM], 1.0 / w, U[:, 8:8 + NM], ALU.mult, ALU.subtract),
                         reads=[curname, ("U", c)], writes=[("m", c)])
                for e_ in range(2):
                    oc = 2 * g + e_
                    for half in range(2):
                        b = nb()
                        for kc2 in range(2):
                            P.op("pe", MM(ps[b][:, 0:257], wp[:, g, kc2, e_ * 128:(e_ + 1) * 128],
                                          mT[:, 2 * g + kc2, 257 * half:257 * half + 257], kc2 == 0, kc2 == 1),
                                 reads=[("m", 2 * g + kc2), ("wp", g)], writes=[("ps", b)])
                        P.op("act", ACTF(ypool[:, oc, 257 * half:257 * half + 257], ps[b][:, 0:257], AF.Identity,
                                         scale=cc(C_PSCALE + oc)),
                             reads=[("ps", b), "cst"], writes=[("ypool", oc)])

            check_stop("P2a")
            prefetch_slabs()
            P.cur_tag = "P2b"
            first = True
            for h in range(16):
                hp, pi = h // 2, h % 2
                pr = slice(64 * pi, 64 * pi + 64)
                for it, (kind, j) in enumerate(items_h):
                    col = h * NHALO + it
                    qcol = NM - 1 if kind in ("post", "metapost") else 0
                    if j is None:
                        lhs = KT[pr, hp, 1152:1280]
                        out = ps[7][:, col:col + 1]
                    else:
                        lhs = KT[pr, hp, 128 * j:128 * j + 128]
                        out = ps[7][:, col:col + 1]
                    P.op("pe", MM(out, lhs, QT[pr, hp, qcol:qcol + 1], True, True),
                         reads=[("KT", hp), ("QT", hp)], writes=[("ps", 7)])
            P.op("dve", STT(sbh[:, :], ps[7][:, 0:240], 0.125, cst[:, C_GH:C_GH + 240], ALU.mult, ALU.add),
                 reads=[("ps", 7), "cst"], writes=["sbh"])
            P.op("dve", TT(sbh[:, :], sbh[:, :], flg[:, fb + F_FH: fb + F_FH + 240], ALU.add), reads=["sbh", FL], writes=["sbh"])
            P.op("act", ACTF(pth[:, :], sbh[:, :], AF.Exp), reads=["sbh"], writes=["pth"])

            def load_gp(hp_):
                P.op("sp", DMA(gpb[hp_ % 2].rearrange("p a b -> p (a b)"), gp_d[hp_]), writes=[("gpb", hp_ % 2)],
                     dsem=("gp", hp_ % 2))

            load_gp(0)
            steps = [None] + [j for j in range(9) if plans[j] is not None]
            nst = len(steps)
            sbank = [0, 1, 2]

            def emit_S(hp, i, pi):
                j = steps[i]
                n = 2 * (hp * nst + i) + pi
                h = 2 * hp + pi
                pr = slice(64 * pi, 64 * pi + 64)
                b = sbank[n % 3]
                if j is None:
                    P.op("pe", MM(ps[b][:, 0:512], KT[pr, hp, 1152:1280], QT[pr, hp, 1:513], True, True),
                         reads=[("KT", hp), ("QT", hp)], writes=[("ps", b)])
                    P.op("act", ACTF(ptb[n % 8][:, :], ps[b][:, 0:512], AF.Exp, bias=cc(C_METAB + h), scale=0.125),
                         reads=[("ps", b), "cst"], writes=[("ptb", n % 8)])
                else:
                    ra, rb, segs = plans[j]
                    qa, qb = 64 * ra, 64 * rb + 64
                    n_ = qb - qa
                    k0 = 2 * j - 6
                    gofs = (ra - k0 + 7) * 64
                    P.op("pe", MM(ps[b][:, 0:n_], KT[pr, hp, 128 * j:128 * j + 128], QT[pr, hp, 1 + qa:1 + qb], True, True),
                         reads=[("KT", hp), ("QT", hp)], writes=[("ps", b)])
                    P.op("dve", STT(sbt[n % 3][:, 0:n_], ps[b][:, 0:n_], 0.125, gpb[hp % 2][:, pi, gofs:gofs + n_],
                                    ALU.mult, ALU.add),
                         reads=[("ps", b), ("gpb", hp % 2)], writes=[("sbt", n % 3)])
                    for (r0, r1, combo) in segs:
                        a0, a1 = 64 * r0 - qa, 64 * r1 + 64 - qa
                        P.op("act", ACTF(ptb[n % 8][:, a0:a1], sbt[n % 3][:, a0:a1], AF.Exp, bias=fcol(F_BTAB + combo)),
                             reads=[("sbt", n % 3), FL], writes=[("ptb", n % 8)])

            def emit_PV(hp, i, pi):
                j = steps[i]
                n = 2 * (hp * nst + i) + pi
                h = 2 * hp + pi
                bV, bO = 3 + hp % 2, 5 + hp % 2
                pr = slice(64 * pi, 64 * pi + 64)
                last = i == nst - 1
                if j is None:
                    P.op("pe", MM(ps[bV][pr, 0:512], Vt[:, 9, 64 * h:64 * h + 64], ptb[n % 8][:, :], True, False),
                         reads=[("ptb", n % 8)] + Vall, writes=[("ps", bV)])
                    P.op("pe", MM(ps[bO][pr, 0:512], onesb[:, 0:64], ptb[n % 8][:, :], True, False),
                         reads=[("ptb", n % 8), "ones"], writes=[("ps", bO)])
                else:
                    ra, rb, segs = plans[j]
                    qa, qb = 64 * ra, 64 * rb + 64
                    n_ = qb - qa
                    P.op("pe", MM(ps[bV][pr, qa:qb], Vt[:, j, 64 * h:64 * h + 64], ptb[n % 8][:, 0:n_], False, last),
                         reads=[("ptb", n % 8)], writes=[("ps", bV)])
                    P.op("pe", MM(ps[bO][pr, qa:qb], onesb[:, 0:64], ptb[n % 8][:, 0:n_], False, last),
                         reads=[("ptb", n % 8), "ones"], writes=[("ps", bO)])

            def emit_norm(hp):
                bV, bO = 3 + hp % 2, 5 + hp % 2
                P.op("dve", lambda e, bO=bO: e.reciprocal(out=pt1[:, 0:512], in_=ps[bO][:, 0:512]), reads=[("ps", bO)],
                     writes=["pt1"])
                P.op("dve", TT(yattn[:, hp, 1:513], ps[bV][:, 0:512], pt1[:, 0:512], ALU.mult),
                     reads=[("ps", bV), "pt1"], writes=[("yattn", hp)])

            allsteps = [(hp, i) for hp in range(8) for i in range(nst)]
            LAG = 2
            for idx in range(len(allsteps) + LAG):
                if idx < len(allsteps):
                    hp, i = allsteps[idx]
                    if i == 0 and hp + 1 < 8:
                        load_gp(hp + 1)
                    emit_S(hp, i, 0)
                    emit_S(hp, i, 1)
                if idx >= LAG:
                    hp2, i2 = allsteps[idx - LAG]
                    emit_PV(hp2, i2, 0)
                    emit_PV(hp2, i2, 1)
                    if i2 == nst - 1:
                        emit_norm(hp2)
            for h in range(16):
                hp, pi = h // 2, h % 2
                pr = slice(64 * pi, 64 * pi + 64)
                for which in range(2):
                    its = [(it, kj) for it, kj in enumerate(items_h)
                           if (kj[0] in ("post", "metapost")) == (which == 1)]
                    cV = 256 + which * 8 + hp
                    cO = 288 + which * 8 + hp
                    for pass_ in range(2):
                        for ii, (it, (kind, j)) in enumerate(its):
                            col = h * NHALO + it
                            firstf, lastf = ii == 0, ii == len(its) - 1
                            tj = 9 if j is None else j
                            if pass_ == 0:
                                P.op("pe", MM(ps[7][pr, cV:cV + 1], Vt[:, tj, 64 * h:64 * h + 64], pth[:, col:col + 1], firstf, lastf),
                                     reads=["pth"] + (Vall if ii == 0 else []), writes=[("ps", 7)])
                            else:
                                P.op("pe", MM(ps[7][pr, cO:cO + 1], onesb[:, 0:64], pth[:, col:col + 1], firstf, lastf),
                                     reads=["pth", "ones"], writes=[("ps", 7)])
            P.op("dve", lambda e: e.reciprocal(out=rech[:, :], in_=ps[7][:, 288:304]), reads=[("ps", 7)], writes=["rech"])
            P.op("dve", TT(yattn[:, :, 0], ps[7][:, 256:264], rech[:, 0:8], ALU.mult), reads=[("ps", 7), "rech"],
                 writes=[("yattn", c) for c in range(8)])
            P.op("dve", TT(yattn[:, :, NM - 1], ps[7][:, 264:272], rech[:, 8:16], ALU.mult), reads=[("ps", 7), "rech"],
                 writes=[("yattn", c) for c in range(8)])
            check_stop("P2b")
            P.fence("A")
            P.fence("B")

            P.cur_tag = "P3"
            ln1 = ln_begin([(0, 257), (257, 257)])
            for s in range(8):
                sl_, sres = next_slab()
                sv = slab[sl_].rearrange("p (kc n) -> p kc n", kc=16)
                for e_ in range(2):
                    oc = 2 * s + e_
                    if oc >= 2:
                        ln_chunk(ln1, oc - 2)
                    for half in range(2):
                        b = nb()
                        hs = slice(257 * half, 257 * half + 257)
                        for kc in range(DC):
                            src = ypool[:, kc, hs] if kc < 8 else yattn[:, kc - 8, hs]
                            rd = ("ypool", kc) if kc < 8 else ("yattn", kc - 8)
                            P.op("pe", MM(ps[b][:, 0:257], sv[:, kc, e_ * 128:(e_ + 1) * 128], src, kc == 0, kc == DC - 1),
                                 reads=[sres, rd], writes=[("ps", b)])
                        P.op("dve", STT(h0f[:, oc, hs], h0f[:, oc, hs], ALPHA, ps[b][:, 0:257], ALU.mult, ALU.add),
                             reads=[("ps", b), ("h0f", oc)], writes=[("h0f", oc)])
            check_stop("P3a")
            P.cur_tag = "LN1"
            ln_chunk(ln1, DC - 2)
            ln_chunk(ln1, DC - 1)
            ln_finish(ln1, C_LN1_G, C_LN1_B, True)
            check_stop("P3")
            P.fence("Y")

            P.cur_tag = "P4up"
            for part in range(2):
                P.cur_tag = "P4up"
                j0 = 0 if part == 0 else 22
                jr = range(0, 22) if part == 0 else range(22, NFC)
                def up_post(j, banks4):
                    zi = j % 2
                    for ag in range(2):
                        for half in range(2):
                            b = banks4[2 * ag + half]
                            hs = slice(257 * half, 257 * half + 257)
                            P.op("act", ACTF(zb[zi][:, ag, hs], ps[b][:, 0:257], AF.Identity, bias=cc(C_BUP + ag * NFC + j)),
                                 reads=[("ps", b), "cst"], writes=[("z", zi)])
                    P.op("dve", TS(zb[zi][:, :, NM - 1:NM], zb[zi][:, :, NM - 1:NM], fcol(F_FPOST), None, ALU.mult),
                         reads=[("z", zi), FL], writes=[("z", zi)])
                    outs = [(cab, "ca"), (cgb, "cg")]
                    for ag in range(2):
                        ob, on = outs[ag]
                        cj = ag * NFC + j
                        z = zb[zi][:, ag, :]
                        P.op("act", ACTF(ob[zi][:, :], z[:, 1:513], AF.Identity, bias=cc(C_CB + cj), scale=cc(C_CW + 86 + cj)),
                             reads=[("z", zi), "cst"], writes=[(on, zi)])
                        P.op("dve", STT(ob[zi][:, :], z[:, 0:512], cc(C_CW + cj), ob[zi][:, :], ALU.mult, ALU.add),
                             reads=[("z", zi), (on, zi), "cst"], writes=[(on, zi)])
                        P.op("dve", STT(ob[zi][:, :], z[:, 2:514], cc(C_CW + 172 + cj), ob[zi][:, :], ALU.mult, ALU.add),
                             reads=[("z", zi), (on, zi), "cst"], writes=[(on, zi)])
                    P.op("act", ACTF(ggb[zi][:, :], cgb[zi][:, :], AF.Gelu), reads=[("cg", zi)], writes=[("gg", zi)])
                    P.op("dve", TT(actb[:, j - j0, :], cab[zi][:, :], ggb[zi][:, :], ALU.mult), reads=[("ca", zi), ("gg", zi)],
                         writes=[("act", j - j0)])

                jlist = list(jr)
                if part == 0:
                    grp = []
                    for j in jlist[:2]:
                        sl_, sres = next_slab(NSLAB if j == jlist[0] else NSLAB - 1)
                        sv = slab[sl_].rearrange("p (kc n) -> p kc n", kc=16)
                        grp.append((j, sres, sv, [nb() for _ in range(4)]))
                    for kc in range(DC):
                        for (j, sres, sv, banks4) in grp:
                            for ag in range(2):
                                for half in range(2):
                                    b = banks4[2 * ag + half]
                                    hs = slice(257 * half, 257 * half + 257)
                                    P.op("pe", MM(ps[b][:, 0:257], sv[:, kc, ag * 128:(ag + 1) * 128], h1bf[:, kc, hs], kc == 0, kc == DC - 1),
                                         reads=[sres, ("h1bf", kc)], writes=[("ps", b)])
                    for (j, sres, sv, banks4) in grp:
                        up_post(j, banks4)
                    jlist = jlist[2:]
                for j in jlist:
                    sl_, sres = next_slab()
                    sv = slab[sl_].rearrange("p (kc n) -> p kc n", kc=16)
                    banks4 = []
                    for ag in range(2):
                        for half in range(2):
                            b = nb()
                            banks4.append(b)
                            hs = slice(257 * half, 257 * half + 257)
                            for kc in range(DC):
                                P.op("pe", MM(ps[b][:, 0:257], sv[:, kc, ag * 128:(ag + 1) * 128], h1bf[:, kc, hs], kc == 0, kc == DC - 1),
                                     reads=[sres, ("h1bf", kc)], writes=[("ps", b)])
                    up_post(j, banks4)
                P.cur_tag = "P4dn"
                k0, nk = (0, 22) if part == 0 else (22, 21)
                if part == 1:
                    ln2 = ln_begin([(1, 512)])
                for oc in range(DC):
                    if part == 1 and oc >= 2:
                        ln_chunk(ln2, oc - 2)
                    b = nb()
                    sl_, sres = next_slab()
                    sv = slab[sl_][:, 0:nk * 128].rearrange("p (kc n) -> p kc n", kc=nk)
                    for kk in range(nk):
                        P.op("pe", MM(ps[b][:, 0:512], sv[:, kk, :], actb[:, kk, :], kk == 0, kk == nk - 1),
                             reads=[sres, ("act", kk)], writes=[("ps", b)])
                    P.op("dve", STT(h0f[:, oc, 1:513], h0f[:, oc, 1:513], ALPHA if part == 0 else 1.0, ps[b][:, 0:512], ALU.mult, ALU.add),
                         reads=[("ps", b), ("h0f", oc)], writes=[("h0f", oc)])
            check_stop("P4")
            P.fence("Y")
            P.fence("B")

            P.cur_tag = "P5"
            ln_chunk(ln2, DC - 2)
            ln_chunk(ln2, DC - 1)
            ln_finish(ln2, C_LN2_G, C_LN2_B, False)
            check_stop("P5a")
            for tt in range(4):
                ob_ = tt % 2
                for g4 in range(4):
                    b = nb()
                    for k in range(4):
                        c = 4 * g4 + k
                        P.op("pe", TR(ps[b][:, k * 128:(k + 1) * 128], h0f[:, c, 1 + 128 * tt:1 + 128 * tt + 128], ident[:, :]),
                             reads=[("h0f", c), "ident"], writes=[("ps", b)])
                    evac(otile[ob_][:, 512 * g4:512 * g4 + 512], ps[b][:, :], [("ps", b)], [("otile", ob_, g4)])
                o = P.op("pool", DMA(y_d[u, 128 * tt:128 * tt + 128, :], otile[ob_][:, :]),
                         reads=[("otile", ob_, g4) for g4 in range(4)], dsem=("out", ob_))
                out_ops.append(o)
            P.fence("Y")
            P.fence("A")
          except _Stop:
            break

        finals = {}
        for o in out_ops:
            finals[o.dsem] = o
        P.emit(final_wait_ops=list(finals.values()))
    return nc


def _unit_table():
    units = []
    for core in range(NCORE):
        ps_, run = core // 4, core % 4
        for k in range(2):
            units.append((ps_, 2 * run + k, 8))
        for k in range(4):
            units.append((2 + ps_, 4 * run + k, 16))
    return units


def _status_value(stt, start, end):
    if stt == ST_I:
        return 0.0
    if stt == ST_S:
        return 0.0 if start else -BIG
    if stt == ST_E:
        return 0.0 if end else -BIG
    if stt == ST_NS:
        return -BIG if start else 0.0
    if stt == ST_NE:
        return -BIG if end else 0.0
    return -BIG


def _build_gp(rpb):
    H = rpb.shape[0]
    gp = np.zeros((H, 2, 64, 16, 64), np.float32)
    c = np.arange(64)
    cs = np.clip(c - 8, 0, 48)
    key = np.arange(64)
    ok = (key[:, None] >= cs[None, :]) & (key[:, None] < cs[None, :] + 16)
    dc = np.clip(key[:, None] - c[None, :] + 15, 0, 30)
    for rp in range(2):
        for di in range(16):
            dr = rp - (di - 7) + 7
            if 0 <= dr <= 14:
                vals = rpb[:, dr, :][:, dc]
                gp[:, rp, :, di, :] = np.where(ok[None], vals, np.float32(-BIG))
            else:
                gp[:, rp, :, di, :] = np.where(ok, np.float32(0.0), np.float32(-BIG))[None]
    return gp.reshape(H, 128, 1024)


def _prepare(x_prompt, x_sample, meta_tokens, ln_in_g, ln_in_b, w_in, w_pool, pool_scale, rpb, meta_bias,
             w_out, ln1_g, ln1_b, w_up, b_up, conv_w, conv_b, w_down, ln2_g, ln2_b, cores=None):
    f32 = np.float32
    xs = [np.asarray(x_prompt[0], f32), np.asarray(x_prompt[1], f32), np.asarray(x_sample[0], f32),
          np.asarray(x_sample[1], f32)]
    meta = np.asarray(meta_tokens, f32)
    rpb0 = np.asarray(rpb, f32)[0]
    mb0 = np.asarray(meta_bias, f32)[0]
    units = _unit_table()
    assert len(units) == NCORE * NU

    gp = _build_gp(rpb0)
    gp_pairs = np.ascontiguousarray(gp.reshape(8, 2, 128, 1024).transpose(0, 2, 1, 3).reshape(8, 128, 2048))
    cst = np.zeros((128, NCONST), f32)
    col = lambda v: np.asarray(v, f32).reshape(-1, 128).T
    cst[:, C_LNIN_G:C_LNIN_G + 16] = col(ln_in_g)
    cst[:, C_LNIN_B:C_LNIN_B + 16] = col(ln_in_b)
    cst[:, C_LN1_G:C_LN1_G + 16] = col(ln1_g)
    cst[:, C_LN1_B:C_LN1_B + 16] = col(ln1_b)
    cst[:, C_LN2_G:C_LN2_G + 16] = col(ln2_g)
    cst[:, C_LN2_B:C_LN2_B + 16] = col(ln2_b)
    cst[:, C_PSCALE:C_PSCALE + 8] = col(pool_scale)
    cst[:, C_BUP:C_BUP + 86] = col(b_up)
    cw = np.asarray(conv_w, f32)[0]
    for k in range(3):
        cst[:, C_CW + 86 * k:C_CW + 86 * k + 86] = col(cw[k])
    cst[:, C_CB:C_CB + 86] = col(conv_b)
    cst[:, C_METAB:C_METAB + 16] = -BIG
    cst[0:16, C_METAB:C_METAB + 16] = mb0.T
    items = halo_items()
    for h in range(16):
        for it, (kind, j) in enumerate(items):
            c_ = C_GH + h * NHALO + it
            if j is None:
                cst[:, c_] = -BIG
                cst[0:16, c_] = mb0[h]
            else:
                dl, cq = halo_geom(kind, j)
                cst[:, c_] = gp[h][:, (dl + 7) * 64 + cq]

    def unit_flags(start, end):
        fl = np.zeros((128, NFLAG), f32)
        for stt in range(6):
            for sbb in range(6):
                fl[0:64, F_BTAB + stt * 6 + sbb] = _status_value(stt, start, end)
                fl[64:128, F_BTAB + stt * 6 + sbb] = _status_value(sbb, start, end)
        for h in range(16):
            for it, (kind, j) in enumerate(items):
                if j is None:
                    continue
                k0 = 2 * j - 6
                c_ = F_FH + h * NHALO + it
                fl[0:64, c_] = _status_value(halo_status(kind, k0), start, end)
                fl[64:128, c_] = _status_value(halo_status(kind, k0 + 1), start, end)
        fl[:, F_FPOST] = 0.0 if end else 1.0
        for g in range(4):
            w = 2 ** (g + 1)
            half = w // 2
            for i in range(8):
                tau_own = 504 + i
                cnt = w
                if end and tau_own + half > 512:
                    cnt = 512 - tau_own + half
                fl[:, F_CORR + 8 * g + i] = f32(w) / f32(cnt)
        return fl

    in_maps = []
    for core in (range(NCORE) if cores is None else cores):
        xu = np.zeros((NU, TOK, D), f32)
        flg = np.zeros((NU, 128, NFLAG), f32)
        for k in range(NU):
            s, uu, nun = units[core * NU + k]
            X = xs[s]
            nrows = nun * 8
            r0 = 8 * uu
            for row in range(-6, 12):
                gr = r0 + row
                if 0 <= gr < nrows:
                    xu[k, (row + 6) * 64:(row + 7) * 64] = X[gr * 64:(gr + 1) * 64]
            if uu == 0:
                xu[k, 384 - 16:384] = meta
            xu[k, 1152:1168] = meta
            flg[k] = unit_flags(uu == 0, uu == nun - 1)
        in_maps.append({
            "xu": xu, "w_in": np.ascontiguousarray(np.asarray(w_in, f32)[0]),
            "w_out": np.ascontiguousarray(np.asarray(w_out, f32)[0]),
            "w_up": np.ascontiguousarray(np.asarray(w_up, f32)[0]),
            "w_down": np.ascontiguousarray(np.asarray(w_down, f32)[0]),
            "w_pool": np.ascontiguousarray(np.asarray(w_pool, f32)[0]),
            "gp": gp_pairs, "cst": cst, "flg": flg,
        })

    return in_maps, units


def kernel(**inputs):
    in_maps, units = _prepare(**inputs)
    nc = build_program()
    res = run_bass_kernel_spmd(nc, in_maps, core_ids=list(range(NCORE)))
    return _assemble(res, units)


def _assemble(res, units):
    f32 = np.float32
    yp = np.zeros((2, 4096, D), f32)
    ysm = np.zeros((2, 8192, D), f32)
    outs = [yp[0], yp[1], ysm[0], ysm[1]]
    for core in range(NCORE):
        y = np.asarray(res.results[core]["y"], f32)
        for k in range(NU):
            s, uu, nun = units[core * NU + k]
            outs[s][512 * uu:512 * uu + 512] = y[k]
    return (yp, ysm)
```

```python
import contextlib
import os
import numpy as np
import concourse.bass as bass
import concourse.mybir as mybir
from concourse.bass_utils import run_bass_kernel_spmd

F32 = mybir.dt.float32
BF = mybir.dt.bfloat16
AF = mybir.ActivationFunctionType
ALU = mybir.AluOpType

D = 2048
DC = 16
NCORE = 8
NU = 6
TOK = 1168
NTILE = 10
NM = 514
NUC = 530
TAU_M = 383
TAU_U = 375
DFF = 5504
NFC = 43
BIG = 30000.0
ALPHA = float(2.0 ** 0.25)
LN_EPS = 1e-5
N_META = 16

C_LNIN_G, C_LNIN_B, C_LN1_G, C_LN1_B, C_LN2_G, C_LN2_B = 0, 16, 32, 48, 64, 80
C_PSCALE = 96
C_BUP = 104
C_CW = 190
C_CB = 448
C_METAB = 534
C_GH = 550
NCONST = 790
F_BTAB, F_FH, F_FPOST, F_CORR = 0, 36, 276, 277
NFLAG = 309
NHALO = 15
NSLAB = 5

ENGS = ("pe", "act", "dve", "pool", "sp")


class Op:
    __slots__ = ("eng", "fn", "deps", "is_dma", "dsem", "needs_inc", "semval")

    def __init__(self, eng, fn, dsem=None):
        self.eng = eng
        self.fn = fn
        self.deps = ()
        self.is_dma = dsem is not None
        self.dsem = dsem
        self.needs_inc = False
        self.semval = None


class Prog:
    def __init__(self, nc):
        self.nc = nc
        self.ops = {e: [] for e in ENGS}
        self.last_w = {}
        self.readers = {}
        self.reg_of = {}
        self.reg_cur = {}
        self.reg_fence = {}
        self.dma_cnt = {}
        self.excl = {}
        self.pe_tags = []
        self.cur_tag = "setup"
        self.trace = None

    def region(self, reg, names):
        for n in names:
            self.reg_of[n] = reg

    def fence(self, reg):
        cur = self.reg_cur.get(reg)
        if cur:
            self.reg_fence[reg] = dict(cur)
            self.reg_cur[reg] = {}

    @staticmethod
    def _key(o):
        return ("dma", id(o)) if o.is_dma else o.eng

    def op(self, eng, fn, reads=(), writes=(), dsem=None, extra_deps=()):
        o = Op(eng, fn, dsem)
        if eng == "pe":
            self.pe_tags.append(self.cur_tag)
        deps = {id(x): x for x in extra_deps}
        regs = set()
        px = [r for r in tuple(reads) + tuple(writes) if isinstance(r, tuple) and r[0] == "ps"]
        if px:
            reads = [r for r in reads if not (isinstance(r, tuple) and r[0] == "ps")]
            writes = [r for r in writes if not (isinstance(r, tuple) and r[0] == "ps")]
            for r in px:
                prev = self.excl.get(r)
                if prev is not None and prev.eng != eng:
                    deps[id(prev)] = prev
                self.excl[r] = o
        for r in reads:
            w = self.last_w.get(r)
            if w is not None:
                deps[id(w)] = w
            nm = r[0] if isinstance(r, tuple) else r
            rg = self.reg_of.get(nm)
            if rg is not None:
                regs.add(rg)
        for r in writes:
            w = self.last_w.get(r)
            if w is not None:
                deps[id(w)] = w
            rd = self.readers.get(r)
            if rd:
                for x in rd.values():
                    deps[id(x)] = x
            nm = r[0] if isinstance(r, tuple) else r
            rg = self.reg_of.get(nm)
            if rg is not None:
                regs.add(rg)
        for rg in regs:
            f = self.reg_fence.get(rg)
            if f:
                for x in f.values():
                    deps[id(x)] = x
            self.reg_cur.setdefault(rg, {})[self._key(o)] = o
        o.deps = tuple(deps.values())
        k = self._key(o)
        for r in reads:
            self.readers.setdefault(r, {})[k] = o
        for r in writes:
            self.last_w[r] = o
            self.readers[r] = {}
        self.ops[eng].append(o)
        return o

    def emit(self, final_wait_ops=()):
        nc = self.nc
        for o in final_wait_ops:
            if not o.is_dma:
                o.needs_inc = True
        for e in ENGS:
            for o in self.ops[e]:
                for d in o.deps:
                    if d.is_dma:
                        continue
                    if d.eng == o.eng and d.eng == "pe":
                        continue
                    d.needs_inc = True
        for e in ENGS:
            c = 0
            for o in self.ops[e]:
                if o.is_dma:
                    self.dma_cnt[o.dsem] = self.dma_cnt.get(o.dsem, 0) + 16
                    o.semval = self.dma_cnt[o.dsem]
                elif o.needs_inc:
                    c += 1
                    o.semval = c
        with contextlib.ExitStack() as st:
            esem = {e: st.enter_context(nc.semaphore("prog_" + e)) for e in ENGS}
            dsems = {}
            for i, k in enumerate(sorted(self.dma_cnt.keys(), key=str)):
                dsems[k] = st.enter_context(nc.semaphore("dma%d" % i))
            block = st.enter_context(nc.Block())

            def run(ename, eng):
                seen = {}
                for o in self.ops[ename]:
                    need = {}
                    for d in o.deps:
                        if d.is_dma:
                            s = ("d", d.dsem)
                        else:
                            if d.eng == ename and ename == "pe":
                                continue
                            s = ("e", d.eng)
                        if need.get(s, 0) < d.semval:
                            need[s] = d.semval
                    for s, v in need.items():
                        if seen.get(s, 0) >= v:
                            continue
                        seen[s] = v
                        eng.wait_ge(dsems[s[1]] if s[0] == "d" else esem[s[1]], v)
                        if self.trace is not None:
                            self.trace.append((ename, "wait", s, v))
                    ins = o.fn(eng)
                    if self.trace is not None:
                        self.trace.append((ename, "op", getattr(o, "tag", None), (o.dsem if o.is_dma else ("inc" if o.needs_inc else None)), o.semval))
                    if o.is_dma:
                        ins.then_inc(dsems[o.dsem], 16)
                    elif o.needs_inc:
                        ins.then_inc(esem[ename], 1)
                if ename == "sp":
                    for o in final_wait_ops:
                        eng.wait_ge(dsems[o.dsem] if o.is_dma else esem[o.eng], o.semval)

            @block.tensor
            def _(e):
                run("pe", e)

            @block.scalar
            def _(e):
                run("act", e)

            @block.vector
            def _(e):
                run("dve", e)

            @block.gpsimd
            def _(e):
                run("pool", e)

            @block.sync
            def _(e):
                run("sp", e)


def MM(out, lhsT, rhs, start, stop):
    return lambda e: e.matmul(out, lhsT=lhsT, rhs=rhs, start=start, stop=stop)


def TR(out, in_, ident):
    return lambda e: e.transpose(out=out, in_=in_, identity=ident)


def ACTF(out, in_, func, bias=None, scale=None):
    kw = {}
    if bias is not None:
        kw["bias"] = bias
    if scale is not None:
        kw["scale"] = scale
    return lambda e: e.activation(out=out, in_=in_, func=func, **kw)


def STT(out, in0, scalar, in1, op0, op1):
    return lambda e: e.scalar_tensor_tensor(out=out, in0=in0, scalar=scalar, in1=in1, op0=op0, op1=op1)


def TT(out, in0, in1, op):
    return lambda e: e.tensor_tensor(out=out, in0=in0, in1=in1, op=op)


def TS(out, in0, s1, s2, op0, op1=None):
    if op1 is None:
        return lambda e: e.tensor_scalar(out=out, in0=in0, scalar1=s1, scalar2=None, op0=op0)
    return lambda e: e.tensor_scalar(out=out, in0=in0, scalar1=s1, scalar2=s2, op0=op0, op1=op1)


def CP(out, in_):
    return lambda e: e.tensor_copy(out=out, in_=in_)


def ACP(out, in_):
    return lambda e: e.copy(out=out, in_=in_)


def DMA(out, in_):
    return lambda e: e.dma_start(out=out, in_=in_)


ST_I, ST_S, ST_E, ST_NS, ST_NE, ST_NV = 0, 1, 2, 3, 4, 5


def key_status_own(kap, rho):
    d = kap - rho
    if kap == -6:
        return ST_NV
    if 0 <= kap <= 7:
        if -4 <= d <= 3:
            return ST_I
        return ST_S if d >= 4 else ST_E
    if kap < 0:
        return ST_NS if d >= -4 else ST_NV
    return ST_NE if d <= 3 else ST_NV


def pair_plan(j):
    k0 = 2 * j - 6
    per = []
    for rho in range(8):
        st, sb = key_status_own(k0, rho), key_status_own(k0 + 1, rho)
        per.append(None if (st == ST_NV and sb == ST_NV) else st * 6 + sb)
    rows = [r for r in range(8) if per[r] is not None]
    if not rows:
        return None
    ra, rb = rows[0], rows[-1]
    segs = []
    r = ra
    while r <= rb:
        assert per[r] is not None
        r2 = r
        while r2 + 1 <= rb and per[r2 + 1] == per[r]:
            r2 += 1
        segs.append((r, r2, per[r]))
        r = r2 + 1
    return ra, rb, segs


def halo_items():
    items = []
    for j in range(0, 5):
        items.append(("preI", j))
    for j in range(3, 7):
        items.append(("preS", j))
    for j in range(5, 9):
        items.append(("post", j))
    items.append(("metapre", None))
    items.append(("metapost", None))
    assert len(items) == NHALO
    return items


def halo_status(kind, kap):
    if kind == "preI":
        return ST_NS if -5 <= kap <= 2 else ST_NV
    if kind == "preS":
        return ST_S if 0 <= kap <= 7 else ST_NV
    if kind == "post":
        if 4 <= kap <= 7:
            return ST_I
        return ST_NE if 8 <= kap <= 11 else ST_NV
    raise ValueError


def halo_geom(kind, j):
    k0 = 2 * j - 6
    if kind == "preI":
        return -1 - k0, 63
    if kind == "preS":
        return 0 - k0, 0
    return 8 - k0, 0


def build_program(NU=NU, stop_after=None):
    nc = bass.Bass("TRN2", target_bir_lowering=False)
    dt_in = lambda n, s: nc.dram_tensor(n, s, F32, kind="ExternalInput").ap()
    xu = dt_in("xu", [NU, TOK, D])
    w_in = dt_in("w_in", [D, 4096])
    w_out = dt_in("w_out", [D, D])
    w_up = dt_in("w_up", [D, 2 * DFF])
    w_down = dt_in("w_down", [DFF, D])
    w_pool = dt_in("w_pool", [4, 256, 256])
    gp_d = dt_in("gp", [8, 128, 2048])
    cst_d = dt_in("cst", [128, NCONST])
    flg_d = dt_in("flg", [NU, 128, NFLAG])
    y_d = nc.dram_tensor("y", [NU, 512, D], F32, kind="ExternalOutput").ap()
    dbg_d = None
    if stop_after is not None:
        dbg_d = {"A": nc.dram_tensor("dbgA", [128, 9344], F32, kind="ExternalOutput").ap(),
                 "H": nc.dram_tensor("dbgH", [128, DC * NM], F32, kind="ExternalOutput").ap(),
                 "B": nc.dram_tensor("dbgB", [128, 16536], F32, kind="ExternalOutput").ap(),
                 "Y": nc.dram_tensor("dbgY", [128, 6168], F32, kind="ExternalOutput").ap()}
    wb_in = nc.dram_tensor("wb_in", [16, 128, 4096], BF, kind="Internal").ap()
    wb_out = nc.dram_tensor("wb_out", [8, 128, 4096], BF, kind="Internal").ap()
    wb_up = nc.dram_tensor("wb_up", [NFC, 128, 4096], BF, kind="Internal").ap()
    wb_dn = nc.dram_tensor("wb_dn", [32, 128, 22 * 128], BF, kind="Internal").ap()

    with contextlib.ExitStack() as st:
        def sb(n, words):
            return st.enter_context(nc.sbuf_tensor(n, [128, words], F32))

        RA = sb("RA", 9344)
        RH = sb("RH", DC * NM)
        RB = sb("RB", 16536)
        RY = sb("RY", 6168)
        RS = sb("RS", NSLAB * 2048)
        cst = sb("cst_sb", NCONST)
        flg = sb("flg_sb", NFLAG)
        ident = sb("ident", 128)
        onesw = sb("onesw", 64)
        wpw = sb("wpw", 1024)
        small = sb("small", 96)
        ps = [st.enter_context(nc.psum_tensor("ps%d" % i, [128, 512], F32)) for i in range(8)]

        def vbf(t, off_w, shape):
            n = int(np.prod(shape))
            a = t.bitcast(BF)[:, 2 * off_w: 2 * off_w + n]
            if len(shape) == 1:
                return a
            names = "abcd"[:len(shape)]
            pat = "p (%s) -> p %s" % (" ".join(names), " ".join(names))
            return a.rearrange(pat, **{names[i]: shape[i] for i in range(len(shape) - 1)})

        def vf(t, off_w, shape):
            n = int(np.prod(shape))
            a = t[:, off_w: off_w + n]
            if len(shape) == 1:
                return a
            names = "abcd"[:len(shape)]
            pat = "p (%s) -> p %s" % (" ".join(names), " ".join(names))
            return a.rearrange(pat, **{names[i]: shape[i] for i in range(len(shape) - 1)})

        h0bf = vbf(RA, 0, [DC, TOK])
        gpb = [vf(RA, 0, [2, 1024]), vf(RA, 2048, [2, 1024])]
        sbt = [vf(RA, 4096, [512]), vf(RA, 4608, [512]), vf(RA, 5120, [512])]
        ptb = [vbf(RA, 5632, [512]), vbf(RA, 5888, [512]), vbf(RA, 6144, [512]), vbf(RA, 6400, [512]),
               vbf(RA, 8100, [512]), vbf(RA, 8356, [512]), vbf(RA, 8612, [512]), vbf(RA, 8868, [512])]
        sbh = vf(RA, 6656, [240])
        pth = vbf(RA, 6896, [240])
        rech = vf(RA, 7016, [16])
        pt1 = vf(RA, 7040, [NUC])
        pt2 = vf(RA, 7040 + NUC, [NUC])
        h1bf = vbf(RA, 0, [DC, NM])
        lxb = [vbf(RA, 4112 + 257 * i, [NM]) for i in range(3)]
        lxq = [vbf(RA, 4112 + 771 + 257 * i, [NM]) for i in range(3)]
        lmean = vf(RA, 5654, [NM])
        lmsq = vf(RA, 5654 + NM, [NM])
        lrstd = vf(RA, 5654 + 2 * NM, [NM])
        ltmp = [vf(RA, 7196 + NM * i, [NM]) for i in range(4)]
        h0f = vf(RH, 0, [DC, NM])
        KT = vbf(RB, 0, [8, 1280])
        Vt = vbf(RB, 5120, [NTILE, 1024])
        QT = vbf(RB, 10240, [8, NM])
        Ut = vf(RB, 12296, [8, NUC])
        actb = vbf(RB, 10240, [22, 512])
        xt = [vf(RY, 0, [D]), vf(RY, 2048, [D]), vf(RY, 4096, [D])]
        mT = vbf(RY, 0, [8, NM])
        ypool = vbf(RY, 2056, [8, NM])
        yattn = vbf(RY, 4112, [8, NM])
        zb = [vf(RY, 0, [2, NM]), vf(RY, 1028, [2, NM])]
        cab = [vf(RY, 2056, [512]), vf(RY, 2568, [512])]
        cgb = [vf(RY, 3080, [512]), vf(RY, 3592, [512])]
        ggb = [vf(RY, 4104, [512]), vf(RY, 4616, [512])]
        otile = [vf(RY, 0, [D]), vf(RY, 2048, [D])]
        slab = [vbf(RS, 2048 * i, [4096]) for i in range(NSLAB)]
        onesb = vbf(onesw, 0, [128])
        wp = vbf(wpw, 0, [4, 2, 256])
        stt_l = [vf(small, 32 * i, [4, 6]) for i in range(3)]
        mv_l = [vf(small, 32 * i + 24, [2]) for i in range(3)]
        rstd_l = [vf(small, 32 * i + 26, [1]) for i in range(3)]
        nmr_l = [vf(small, 32 * i + 27, [1]) for i in range(3)]

        P = Prog(nc)
        nc._prog = P
        if "trace" in os.environ.get("KDBG", ""):
            P.trace = []
            nc._ptrace = P.trace
        P.region("A", ["h0bf", "gpb", "sbt", "ptb", "sbh", "pth", "rech", "pt1", "pt2", "h1bf", "lxb", "lxq",
                       "lstat", "ltmp"])
        P.region("B", ["KT", "V", "QT", "U", "act"])
        P.region("Y", ["xt", "m", "ypool", "yattn", "z", "ca", "cg", "gg", "otile"])

        bank_ctr = [0]

        reserved = set()

        def nb():
            while True:
                b = bank_ctr[0] % 8
                bank_ctr[0] += 1
                if b not in reserved:
                    return b

        cc = lambda col: cst[:, col:col + 1]

        P.op("sp", DMA(cst[:], cst_d[:]), writes=["cst"], dsem="cst")
        P.op("dve", lambda e: e.memset(ident[:], 0.0), writes=["ident"])
        P.op("pool", lambda e: e.affine_select(out=ident[:], in_=ident[:], pattern=[[-1, 128]],
                                               compare_op=ALU.not_equal, fill=1.0, base=0, channel_multiplier=1),
             reads=["ident"], writes=["ident"])
        P.op("dve", lambda e: e.memset(onesb, 1.0), writes=["ones"])
        P.op("dve", lambda e: e.memset(RB[:, :], 0.0), writes=[("KT", c) for c in range(8)] + [("V", 9, q) for q in range(4)])

        cv_n = [0]


        dbgflags = os.environ.get("KDBG", "")

        def conv_dma(out, in_, res):
            if "noconv" in dbgflags:
                return
            if "only_" in dbgflags and ("only_" + res[0]) not in dbgflags:
                return
            n = cv_n[0]
            cv_n[0] += 1
            P.op("pool", DMA(out, in_), writes=[res, ("cvslot", n % 5)], dsem=("cv", n % 5))

        w_in_v = w_in.rearrange("(kc p) n -> p kc n", p=128)
        w_out_v = w_out.rearrange("(kc p) n -> p kc n", p=128)
        w_up_v = w_up.rearrange("(kc p) n -> p kc n", p=128)
        w_dn_v = w_down.rearrange("(kc p) n -> p kc n", p=128)
        for s in range(16):
            conv_dma(wb_in[s].rearrange("p (kc n) -> p kc n", kc=16), w_in_v[:, :, 256 * s:256 * s + 256], ("wb_in", s))
        for g in range(4):
            if "nowp" in dbgflags:
                continue
            P.op("pool", DMA(wp[:, g], w_pool[g].rearrange("(kc p) e -> p kc e", p=128)), writes=[("wp", g)],
                 dsem=("wpl", g))
        for s in range(8):
            conv_dma(wb_out[s].rearrange("p (kc n) -> p kc n", kc=16), w_out_v[:, :, 256 * s:256 * s + 256], ("wb_out", s))
        def conv_up(j):
            ov = wb_up[j].rearrange("p (kc n) -> p kc n", kc=16)
            conv_dma(ov[:, :, 0:128], w_up_v[:, :, 128 * j:128 * j + 128], ("wb_up", j, 0))
            conv_dma(ov[:, :, 128:256], w_up_v[:, :, DFF + 128 * j:DFF + 128 * j + 128], ("wb_up", j, 1))

        def conv_dn(oc, part):
            k0, nk = (0, 22) if part == 0 else (22, 21)
            ov = wb_dn[2 * oc + part].rearrange("p (kc n) -> p kc n", kc=22)
            conv_dma(ov[:, 0:nk, :], w_dn_v[:, k0:k0 + nk, 128 * oc:128 * oc + 128], ("wb_dn", 2 * oc + part))

        for part in range(2):
            for j in (range(0, 22) if part == 0 else range(22, NFC)):
                conv_up(j)
            for oc in range(16):
                conv_dn(oc, part)

        slab_list = []
        for u in range(NU):
            for s in range(16):
                slab_list.append((wb_in[s], 4096, [("wb_in", s)]))
            for s in range(8):
                slab_list.append((wb_out[s], 4096, [("wb_out", s)]))
            for part in range(2):
                for j in (range(0, 22) if part == 0 else range(22, NFC)):
                    slab_list.append((wb_up[j], 4096, [("wb_up", j, 0), ("wb_up", j, 1)]))
                nk = 22 if part == 0 else 21
                for oc in range(DC):
                    slab_list.append((wb_dn[2 * oc + part][:, 0:nk * 128], nk * 128, [("wb_dn", 2 * oc + part)]))
        slab_issued = [0]
        slab_cur = [0]

        def prefetch_slabs():
            i = slab_cur[0]
            while slab_issued[0] < min(len(slab_list), i + NSLAB):
                n = slab_issued[0]
                src, ncol, res = slab_list[n]
                P.op("sp", DMA(slab[n % NSLAB][:, 0:ncol], src), reads=res, writes=[("slab", n % NSLAB)],
                     dsem=("slab", n % NSLAB))
                slab_issued[0] += 1

        def next_slab(ahead=NSLAB):
            i = slab_cur[0]
            slab_cur[0] += 1
            while slab_issued[0] < min(len(slab_list), i + ahead):
                n = slab_issued[0]
                src, ncol, res = slab_list[n]
                P.op("sp", DMA(slab[n % NSLAB][:, 0:ncol], src), reads=res, writes=[("slab", n % NSLAB)],
                     dsem=("slab", n % NSLAB))
                slab_issued[0] += 1
            return i % NSLAB, ("slab", i % NSLAB)

        def ln_begin(pieces):
            banks = [(nb(), nb()) for _ in pieces]
            for b1, b2 in banks:
                reserved.add(b1)
                reserved.add(b2)
            return {"pieces": pieces, "banks": banks}

        def ln_chunk(stt, c):
            r = c % 3
            P.op("act", ACP(lxb[r][:, 0:NM], h0f[:, c, :]), reads=[("h0f", c)], writes=[("lxb", r)])
            P.op("act", ACTF(lxq[r][:, 0:NM], h0f[:, c, :], AF.Square), reads=[("h0f", c)], writes=[("lxq", r)])
            for pi, (c0, n) in enumerate(stt["pieces"]):
                b1, b2 = stt["banks"][pi]
                P.op("pe", MM(ps[b1][:, 0:n], onesb, lxb[r][:, c0:c0 + n], c == 0, c == DC - 1),
                     reads=[("lxb", r), "ones"], writes=[("ps", b1)])
                P.op("pe", MM(ps[b2][:, 0:n], onesb, lxq[r][:, c0:c0 + n], c == 0, c == DC - 1),
                     reads=[("lxq", r), "ones"], writes=[("ps", b2)])

        def ln_finish(stt, gcol, bcol, out_bf):
            pieces, banks = stt["pieces"], stt["banks"]
            for pi, (c0, n) in enumerate(pieces):
                b1, b2 = banks[pi]
                sl = slice(c0, c0 + n)
                P.op("dve", TS(lmean[:, sl], ps[b1][:, 0:n], 1.0 / D, None, ALU.mult), reads=[("ps", b1)],
                     writes=[("lstat", "mean", pi)])
                P.op("dve", TT(lmsq[:, sl], lmean[:, sl], lmean[:, sl], ALU.mult), reads=[("lstat", "mean", pi)],
                     writes=[("lstat", "msq", pi)])
                P.op("dve", STT(lrstd[:, sl], ps[b2][:, 0:n], 1.0 / D, lmsq[:, sl], ALU.mult, ALU.subtract),
                     reads=[("ps", b2), ("lstat", "msq", pi)], writes=[("lstat", "rstd", pi)])
                P.op("act", ACTF(lrstd[:, sl], lrstd[:, sl], AF.Sqrt, bias=LN_EPS),
                     reads=[("lstat", "rstd", pi)], writes=[("lstat", "rstd", pi)])
                P.op("dve", lambda e, sl=sl: e.reciprocal(out=lrstd[:, sl], in_=lrstd[:, sl]),
                     reads=[("lstat", "rstd", pi)], writes=[("lstat", "rstd", pi)])
                reserved.discard(b1)
                reserved.discard(b2)
            lo = pieces[0][0]
            hi = pieces[-1][0] + pieces[-1][1]
            sl = slice(lo, hi)
            srd = [("lstat", "mean", pi) for pi in range(len(pieces))] + [("lstat", "rstd", pi) for pi in range(len(pieces))]
            for c in range(DC):
                eng_ = "dve"
                r = c % 4
                P.op(eng_, TT(ltmp[r][:, sl], h0f[:, c, sl], lmean[:, sl], ALU.subtract),
                     reads=[("h0f", c)] + srd, writes=[("ltmp", r)])
                P.op(eng_, TT(ltmp[r][:, sl], ltmp[r][:, sl], lrstd[:, sl], ALU.mult),
                     reads=[("ltmp", r)] + srd, writes=[("ltmp", r)])
                P.op("act", ACTF(h0f[:, c, sl], ltmp[r][:, sl], AF.Identity, bias=cc(bcol + c), scale=cc(gcol + c)),
                     reads=[("ltmp", r), "cst"], writes=[("h0f", c)])
                if out_bf:
                    P.op("act", ACTF(h1bf[:, c, sl], ltmp[r][:, sl], AF.Identity, bias=cc(bcol + c), scale=cc(gcol + c)),
                         reads=[("ltmp", r), "cst"], writes=[("h1bf", c)])

        items_h = halo_items()
        plans = [pair_plan(j) for j in range(9)]
        out_ops = []

        class _Stop(Exception):
            pass

        def check_stop(name):
            if stop_after != name:
                return
            lasts = [P.ops[e][-1] for e in ("pe", "act", "dve") if P.ops[e]]
            for nm, t in (("A", RA), ("H", RH), ("B", RB), ("Y", RY)):
                o = P.op("sp", DMA(dbg_d[nm][:, :], t[:, :]), dsem=("dbg", nm), extra_deps=lasts)
                out_ops.append(o)
            raise _Stop()

        CONT = (1, 3, 4, 5)
        for u in range(NU):
          try:
            check_stop("setup")
            fb = 0
            fcol = lambda col: flg[:, fb + col: fb + col + 1]
            P.op("sp", DMA(flg[:, fb:fb + NFLAG], flg_d[u]), writes=[("flg", 0)], dsem=("flg", 0))
            FL = ("flg", 0)

            P.cur_tag = "P1a"
            def stage_a(t):
                    p = 128 if t < 9 else 16
                    xb_ = t % 3
                    stt_t, mv, rstd_s, nmr_s = stt_l[xb_], mv_l[xb_], rstd_l[xb_], nmr_l[xb_]
                    tok0 = t * 128
                    P.op("sp", DMA(xt[xb_][0:p, :], xu[u, tok0:tok0 + p, :]), writes=[("xt", xb_)], dsem=("x", xb_))
                    for q in range(4):
                        P.op("dve", lambda e, q=q, p=p, xb_=xb_, stt_t=stt_t: e.bn_stats(out=stt_t[0:p, q, :], in_=xt[xb_][0:p, 512 * q:512 * q + 512]),
                             reads=[("xt", xb_)], writes=[("bnst", xb_, q)])
                    P.op("dve", lambda e, p=p, stt_t=stt_t, mv=mv: e.bn_aggr(out=mv[0:p, :], in_=stt_t[0:p].rearrange("p a b -> p (a b)")),
                         reads=[("bnst", xb_, q) for q in range(4)], writes=[("mv", xb_)])
                    P.op("act", ACTF(rstd_s[0:p, :], mv[0:p, 1:2], AF.Sqrt, bias=LN_EPS), reads=[("mv", xb_)], writes=[("rstd_s", xb_)])
                    P.op("dve", lambda e, p=p, rstd_s=rstd_s: e.reciprocal(out=rstd_s[0:p, :], in_=rstd_s[0:p, :]),
                         reads=[("rstd_s", xb_)], writes=[("rstd_s", xb_)])
                    P.op("dve", TS(nmr_s[0:p, :], mv[0:p, 0:1], rstd_s[0:p, 0:1], -1.0, ALU.mult, ALU.mult),
                         reads=[("mv", xb_), ("rstd_s", xb_)], writes=[("nmr_s", xb_)])
                    P.op("act", ACTF(xt[xb_][0:p, :], xt[xb_][0:p, :], AF.Identity, bias=nmr_s[0:p, 0:1], scale=rstd_s[0:p, 0:1]),
                         reads=[("xt", xb_), ("rstd_s", xb_), ("nmr_s", xb_)], writes=[("xt", xb_)])

            def stage_b(t):
                    p = 128 if t < 9 else 16
                    xb_ = t % 3
                    tok0 = t * 128
                    ja = max(0, tok0 - TAU_M)
                    jb = min(NM, tok0 + p - TAU_M)
                    for g4 in range(4):
                        b = nb()
                        eng_ = "act" if (g4 + t) % 2 == 0 else "dve"
                        for k in range(4):
                            c = 4 * g4 + k
                            P.op("pe", TR(ps[b][:, k * 128:k * 128 + p], xt[xb_][0:p, c * 128:(c + 1) * 128], ident[0:p, 0:p]),
                                 reads=[("xt", xb_), "ident"], writes=[("ps", b)])
                        for k in range(4):
                            c = 4 * g4 + k
                            src = ps[b][:, k * 128:k * 128 + p]
                            if eng_ == "act":
                                P.op("act", ACTF(h0bf[:, c, tok0:tok0 + p], src, AF.Identity,
                                                 bias=cc(C_LNIN_B + c), scale=cc(C_LNIN_G + c)),
                                     reads=[("ps", b), "cst"], writes=[("h0bf", c, t)])
                            else:
                                P.op("dve", TS(h0bf[:, c, tok0:tok0 + p], src, cc(C_LNIN_G + c), cc(C_LNIN_B + c), ALU.mult, ALU.add),
                                     reads=[("ps", b), "cst"], writes=[("h0bf", c, t)])
                            if jb > ja:
                                o0 = TAU_M + ja - tok0
                                src2 = ps[b][:, k * 128 + o0:k * 128 + o0 + (jb - ja)]
                                if eng_ == "act":
                                    P.op("act", ACTF(h0f[:, c, ja:jb], src2, AF.Identity, bias=cc(C_LNIN_B + c), scale=cc(C_LNIN_G + c)),
                                         reads=[("ps", b), "cst"], writes=[("h0f", c)])
                                else:
                                    P.op("dve", TS(h0f[:, c, ja:jb], src2, cc(C_LNIN_G + c), cc(C_LNIN_B + c), ALU.mult, ALU.add),
                                         reads=[("ps", b), "cst"], writes=[("h0f", c)])

            cont = u in CONT
            tiles = list(range(2, 9)) if cont else list(range(NTILE))
            for i_ in range(len(tiles) + 1):
                if i_ < len(tiles):
                    stage_a(tiles[i_])
                if i_ >= 1:
                    stage_b(tiles[i_ - 1])
            P.fence("Y")
            h0all = [("h0bf", c, t) for c in range(DC) for t in tiles]
            check_stop("P1a")

            P.cur_tag = "P1b"
            ev = [0]

            def evac(out, in_, reads, writes):
                ev[0] += 1
                if ev[0] % 2:
                    P.op("act", ACP(out, in_), reads=reads, writes=writes)
                else:
                    P.op("dve", CP(out, in_), reads=reads, writes=writes)

            if cont:
                for tt in range(5):
                    P.op("pool", CP(KT[:, :, 128 * tt:128 * tt + 128], KT[:, :, 128 * (tt + 4):128 * (tt + 4) + 128]),
                         reads=[("KT", c) for c in range(8)], writes=[("KT", c) for c in range(8)])
                    P.op("pool", CP(Vt[:, tt, :], Vt[:, tt + 4, :]), reads=[("V", tt + 4, q) for q in range(4)],
                         writes=[("V", tt, q) for q in range(4)])
            for s in range(16):
                sl_, sres = next_slab()
                sv = slab[sl_].rearrange("p (kc n) -> p kc n", kc=16)
                kind = s // 4
                for e_ in range(2):
                    oc = 2 * (s % 4) + e_
                    if kind == 0:
                        pcs = [(TAU_U, 0, 265), (TAU_U + 265, 265, 265)]
                    elif kind == 1:
                        pcs = [(TAU_M, 0, 257), (TAU_M + 257, 257, 257)]
                    elif kind == 2:
                        pcs = [(640, 640, 512)] if cont else [(0, 0, 512), (512, 512, 512), (1024, 1024, 144)]
                    else:
                        pcs = []
                    for (tau0, o0, n) in pcs:
                        b = nb()
                        for kc in range(DC):
                            P.op("pe", MM(ps[b][:, 0:n], sv[:, kc, e_ * 128:(e_ + 1) * 128], h0bf[:, kc, tau0:tau0 + n],
                                          kc == 0, kc == DC - 1),
                                 reads=[sres] + (h0all if kc == 0 else []), writes=[("ps", b)])
                        if kind == 0:
                            evac(Ut[:, oc, o0:o0 + n], ps[b][:, 0:n], [("ps", b)], [("U", oc)])
                        elif kind == 1:
                            evac(QT[:, oc, o0:o0 + n], ps[b][:, 0:n], [("ps", b)], [("QT", oc)])
                        else:
                            evac(KT[:, oc, o0:o0 + n], ps[b][:, 0:n], [("ps", b)], [("KT", oc)])
                if kind == 3:
                    f0 = 256 * (s % 4)
                    for t2 in ((5, 7) if cont else range(0, NTILE, 2)):
                        b = nb()
                        for tt in (t2, t2 + 1):
                            p = 128 if tt < 9 else 16
                            off = 256 * (tt - t2)
                            for kc in range(DC):
                                P.op("pe", MM(ps[b][0:p, off:off + 256], h0bf[:, kc, tt * 128:tt * 128 + p], sv[:, kc, :],
                                              kc == 0, kc == DC - 1),
                                     reads=[sres] + (h0all if kc == 0 else []), writes=[("ps", b)])
                        evac(Vt[:, t2, f0:f0 + 256], ps[b][:, 0:256], [("ps", b)], [("V", t2, s % 4)])
                        p = 128 if t2 + 1 < 9 else 16
                        evac(Vt[0:p, t2 + 1, f0:f0 + 256], ps[b][0:p, 256:512], [("ps", b)], [("V", t2 + 1, s % 4)])
            check_stop("P1b")
            P.fence("A")
            Vall = [("V", t, q) for t in range(NTILE) for q in range(4)]

            P.cur_tag = "P2a"
            P.op("dve", TS(Ut[:, :, 521:530], Ut[:, :, 521:530], fcol(F_FPOST), None, ALU.mult),
                 reads=[("U", c) for c in range(8)] + [FL], writes=[("U", c) for c in range(8)])
            for g in range(4):
                w = 2 ** (g + 1)
                for kc2 in range(2):
                    c = 2 * g + kc2
                    U = Ut[:, c, :]
                    P.op("dve", TT(pt1[:, 1:530], U[:, 0:529], U[:, 1:530], ALU.add), reads=[("U", c)], writes=["pt1"])
                    cur, curname = pt1, "pt1"
                    if g >= 1:
                        P.op("dve", TT(pt2[:, 2:529], pt1[:, 1:528], pt1[:, 3:530], ALU.add), reads=["pt1"], writes=["pt2"])
                        cur, curname = pt2, "pt2"
                    if g >= 2:
                        P.op("dve", TT(pt1[:, 4:527], pt2[:, 2:525], pt2[:, 6:529], ALU.add), reads=["pt2"], writes=["pt1"])
                        cur, curname = pt1, "pt1"
                    if g >= 3:
                        P.op("dve", TT(pt2[:, 8:523], pt1[:, 4:519], pt1[:, 12:527], ALU.add), reads=["pt1"], writes=["pt2"])
                        cur, curname = pt2, "pt2"
                    P.op("dve", TT(cur[:, 513:521], cur[:, 513:521], flg[:, fb + F_CORR + 8 * g: fb + F_CORR + 8 * g + 8], ALU.mult),
                         reads=[curname, FL], writes=[curname])
                    P.op("dve", STT(mT[:, c, :], cur[:, 8:8 + NM], 1.0 / w, U[:, 8:8 + NM], ALU.mult, ALU.subtract),
                         reads=[curname, ("U", c)], writes=[("m", c)])
                for e_ in range(2):
                    oc = 2 * g + e_
                    for half in range(2):
                        b = nb()
                        for kc2 in range(2):
                            P.op("pe", MM(ps[b][:, 0:257], wp[:, g, kc2, e_ * 128:(e_ + 1) * 128],
                                          mT[:, 2 * g + kc2, 257 * half:257 * half + 257], kc2 == 0, kc2 == 1),
                                 reads=[("m", 2 * g + kc2), ("wp", g)], writes=[("ps", b)])
                        P.op("act", ACTF(ypool[:, oc, 257 * half:257 * half + 257], ps[b][:, 0:257], AF.Identity,
                                         scale=cc(C_PSCALE + oc)),
                             reads=[("ps", b), "cst"], writes=[("ypool", oc)])

            check_stop("P2a")
            prefetch_slabs()
            P.cur_tag = "P2b"
            first = True
            for h in range(16):
                hp, pi = h // 2, h % 2
                pr = slice(64 * pi, 64 * pi + 64)
                for it, (kind, j) in enumerate(items_h):
                    col = h * NHALO + it
                    qcol = NM - 1 if kind in ("post", "metapost") else 0
                    if j is None:
                        lhs = KT[pr, hp, 1152:1280]
                        out = ps[7][:, col:col + 1]
                    else:
                        lhs = KT[pr, hp, 128 * j:128 * j + 128]
                        out = ps[7][:, col:col + 1]
                    P.op("pe", MM(out, lhs, QT[pr, hp, qcol:qcol + 1], True, True),
                         reads=[("KT", hp), ("QT", hp)], writes=[("ps", 7)])
            P.op("dve", STT(sbh[:, :], ps[7][:, 0:240], 0.125, cst[:, C_GH:C_GH + 240], ALU.mult, ALU.add),
                 reads=[("ps", 7), "cst"], writes=["sbh"])
            P.op("dve", TT(sbh[:, :], sbh[:, :], flg[:, fb + F_FH: fb + F_FH + 240], ALU.add), reads=["sbh", FL], writes=["sbh"])
            P.op("act", ACTF(pth[:, :], sbh[:, :], AF.Exp), reads=["sbh"], writes=["pth"])

            def load_gp(hp_):
                P.op("sp", DMA(gpb[hp_ % 2].rearrange("p a b -> p (a b)"), gp_d[hp_]), writes=[("gpb", hp_ % 2)],
                     dsem=("gp", hp_ % 2))

            load_gp(0)
            steps = [None] + [j for j in range(9) if plans[j] is not None]
            nst = len(steps)
            sbank = [0, 1, 2]

            def emit_S(hp, i, pi):
                j = steps[i]
                n = 2 * (hp * nst + i) + pi
                h = 2 * hp + pi
                pr = slice(64 * pi, 64 * pi + 64)
                b = sbank[n % 3]
                if j is None:
                    P.op("pe", MM(ps[b][:, 0:512], KT[pr, hp, 1152:1280], QT[pr, hp, 1:513], True, True),
                         reads=[("KT", hp), ("QT", hp)], writes=[("ps", b)])
                    P.op("act", ACTF(ptb[n % 8][:, :], ps[b][:, 0:512], AF.Exp, bias=cc(C_METAB + h), scale=0.125),
                         reads=[("ps", b), "cst"], writes=[("ptb", n % 8)])
                else:
                    ra, rb, segs = plans[j]
                    qa, qb = 64 * ra, 64 * rb + 64
                    n_ = qb - qa
                    k0 = 2 * j - 6
                    gofs = (ra - k0 + 7) * 64
                    P.op("pe", MM(ps[b][:, 0:n_], KT[pr, hp, 128 * j:128 * j + 128], QT[pr, hp, 1 + qa:1 + qb], True, True),
                         reads=[("KT", hp), ("QT", hp)], writes=[("ps", b)])
                    P.op("dve", STT(sbt[n % 3][:, 0:n_], ps[b][:, 0:n_], 0.125, gpb[hp % 2][:, pi, gofs:gofs + n_],
                                    ALU.mult, ALU.add),
                         reads=[("ps", b), ("gpb", hp % 2)], writes=[("sbt", n % 3)])
                    for (r0, r1, combo) in segs:
                        a0, a1 = 64 * r0 - qa, 64 * r1 + 64 - qa
                        P.op("act", ACTF(ptb[n % 8][:, a0:a1], sbt[n % 3][:, a0:a1], AF.Exp, bias=fcol(F_BTAB + combo)),
                             reads=[("sbt", n % 3), FL], writes=[("ptb", n % 8)])

            def emit_PV(hp, i, pi):
                j = steps[i]
                n = 2 * (hp * nst + i) + pi
                h = 2 * hp + pi
                bV, bO = 3 + hp % 2, 5 + hp % 2
                pr = slice(64 * pi, 64 * pi + 64)
                last = i == nst - 1
                if j is None:
                    P.op("pe", MM(ps[bV][pr, 0:512], Vt[:, 9, 64 * h:64 * h + 64], ptb[n % 8][:, :], True, False),
                         reads=[("ptb", n % 8)] + Vall, writes=[("ps", bV)])
                    P.op("pe", MM(ps[bO][pr, 0:512], onesb[:, 0:64], ptb[n % 8][:, :], True, False),
                         reads=[("ptb", n % 8), "ones"], writes=[("ps", bO)])
                else:
                    ra, rb, segs = plans[j]
                    qa, qb = 64 * ra, 64 * rb + 64
                    n_ = qb - qa
                    P.op("pe", MM(ps[bV][pr, qa:qb], Vt[:, j, 64 * h:64 * h + 64], ptb[n % 8][:, 0:n_], False, last),
                         reads=[("ptb", n % 8)], writes=[("ps", bV)])
                    P.op("pe", MM(ps[bO][pr, qa:qb], onesb[:, 0:64], ptb[n % 8][:, 0:n_], False, last),
                         reads=[("ptb", n % 8), "ones"], writes=[("ps", bO)])

            def emit_norm(hp):
                bV, bO = 3 + hp % 2, 5 + hp % 2
                P.op("dve", lambda e, bO=bO: e.reciprocal(out=pt1[:, 0:512], in_=ps[bO][:, 0:512]), reads=[("ps", bO)],
                     writes=["pt1"])
                P.op("dve", TT(yattn[:, hp, 1:513], ps[bV][:, 0:512], pt1[:, 0:512], ALU.mult),
                     reads=[("ps", bV), "pt1"], writes=[("yattn", hp)])

            allsteps = [(hp, i) for hp in range(8) for i in range(nst)]
            LAG = 2
            for idx in range(len(allsteps) + LAG):
                if idx < len(allsteps):
                    hp, i = allsteps[idx]
                    if i == 0 and hp + 1 < 8:
                        load_gp(hp + 1)
                    emit_S(hp, i, 0)
                    emit_S(hp, i, 1)
                if idx >= LAG:
                    hp2, i2 = allsteps[idx - LAG]
                    emit_PV(hp2, i2, 0)
                    emit_PV(hp2, i2, 1)
                    if i2 == nst - 1:
                        emit_norm(hp2)
            for h in range(16):
                hp, pi = h // 2, h % 2
                pr = slice(64 * pi, 64 * pi + 64)
                for which in range(2):
                    its = [(it, kj) for it, kj in enumerate(items_h)
                           if (kj[0] in ("post", "metapost")) == (which == 1)]
                    cV = 256 + which * 8 + hp
                    cO = 288 + which * 8 + hp
                    for pass_ in range(2):
                        for ii, (it, (kind, j)) in enumerate(its):
                            col = h * NHALO + it
                            firstf, lastf = ii == 0, ii == len(its) - 1
                            tj = 9 if j is None else j
                            if pass_ == 0:
                                P.op("pe", MM(ps[7][pr, cV:cV + 1], Vt[:, tj, 64 * h:64 * h + 64], pth[:, col:col + 1], firstf, lastf),
                                     reads=["pth"] + (Vall if ii == 0 else []), writes=[("ps", 7)])
                            else:
                                P.op("pe", MM(ps[7][pr, cO:cO + 1], onesb[:, 0:64], pth[:, col:col + 1], firstf, lastf),
                                     reads=["pth", "ones"], writes=[("ps", 7)])
            P.op("dve", lambda e: e.reciprocal(out=rech[:, :], in_=ps[7][:, 288:304]), reads=[("ps", 7)], writes=["rech"])
            P.op("dve", TT(yattn[:, :, 0], ps[7][:, 256:264], rech[:, 0:8], ALU.mult), reads=[("ps", 7), "rech"],
                 writes=[("yattn", c) for c in range(8)])
            P.op("dve", TT(yattn[:, :, NM - 1], ps[7][:, 264:272], rech[:, 8:16], ALU.mult), reads=[("ps", 7), "rech"],
                 writes=[("yattn", c) for c in range(8)])
            check_stop("P2b")
            P.fence("A")
            P.fence("B")

            P.cur_tag = "P3"
            ln1 = ln_begin([(0, 257), (257, 257)])
            for s in range(8):
                sl_, sres = next_slab()
                sv = slab[sl_].rearrange("p (kc n) -> p kc n", kc=16)
                for e_ in range(2):
                    oc = 2 * s + e_
                    if oc >= 2:
                        ln_chunk(ln1, oc - 2)
                    for half in range(2):
                        b = nb()
                        hs = slice(257 * half, 257 * half + 257)
                        for kc in range(DC):
                            src = ypool[:, kc, hs] if kc < 8 else yattn[:, kc - 8, hs]
                            rd = ("ypool", kc) if kc < 8 else ("yattn", kc - 8)
                            P.op("pe", MM(ps[b][:, 0:257], sv[:, kc, e_ * 128:(e_ + 1) * 128], src, kc == 0, kc == DC - 1),
                                 reads=[sres, rd], writes=[("ps", b)])
                        P.op("dve", STT(h0f[:, oc, hs], h0f[:, oc, hs], ALPHA, ps[b][:, 0:257], ALU.mult, ALU.add),
                             reads=[("ps", b), ("h0f", oc)], writes=[("h0f", oc)])
            check_stop("P3a")
            P.cur_tag = "LN1"
            ln_chunk(ln1, DC - 2)
            ln_chunk(ln1, DC - 1)
            ln_finish(ln1, C_LN1_G, C_LN1_B, True)
            check_stop("P3")
            P.fence("Y")

            P.cur_tag = "P4up"
            for part in range(2):
                P.cur_tag = "P4up"
                j0 = 0 if part == 0 else 22
                jr = range(0, 22) if part == 0 else range(22, NFC)
                def up_post(j, banks4):
                    zi = j % 2
                    for ag in range(2):
                        for half in range(2):
                            b = banks4[2 * ag + half]
                            hs = slice(257 * half, 257 * half + 257)
                            P.op("act", ACTF(zb[zi][:, ag, hs], ps[b][:, 0:257], AF.Identity, bias=cc(C_BUP + ag * NFC + j)),
                                 reads=[("ps", b), "cst"], writes=[("z", zi)])
                    P.op("dve", TS(zb[zi][:, :, NM - 1:NM], zb[zi][:, :, NM - 1:NM], fcol(F_FPOST), None, ALU.mult),
                         reads=[("z", zi), FL], writes=[("z", zi)])
                    outs = [(cab, "ca"), (cgb, "cg")]
                    for ag in range(2):
                        ob, on = outs[ag]
                        cj = ag * NFC + j
                        z = zb[zi][:, ag, :]
                        P.op("act", ACTF(ob[zi][:, :], z[:, 1:513], AF.Identity, bias=cc(C_CB + cj), scale=cc(C_CW + 86 + cj)),
                             reads=[("z", zi), "cst"], writes=[(on, zi)])
                        P.op("dve", STT(ob[zi][:, :], z[:, 0:512], cc(C_CW + cj), ob[zi][:, :], ALU.mult, ALU.add),
                             reads=[("z", zi), (on, zi), "cst"], writes=[(on, zi)])
                        P.op("dve", STT(ob[zi][:, :], z[:, 2:514], cc(C_CW + 172 + cj), ob[zi][:, :], ALU.mult, ALU.add),
                             reads=[("z", zi), (on, zi), "cst"], writes=[(on, zi)])
                    P.op("act", ACTF(ggb[zi][:, :], cgb[zi][:, :], AF.Gelu), reads=[("cg", zi)], writes=[("gg", zi)])
                    P.op("dve", TT(actb[:, j - j0, :], cab[zi][:, :], ggb[zi][:, :], ALU.mult), reads=[("ca", zi), ("gg", zi)],
                         writes=[("act", j - j0)])

                jlist = list(jr)
                if part == 0:
                    grp = []
                    for j in jlist[:2]:
                        sl_, sres = next_slab(NSLAB if j == jlist[0] else NSLAB - 1)
                        sv = slab[sl_].rearrange("p (kc n) -> p kc n", kc=16)
                        grp.append((j, sres, sv, [nb() for _ in range(4)]))
                    for kc in range(DC):
                        for (j, sres, sv, banks4) in grp:
                            for ag in range(2):
                                for half in range(2):
                                    b = banks4[2 * ag + half]
                                    hs = slice(257 * half, 257 * half + 257)
                                    P.op("pe", MM(ps[b][:, 0:257], sv[:, kc, ag * 128:(ag + 1) * 128], h1bf[:, kc, hs], kc == 0, kc == DC - 1),
                                         reads=[sres, ("h1bf", kc)], writes=[("ps", b)])
                    for (j, sres, sv, banks4) in grp:
                        up_post(j, banks4)
                    jlist = jlist[2:]
                for j in jlist:
                    sl_, sres = next_slab()
                    sv = slab[sl_].rearrange("p (kc n) -> p kc n", kc=16)
                    banks4 = []
                    for ag in range(2):
                        for half in range(2):
                            b = nb()
                            banks4.append(b)
                            hs = slice(257 * half, 257 * half + 257)
                            for kc in range(DC):
                                P.op("pe", MM(ps[b][:, 0:257], sv[:, kc, ag * 128:(ag + 1) * 128], h1bf[:, kc, hs], kc == 0, kc == DC - 1),
                                     reads=[sres, ("h1bf", kc)], writes=[("ps", b)])
                    up_post(j, banks4)
                P.cur_tag = "P4dn"
                k0, nk = (0, 22) if part == 0 else (22, 21)
                if part == 1:
                    ln2 = ln_begin([(1, 512)])
                def dn_evac(oc, b):
                    P.op("dve", STT(h0f[:, oc, 1:513], h0f[:, oc, 1:513], ALPHA if part == 0 else 1.0, ps[b][:, 0:512], ALU.mult, ALU.add),
                         reads=[("ps", b), ("h0f", oc)], writes=[("h0f", oc)])

                grp = []
                for oc in range(2):
                    sl_, sres = next_slab(NSLAB if oc == 0 else NSLAB - 1)
                    sv = slab[sl_][:, 0:nk * 128].rearrange("p (kc n) -> p kc n", kc=nk)
                    grp.append((oc, sres, sv, nb()))
                for kk in range(nk):
                    for (oc, sres, sv, b) in grp:
                        P.op("pe", MM(ps[b][:, 0:512], sv[:, kk, :], actb[:, kk, :], kk == 0, kk == nk - 1),
                             reads=[sres, ("act", kk)], writes=[("ps", b)])
                for (oc, sres, sv, b) in grp:
                    dn_evac(oc, b)
                for oc in range(2, DC):
                    if part == 1:
                        ln_chunk(ln2, oc - 2)
                    b = nb()
                    sl_, sres = next_slab()
                    sv = slab[sl_][:, 0:nk * 128].rearrange("p (kc n) -> p kc n", kc=nk)
                    for kk in range(nk):
                        P.op("pe", MM(ps[b][:, 0:512], sv[:, kk, :], actb[:, kk, :], kk == 0, kk == nk - 1),
                             reads=[sres, ("act", kk)], writes=[("ps", b)])
                    dn_evac(oc, b)
            check_stop("P4")
            P.fence("Y")
            P.fence("B")

            P.cur_tag = "P5"
            ln_chunk(ln2, DC - 2)
            ln_chunk(ln2, DC - 1)
            ln_finish(ln2, C_LN2_G, C_LN2_B, False)
            check_stop("P5a")
            for tt in range(4):
                ob_ = tt % 2
                for g4 in range(4):
                    b = nb()
                    for k in range(4):
                        c = 4 * g4 + k
                        P.op("pe", TR(ps[b][:, k * 128:(k + 1) * 128], h0f[:, c, 1 + 128 * tt:1 + 128 * tt + 128], ident[:, :]),
                             reads=[("h0f", c), "ident"], writes=[("ps", b)])
                    evac(otile[ob_][:, 512 * g4:512 * g4 + 512], ps[b][:, :], [("ps", b)], [("otile", ob_, g4)])
                o = P.op("pool", DMA(y_d[u, 128 * tt:128 * tt + 128, :], otile[ob_][:, :]),
                         reads=[("otile", ob_, g4) for g4 in range(4)], dsem=("out", ob_))
                out_ops.append(o)
            P.fence("Y")
            P.fence("A")
          except _Stop:
            break

        finals = {}
        for o in out_ops:
            finals[o.dsem] = o
        P.emit(final_wait_ops=list(finals.values()))
    return nc


def _unit_table():
    units = []
    for core in range(NCORE):
        ps_, run = core // 4, core % 4
        for k in range(2):
            units.append((ps_, 2 * run + k, 8))
        for k in range(4):
            units.append((2 + ps_, 4 * run + k, 16))
    return units


def _status_value(stt, start, end):
    if stt == ST_I:
        return 0.0
    if stt == ST_S:
        return 0.0 if start else -BIG
    if stt == ST_E:
        return 0.0 if end else -BIG
    if stt == ST_NS:
        return -BIG if start else 0.0
    if stt == ST_NE:
        return -BIG if end else 0.0
    return -BIG


def _build_gp(rpb):
    H = rpb.shape[0]
    gp = np.zeros((H, 2, 64, 16, 64), np.float32)
    c = np.arange(64)
    cs = np.clip(c - 8, 0, 48)
    key = np.arange(64)
    ok = (key[:, None] >= cs[None, :]) & (key[:, None] < cs[None, :] + 16)
    dc = np.clip(key[:, None] - c[None, :] + 15, 0, 30)
    for rp in range(2):
        for di in range(16):
            dr = rp - (di - 7) + 7
            if 0 <= dr <= 14:
                vals = rpb[:, dr, :][:, dc]
                gp[:, rp, :, di, :] = np.where(ok[None], vals, np.float32(-BIG))
            else:
                gp[:, rp, :, di, :] = np.where(ok, np.float32(0.0), np.float32(-BIG))[None]
    return gp.reshape(H, 128, 1024)


def _prepare(x_prompt, x_sample, meta_tokens, ln_in_g, ln_in_b, w_in, w_pool, pool_scale, rpb, meta_bias,
             w_out, ln1_g, ln1_b, w_up, b_up, conv_w, conv_b, w_down, ln2_g, ln2_b, cores=None):
    f32 = np.float32
    xs = [np.asarray(x_prompt[0], f32), np.asarray(x_prompt[1], f32), np.asarray(x_sample[0], f32),
          np.asarray(x_sample[1], f32)]
    meta = np.asarray(meta_tokens, f32)
    rpb0 = np.asarray(rpb, f32)[0]
    mb0 = np.asarray(meta_bias, f32)[0]
    units = _unit_table()
    assert len(units) == NCORE * NU

    gp = _build_gp(rpb0)
    gp_pairs = np.ascontiguousarray(gp.reshape(8, 2, 128, 1024).transpose(0, 2, 1, 3).reshape(8, 128, 2048))
    cst = np.zeros((128, NCONST), f32)
    col = lambda v: np.asarray(v, f32).reshape(-1, 128).T
    cst[:, C_LNIN_G:C_LNIN_G + 16] = col(ln_in_g)
    cst[:, C_LNIN_B:C_LNIN_B + 16] = col(ln_in_b)
    cst[:, C_LN1_G:C_LN1_G + 16] = col(ln1_g)
    cst[:, C_LN1_B:C_LN1_B + 16] = col(ln1_b)
    cst[:, C_LN2_G:C_LN2_G + 16] = col(ln2_g)
    cst[:, C_LN2_B:C_LN2_B + 16] = col(ln2_b)
    cst[:, C_PSCALE:C_PSCALE + 8] = col(pool_scale)
    cst[:, C_BUP:C_BUP + 86] = col(b_up)
    cw = np.asarray(conv_w, f32)[0]
    for k in range(3):
        cst[:, C_CW + 86 * k:C_CW + 86 * k + 86] = col(cw[k])
    cst[:, C_CB:C_CB + 86] = col(conv_b)
    cst[:, C_METAB:C_METAB + 16] = -BIG
    cst[0:16, C_METAB:C_METAB + 16] = mb0.T
    items = halo_items()
    for h in range(16):
        for it, (kind, j) in enumerate(items):
            c_ = C_GH + h * NHALO + it
            if j is None:
                cst[:, c_] = -BIG
                cst[0:16, c_] = mb0[h]
            else:
                dl, cq = halo_geom(kind, j)
                cst[:, c_] = gp[h][:, (dl + 7) * 64 + cq]

    def unit_flags(start, end):
        fl = np.zeros((128, NFLAG), f32)
        for stt in range(6):
            for sbb in range(6):
                fl[0:64, F_BTAB + stt * 6 + sbb] = _status_value(stt, start, end)
                fl[64:128, F_BTAB + stt * 6 + sbb] = _status_value(sbb, start, end)
        for h in range(16):
            for it, (kind, j) in enumerate(items):
                if j is None:
                    continue
                k0 = 2 * j - 6
                c_ = F_FH + h * NHALO + it
                fl[0:64, c_] = _status_value(halo_status(kind, k0), start, end)
                fl[64:128, c_] = _status_value(halo_status(kind, k0 + 1), start, end)
        fl[:, F_FPOST] = 0.0 if end else 1.0
        for g in range(4):
            w = 2 ** (g + 1)
            half = w // 2
            for i in range(8):
                tau_own = 504 + i
                cnt = w
                if end and tau_own + half > 512:
                    cnt = 512 - tau_own + half
                fl[:, F_CORR + 8 * g + i] = f32(w) / f32(cnt)
        return fl

    in_maps = []
    for core in (range(NCORE) if cores is None else cores):
        xu = np.zeros((NU, TOK, D), f32)
        flg = np.zeros((NU, 128, NFLAG), f32)
        for k in range(NU):
            s, uu, nun = units[core * NU + k]
            X = xs[s]
            nrows = nun * 8
            r0 = 8 * uu
            for row in range(-6, 12):
                gr = r0 + row
                if 0 <= gr < nrows:
                    xu[k, (row + 6) * 64:(row + 7) * 64] = X[gr * 64:(gr + 1) * 64]
            if uu == 0:
                xu[k, 384 - 16:384] = meta
            xu[k, 1152:1168] = meta
            flg[k] = unit_flags(uu == 0, uu == nun - 1)
        in_maps.append({
            "xu": xu, "w_in": np.ascontiguousarray(np.asarray(w_in, f32)[0]),
            "w_out": np.ascontiguousarray(np.asarray(w_out, f32)[0]),
            "w_up": np.ascontiguousarray(np.asarray(w_up, f32)[0]),
            "w_down": np.ascontiguousarray(np.asarray(w_down, f32)[0]),
            "w_pool": np.ascontiguousarray(np.asarray(w_pool, f32)[0]),
            "gp": gp_pairs, "cst": cst, "flg": flg,
        })

    return in_maps, units


def kernel(**inputs):
    in_maps, units = _prepare(**inputs)
    nc = build_program()
    res = run_bass_kernel_spmd(nc, in_maps, core_ids=list(range(NCORE)))
    return _assemble(res, units)


def _assemble(res, units):
    f32 = np.float32
    yp = np.zeros((2, 4096, D), f32)
    ysm = np.zeros((2, 8192, D), f32)
    outs = [yp[0], yp[1], ysm[0], ysm[1]]
    for core in range(NCORE):
        y = np.asarray(res.results[core]["y"], f32)
        for k in range(NU):
            s, uu, nun = units[core * NU + k]
            outs[s][512 * uu:512 * uu + 512] = y[k]
    return (yp, ysm)
```

```python
import contextlib
import os
import numpy as np
import concourse.bass as bass
import concourse.mybir as mybir
from concourse.bass_utils import run_bass_kernel_spmd

F32 = mybir.dt.float32
BF = mybir.dt.bfloat16
AF = mybir.ActivationFunctionType
ALU = mybir.AluOpType

D = 2048
DC = 16
NCORE = 8
NU = 6
TOK = 1168
NTILE = 10
NM = 514
NUC = 530
TAU_M = 383
TAU_U = 375
DFF = 5504
NFC = 43
BIG = 30000.0
ALPHA = float(2.0 ** 0.25)
LN_EPS = 1e-5
N_META = 16

C_LNIN_G, C_LNIN_B, C_LN1_G, C_LN1_B, C_LN2_G, C_LN2_B = 0, 16, 32, 48, 64, 80
C_PSCALE = 96
C_BUP = 104
C_CW = 190
C_CB = 448
C_METAB = 534
C_GH = 550
NCONST = 790
F_BTAB, F_FH, F_FPOST, F_CORR = 0, 36, 276, 277
NFLAG = 309
NHALO = 15
NSLAB = 5

ENGS = ("pe", "act", "dve", "pool", "sp")


class Op:
    __slots__ = ("eng", "fn", "deps", "is_dma", "dsem", "needs_inc", "semval")

    def __init__(self, eng, fn, dsem=None):
        self.eng = eng
        self.fn = fn
        self.deps = ()
        self.is_dma = dsem is not None
        self.dsem = dsem
        self.needs_inc = False
        self.semval = None


class Prog:
    def __init__(self, nc):
        self.nc = nc
        self.ops = {e: [] for e in ENGS}
        self.last_w = {}
        self.readers = {}
        self.reg_of = {}
        self.reg_cur = {}
        self.reg_fence = {}
        self.dma_cnt = {}
        self.excl = {}
        self.pe_tags = []
        self.cur_tag = "setup"
        self.trace = None

    def region(self, reg, names):
        for n in names:
            self.reg_of[n] = reg

    def fence(self, reg):
        cur = self.reg_cur.get(reg)
        if cur:
            self.reg_fence[reg] = dict(cur)
            self.reg_cur[reg] = {}

    @staticmethod
    def _key(o):
        return ("dma", id(o)) if o.is_dma else o.eng

    def op(self, eng, fn, reads=(), writes=(), dsem=None, extra_deps=()):
        o = Op(eng, fn, dsem)
        if eng == "pe":
            self.pe_tags.append(self.cur_tag)
        deps = {id(x): x for x in extra_deps}
        regs = set()
        px = [r for r in tuple(reads) + tuple(writes) if isinstance(r, tuple) and r[0] == "ps"]
        if px:
            reads = [r for r in reads if not (isinstance(r, tuple) and r[0] == "ps")]
            writes = [r for r in writes if not (isinstance(r, tuple) and r[0] == "ps")]
            for r in px:
                prev = self.excl.get(r)
                if prev is not None and prev.eng != eng:
                    deps[id(prev)] = prev
                self.excl[r] = o
        for r in reads:
            w = self.last_w.get(r)
            if w is not None:
                deps[id(w)] = w
            nm = r[0] if isinstance(r, tuple) else r
            rg = self.reg_of.get(nm)
            if rg is not None:
                regs.add(rg)
        for r in writes:
            w = self.last_w.get(r)
            if w is not None:
                deps[id(w)] = w
            rd = self.readers.get(r)
            if rd:
                for x in rd.values():
                    deps[id(x)] = x
            nm = r[0] if isinstance(r, tuple) else r
            rg = self.reg_of.get(nm)
            if rg is not None:
                regs.add(rg)
        for rg in regs:
            f = self.reg_fence.get(rg)
            if f:
                for x in f.values():
                    deps[id(x)] = x
            self.reg_cur.setdefault(rg, {})[self._key(o)] = o
        o.deps = tuple(deps.values())
        k = self._key(o)
        for r in reads:
            self.readers.setdefault(r, {})[k] = o
        for r in writes:
            self.last_w[r] = o
            self.readers[r] = {}
        self.ops[eng].append(o)
        return o

    def emit(self, final_wait_ops=()):
        nc = self.nc
        for o in final_wait_ops:
            if not o.is_dma:
                o.needs_inc = True
        for e in ENGS:
            for o in self.ops[e]:
                for d in o.deps:
                    if d.is_dma:
                        continue
                    if d.eng == o.eng and d.eng == "pe":
                        continue
                    d.needs_inc = True
        for e in ENGS:
            c = 0
            for o in self.ops[e]:
                if o.is_dma:
                    self.dma_cnt[o.dsem] = self.dma_cnt.get(o.dsem, 0) + 16
                    o.semval = self.dma_cnt[o.dsem]
                elif o.needs_inc:
                    c += 1
                    o.semval = c
        with contextlib.ExitStack() as st:
            esem = {e: st.enter_context(nc.semaphore("prog_" + e)) for e in ENGS}
            dsems = {}
            for i, k in enumerate(sorted(self.dma_cnt.keys(), key=str)):
                dsems[k] = st.enter_context(nc.semaphore("dma%d" % i))
            block = st.enter_context(nc.Block())

            def run(ename, eng):
                seen = {}
                for o in self.ops[ename]:
                    need = {}
                    for d in o.deps:
                        if d.is_dma:
                            s = ("d", d.dsem)
                        else:
                            if d.eng == ename and ename == "pe":
                                continue
                            s = ("e", d.eng)
                        if need.get(s, 0) < d.semval:
                            need[s] = d.semval
                    for s, v in need.items():
                        if seen.get(s, 0) >= v:
                            continue
                        seen[s] = v
                        eng.wait_ge(dsems[s[1]] if s[0] == "d" else esem[s[1]], v)
                        if self.trace is not None:
                            self.trace.append((ename, "wait", s, v))
                    ins = o.fn(eng)
                    if self.trace is not None:
                        self.trace.append((ename, "op", getattr(o, "tag", None), (o.dsem if o.is_dma else ("inc" if o.needs_inc else None)), o.semval))
                    if o.is_dma:
                        ins.then_inc(dsems[o.dsem], 16)
                    elif o.needs_inc:
                        ins.then_inc(esem[ename], 1)
                if ename == "sp":
                    for o in final_wait_ops:
                        eng.wait_ge(dsems[o.dsem] if o.is_dma else esem[o.eng], o.semval)

            @block.tensor
            def _(e):
                run("pe", e)

            @block.scalar
            def _(e):
                run("act", e)

            @block.vector
            def _(e):
                run("dve", e)

            @block.gpsimd
            def _(e):
                run("pool", e)

            @block.sync
            def _(e):
                run("sp", e)


def MM(out, lhsT, rhs, start, stop):
    return lambda e: e.matmul(out, lhsT=lhsT, rhs=rhs, start=start, stop=stop)


def TR(out, in_, ident):
    return lambda e: e.transpose(out=out, in_=in_, identity=ident)


def ACTF(out, in_, func, bias=None, scale=None):
    kw = {}
    if bias is not None:
        kw["bias"] = bias
    if scale is not None:
        kw["scale"] = scale
    return lambda e: e.activation(out=out, in_=in_, func=func, **kw)


def STT(out, in0, scalar, in1, op0, op1):
    return lambda e: e.scalar_tensor_tensor(out=out, in0=in0, scalar=scalar, in1=in1, op0=op0, op1=op1)


def TT(out, in0, in1, op):
    return lambda e: e.tensor_tensor(out=out, in0=in0, in1=in1, op=op)


def TS(out, in0, s1, s2, op0, op1=None):
    if op1 is None:
        return lambda e: e.tensor_scalar(out=out, in0=in0, scalar1=s1, scalar2=None, op0=op0)
    return lambda e: e.tensor_scalar(out=out, in0=in0, scalar1=s1, scalar2=s2, op0=op0, op1=op1)


def CP(out, in_):
    return lambda e: e.tensor_copy(out=out, in_=in_)


def ACP(out, in_):
    return lambda e: e.copy(out=out, in_=in_)


def DMA(out, in_):
    return lambda e: e.dma_start(out=out, in_=in_)


ST_I, ST_S, ST_E, ST_NS, ST_NE, ST_NV = 0, 1, 2, 3, 4, 5


def key_status_own(kap, rho):
    d = kap - rho
    if kap == -6:
        return ST_NV
    if 0 <= kap <= 7:
        if -4 <= d <= 3:
            return ST_I
        return ST_S if d >= 4 else ST_E
    if kap < 0:
        return ST_NS if d >= -4 else ST_NV
    return ST_NE if d <= 3 else ST_NV


def pair_plan(j):
    k0 = 2 * j - 6
    per = []
    for rho in range(8):
        st, sb = key_status_own(k0, rho), key_status_own(k0 + 1, rho)
        per.append(None if (st == ST_NV and sb == ST_NV) else st * 6 + sb)
    rows = [r for r in range(8) if per[r] is not None]
    if not rows:
        return None
    ra, rb = rows[0], rows[-1]
    segs = []
    r = ra
    while r <= rb:
        assert per[r] is not None
        r2 = r
        while r2 + 1 <= rb and per[r2 + 1] == per[r]:
            r2 += 1
        segs.append((r, r2, per[r]))
        r = r2 + 1
    return ra, rb, segs


def halo_items():
    items = []
    for j in range(0, 5):
        items.append(("preI", j))
    for j in range(3, 7):
        items.append(("preS", j))
    for j in range(5, 9):
        items.append(("post", j))
    items.append(("metapre", None))
    items.append(("metapost", None))
    assert len(items) == NHALO
    return items


def halo_status(kind, kap):
    if kind == "preI":
        return ST_NS if -5 <= kap <= 2 else ST_NV
    if kind == "preS":
        return ST_S if 0 <= kap <= 7 else ST_NV
    if kind == "post":
        if 4 <= kap <= 7:
            return ST_I
        return ST_NE if 8 <= kap <= 11 else ST_NV
    raise ValueError


def halo_geom(kind, j):
    k0 = 2 * j - 6
    if kind == "preI":
        return -1 - k0, 63
    if kind == "preS":
        return 0 - k0, 0
    return 8 - k0, 0


def build_program(NU=NU, stop_after=None):
    nc = bass.Bass("TRN2", target_bir_lowering=False)
    dt_in = lambda n, s: nc.dram_tensor(n, s, F32, kind="ExternalInput").ap()
    xu = dt_in("xu", [NU, TOK, D])
    w_in = dt_in("w_in", [D, 4096])
    w_out = dt_in("w_out", [D, D])
    w_up = dt_in("w_up", [D, 2 * DFF])
    w_down = dt_in("w_down", [DFF, D])
    w_pool = dt_in("w_pool", [4, 256, 256])
    gp_d = dt_in("gp", [8, 128, 2048])
    cst_d = dt_in("cst", [128, NCONST])
    flg_d = dt_in("flg", [NU, 128, NFLAG])
    y_d = nc.dram_tensor("y", [NU, 512, D], F32, kind="ExternalOutput").ap()
    dbg_d = None
    if stop_after is not None:
        dbg_d = {"A": nc.dram_tensor("dbgA", [128, 9344], F32, kind="ExternalOutput").ap(),
                 "H": nc.dram_tensor("dbgH", [128, DC * NM], F32, kind="ExternalOutput").ap(),
                 "B": nc.dram_tensor("dbgB", [128, 16536], F32, kind="ExternalOutput").ap(),
                 "Y": nc.dram_tensor("dbgY", [128, 6168], F32, kind="ExternalOutput").ap()}
    wb_in = nc.dram_tensor("wb_in", [16, 128, 4096], BF, kind="Internal").ap()
    wb_out = nc.dram_tensor("wb_out", [8, 128, 4096], BF, kind="Internal").ap()
    wb_up = nc.dram_tensor("wb_up", [NFC, 128, 4096], BF, kind="Internal").ap()
    wb_dn = nc.dram_tensor("wb_dn", [32, 128, 22 * 128], BF, kind="Internal").ap()

    with contextlib.ExitStack() as st:
        def sb(n, words):
            return st.enter_context(nc.sbuf_tensor(n, [128, words], F32))

        RA = sb("RA", 9344)
        RH = sb("RH", DC * NM)
        RB = sb("RB", 16536)
        RY = sb("RY", 6168)
        RS = sb("RS", NSLAB * 2048)
        cst = sb("cst_sb", NCONST)
        flg = sb("flg_sb", NFLAG)
        ident = sb("ident", 128)
        onesw = sb("onesw", 64)
        wpw = sb("wpw", 1024)
        small = sb("small", 96)
        ps = [st.enter_context(nc.psum_tensor("ps%d" % i, [128, 512], F32)) for i in range(8)]

        def vbf(t, off_w, shape):
            n = int(np.prod(shape))
            a = t.bitcast(BF)[:, 2 * off_w: 2 * off_w + n]
            if len(shape) == 1:
                return a
            names = "abcd"[:len(shape)]
            pat = "p (%s) -> p %s" % (" ".join(names), " ".join(names))
            return a.rearrange(pat, **{names[i]: shape[i] for i in range(len(shape) - 1)})

        def vf(t, off_w, shape):
            n = int(np.prod(shape))
            a = t[:, off_w: off_w + n]
            if len(shape) == 1:
                return a
            names = "abcd"[:len(shape)]
            pat = "p (%s) -> p %s" % (" ".join(names), " ".join(names))
            return a.rearrange(pat, **{names[i]: shape[i] for i in range(len(shape) - 1)})

        h0bf = vbf(RA, 0, [DC, TOK])
        gpb = [vf(RA, 0, [2, 1024]), vf(RA, 2048, [2, 1024])]
        sbt = [vf(RA, 4096, [512]), vf(RA, 4608, [512]), vf(RA, 5120, [512])]
        ptb = [vbf(RA, 5632, [512]), vbf(RA, 5888, [512]), vbf(RA, 6144, [512]), vbf(RA, 6400, [512]),
               vbf(RA, 8100, [512]), vbf(RA, 8356, [512]), vbf(RA, 8612, [512]), vbf(RA, 8868, [512])]
        sbh = vf(RA, 6656, [240])
        pth = vbf(RA, 6896, [240])
        rech = vf(RA, 7016, [16])
        pt1 = vf(RA, 7040, [NUC])
        pt2 = vf(RA, 7040 + NUC, [NUC])
        h1bf = vbf(RA, 0, [DC, NM])
        lxb = [vbf(RA, 4112 + 257 * i, [NM]) for i in range(3)]
        lxq = [vbf(RA, 4112 + 771 + 257 * i, [NM]) for i in range(3)]
        lmean = vf(RA, 5654, [NM])
        lmsq = vf(RA, 5654 + NM, [NM])
        lrstd = vf(RA, 5654 + 2 * NM, [NM])
        ltmp = [vf(RA, 7196 + NM * i, [NM]) for i in range(4)]
        h0f = vf(RH, 0, [DC, NM])
        KT = vbf(RB, 0, [8, 1280])
        Vt = vbf(RB, 5120, [NTILE, 1024])
        QT = vbf(RB, 10240, [8, NM])
        Ut = vf(RB, 12296, [8, NUC])
        actb = vbf(RB, 10240, [22, 512])
        xt = [vf(RY, 0, [D]), vf(RY, 2048, [D]), vf(RY, 4096, [D])]
        mT = vbf(RY, 0, [8, NM])
        ypool = vbf(RY, 2056, [8, NM])
        yattn = vbf(RY, 4112, [8, NM])
        zb = [vf(RY, 0, [2, NM]), vf(RY, 1028, [2, NM])]
        cab = [vf(RY, 2056, [512]), vf(RY, 2568, [512])]
        cgb = [vf(RY, 3080, [512]), vf(RY, 3592, [512])]
        ggb = [vf(RY, 4104, [512]), vf(RY, 4616, [512])]
        otile = [vf(RY, 0, [D]), vf(RY, 2048, [D])]
        slab = [vbf(RS, 2048 * i, [4096]) for i in range(NSLAB)]
        onesb = vbf(onesw, 0, [128])
        wp = vbf(wpw, 0, [4, 2, 256])
        stt_l = [vf(small, 32 * i, [4, 6]) for i in range(3)]
        mv_l = [vf(small, 32 * i + 24, [2]) for i in range(3)]
        rstd_l = [vf(small, 32 * i + 26, [1]) for i in range(3)]
        nmr_l = [vf(small, 32 * i + 27, [1]) for i in range(3)]

        P = Prog(nc)
        nc._prog = P
        if "trace" in os.environ.get("KDBG", ""):
            P.trace = []
            nc._ptrace = P.trace
        P.region("A", ["h0bf", "gpb", "sbt", "ptb", "sbh", "pth", "rech", "pt1", "pt2", "h1bf", "lxb", "lxq",
                       "lstat", "ltmp"])
        P.region("B", ["KT", "V", "QT", "U", "act"])
        P.region("Y", ["xt", "m", "ypool", "yattn", "z", "ca", "cg", "gg", "otile"])

        bank_ctr = [0]

        reserved = set()

        def nb():
            while True:
                b = bank_ctr[0] % 8
                bank_ctr[0] += 1
                if b not in reserved:
                    return b

        cc = lambda col: cst[:, col:col + 1]

        P.op("sp", DMA(cst[:], cst_d[:]), writes=["cst"], dsem="cst")
        P.op("dve", lambda e: e.memset(ident[:], 0.0), writes=["ident"])
        P.op("pool", lambda e: e.affine_select(out=ident[:], in_=ident[:], pattern=[[-1, 128]],
                                               compare_op=ALU.not_equal, fill=1.0, base=0, channel_multiplier=1),
             reads=["ident"], writes=["ident"])
        P.op("dve", lambda e: e.memset(onesb, 1.0), writes=["ones"])
        P.op("dve", lambda e: e.memset(RB[:, :], 0.0), writes=[("KT", c) for c in range(8)] + [("V", 9, q) for q in range(4)])

        cv_n = [0]


        dbgflags = os.environ.get("KDBG", "")

        def conv_dma(out, in_, res):
            if "noconv" in dbgflags:
                return
            if "only_" in dbgflags and ("only_" + res[0]) not in dbgflags:
                return
            n = cv_n[0]
            cv_n[0] += 1
            P.op("pool", DMA(out, in_), writes=[res, ("cvslot", n % 5)], dsem=("cv", n % 5))

        w_in_v = w_in.rearrange("(kc p) n -> p kc n", p=128)
        w_out_v = w_out.rearrange("(kc p) n -> p kc n", p=128)
        w_up_v = w_up.rearrange("(kc p) n -> p kc n", p=128)
        w_dn_v = w_down.rearrange("(kc p) n -> p kc n", p=128)
        for s in range(16):
            conv_dma(wb_in[s].rearrange("p (kc n) -> p kc n", kc=16), w_in_v[:, :, 256 * s:256 * s + 256], ("wb_in", s))
        for g in range(4):
            if "nowp" in dbgflags:
                continue
            P.op("pool", DMA(wp[:, g], w_pool[g].rearrange("(kc p) e -> p kc e", p=128)), writes=[("wp", g)],
                 dsem=("wpl", g))
        for s in range(8):
            conv_dma(wb_out[s].rearrange("p (kc n) -> p kc n", kc=16), w_out_v[:, :, 256 * s:256 * s + 256], ("wb_out", s))
        def conv_up(j):
            ov = wb_up[j].rearrange("p (kc n) -> p kc n", kc=16)
            conv_dma(ov[:, :, 0:128], w_up_v[:, :, 128 * j:128 * j + 128], ("wb_up", j, 0))
            conv_dma(ov[:, :, 128:256], w_up_v[:, :, DFF + 128 * j:DFF + 128 * j + 128], ("wb_up", j, 1))

        def conv_dn(oc, part):
            k0, nk = (0, 22) if part == 0 else (22, 21)
            ov = wb_dn[2 * oc + part].rearrange("p (kc n) -> p kc n", kc=22)
            conv_dma(ov[:, 0:nk, :], w_dn_v[:, k0:k0 + nk, 128 * oc:128 * oc + 128], ("wb_dn", 2 * oc + part))

        for part in range(2):
            for j in (range(0, 22) if part == 0 else range(22, NFC)):
                conv_up(j)
            for oc in range(16):
                conv_dn(oc, part)

        slab_list = []
        for u in range(NU):
            for s in range(16):
                slab_list.append((wb_in[s], 4096, [("wb_in", s)]))
            for s in range(8):
                slab_list.append((wb_out[s], 4096, [("wb_out", s)]))
            for part in range(2):
                for j in (range(0, 22) if part == 0 else range(22, NFC)):
                    slab_list.append((wb_up[j], 4096, [("wb_up", j, 0), ("wb_up", j, 1)]))
                nk = 22 if part == 0 else 21
                for oc in range(DC):
                    slab_list.append((wb_dn[2 * oc + part][:, 0:nk * 128], nk * 128, [("wb_dn", 2 * oc + part)]))
        slab_issued = [0]
        slab_cur = [0]

        def prefetch_slabs():
            i = slab_cur[0]
            while slab_issued[0] < min(len(slab_list), i + NSLAB):
                n = slab_issued[0]
                src, ncol, res = slab_list[n]
                P.op("sp", DMA(slab[n % NSLAB][:, 0:ncol], src), reads=res, writes=[("slab", n % NSLAB)],
                     dsem=("slab", n % NSLAB))
                slab_issued[0] += 1

        def next_slab(ahead=NSLAB):
            i = slab_cur[0]
            slab_cur[0] += 1
            while slab_issued[0] < min(len(slab_list), i + ahead):
                n = slab_issued[0]
                src, ncol, res = slab_list[n]
                P.op("sp", DMA(slab[n % NSLAB][:, 0:ncol], src), reads=res, writes=[("slab", n % NSLAB)],
                     dsem=("slab", n % NSLAB))
                slab_issued[0] += 1
            return i % NSLAB, ("slab", i % NSLAB)

        def ln_begin(pieces):
            banks = [(nb(), nb()) for _ in pieces]
            for b1, b2 in banks:
                reserved.add(b1)
                reserved.add(b2)
            return {"pieces": pieces, "banks": banks}

        def ln_chunk(stt, c):
            r = c % 3
            P.op("act", ACP(lxb[r][:, 0:NM], h0f[:, c, :]), reads=[("h0f", c)], writes=[("lxb", r)])
            P.op("act", ACTF(lxq[r][:, 0:NM], h0f[:, c, :], AF.Square), reads=[("h0f", c)], writes=[("lxq", r)])
            for pi, (c0, n) in enumerate(stt["pieces"]):
                b1, b2 = stt["banks"][pi]
                P.op("pe", MM(ps[b1][:, 0:n], onesb, lxb[r][:, c0:c0 + n], c == 0, c == DC - 1),
                     reads=[("lxb", r), "ones"], writes=[("ps", b1)])
                P.op("pe", MM(ps[b2][:, 0:n], onesb, lxq[r][:, c0:c0 + n], c == 0, c == DC - 1),
                     reads=[("lxq", r), "ones"], writes=[("ps", b2)])

        def ln_finish(stt, gcol, bcol, out_bf):
            pieces, banks = stt["pieces"], stt["banks"]
            for pi, (c0, n) in enumerate(pieces):
                b1, b2 = banks[pi]
                sl = slice(c0, c0 + n)
                P.op("dve", TS(lmean[:, sl], ps[b1][:, 0:n], 1.0 / D, None, ALU.mult), reads=[("ps", b1)],
                     writes=[("lstat", "mean", pi)])
                P.op("dve", TT(lmsq[:, sl], lmean[:, sl], lmean[:, sl], ALU.mult), reads=[("lstat", "mean", pi)],
                     writes=[("lstat", "msq", pi)])
                P.op("dve", STT(lrstd[:, sl], ps[b2][:, 0:n], 1.0 / D, lmsq[:, sl], ALU.mult, ALU.subtract),
                     reads=[("ps", b2), ("lstat", "msq", pi)], writes=[("lstat", "rstd", pi)])
                P.op("act", ACTF(lrstd[:, sl], lrstd[:, sl], AF.Sqrt, bias=LN_EPS),
                     reads=[("lstat", "rstd", pi)], writes=[("lstat", "rstd", pi)])
                P.op("dve", lambda e, sl=sl: e.reciprocal(out=lrstd[:, sl], in_=lrstd[:, sl]),
                     reads=[("lstat", "rstd", pi)], writes=[("lstat", "rstd", pi)])
                reserved.discard(b1)
                reserved.discard(b2)
            lo = pieces[0][0]
            hi = pieces[-1][0] + pieces[-1][1]
            sl = slice(lo, hi)
            srd = [("lstat", "mean", pi) for pi in range(len(pieces))] + [("lstat", "rstd", pi) for pi in range(len(pieces))]
            for c in range(DC):
                eng_ = "dve"
                r = c % 4
                P.op(eng_, TT(ltmp[r][:, sl], h0f[:, c, sl], lmean[:, sl], ALU.subtract),
                     reads=[("h0f", c)] + srd, writes=[("ltmp", r)])
                P.op(eng_, TT(ltmp[r][:, sl], ltmp[r][:, sl], lrstd[:, sl], ALU.mult),
                     reads=[("ltmp", r)] + srd, writes=[("ltmp", r)])
                P.op("act", ACTF(h0f[:, c, sl], ltmp[r][:, sl], AF.Identity, bias=cc(bcol + c), scale=cc(gcol + c)),
                     reads=[("ltmp", r), "cst"], writes=[("h0f", c)])
                if out_bf:
                    P.op("act", ACTF(h1bf[:, c, sl], ltmp[r][:, sl], AF.Identity, bias=cc(bcol + c), scale=cc(gcol + c)),
                         reads=[("ltmp", r), "cst"], writes=[("h1bf", c)])

        items_h = halo_items()
        plans = [pair_plan(j) for j in range(9)]
        out_ops = []

        class _Stop(Exception):
            pass

        def check_stop(name):
            if stop_after != name:
                return
            lasts = [P.ops[e][-1] for e in ("pe", "act", "dve") if P.ops[e]]
            for nm, t in (("A", RA), ("H", RH), ("B", RB), ("Y", RY)):
                o = P.op("sp", DMA(dbg_d[nm][:, :], t[:, :]), dsem=("dbg", nm), extra_deps=lasts)
                out_ops.append(o)
            raise _Stop()

        CONT = (1, 3, 4, 5)
        for u in range(NU):
          try:
            check_stop("setup")
            fb = 0
            fcol = lambda col: flg[:, fb + col: fb + col + 1]
            P.op("sp", DMA(flg[:, fb:fb + NFLAG], flg_d[u]), writes=[("flg", 0)], dsem=("flg", 0))
            FL = ("flg", 0)

            P.cur_tag = "P1a"
            def stage_a(t):
                    p = 128 if t < 9 else 16
                    xb_ = t % 3
                    stt_t, mv, rstd_s, nmr_s = stt_l[xb_], mv_l[xb_], rstd_l[xb_], nmr_l[xb_]
                    tok0 = t * 128
                    P.op("sp", DMA(xt[xb_][0:p, :], xu[u, tok0:tok0 + p, :]), writes=[("xt", xb_)], dsem=("x", xb_))
                    for q in range(4):
                        P.op("dve", lambda e, q=q, p=p, xb_=xb_, stt_t=stt_t: e.bn_stats(out=stt_t[0:p, q, :], in_=xt[xb_][0:p, 512 * q:512 * q + 512]),
                             reads=[("xt", xb_)], writes=[("bnst", xb_, q)])
                    P.op("dve", lambda e, p=p, stt_t=stt_t, mv=mv: e.bn_aggr(out=mv[0:p, :], in_=stt_t[0:p].rearrange("p a b -> p (a b)")),
                         reads=[("bnst", xb_, q) for q in range(4)], writes=[("mv", xb_)])
                    P.op("act", ACTF(rstd_s[0:p, :], mv[0:p, 1:2], AF.Sqrt, bias=LN_EPS), reads=[("mv", xb_)], writes=[("rstd_s", xb_)])
                    P.op("dve", lambda e, p=p, rstd_s=rstd_s: e.reciprocal(out=rstd_s[0:p, :], in_=rstd_s[0:p, :]),
                         reads=[("rstd_s", xb_)], writes=[("rstd_s", xb_)])
                    P.op("dve", TS(nmr_s[0:p, :], mv[0:p, 0:1], rstd_s[0:p, 0:1], -1.0, ALU.mult, ALU.mult),
                         reads=[("mv", xb_), ("rstd_s", xb_)], writes=[("nmr_s", xb_)])
                    P.op("act", ACTF(xt[xb_][0:p, :], xt[xb_][0:p, :], AF.Identity, bias=nmr_s[0:p, 0:1], scale=rstd_s[0:p, 0:1]),
                         reads=[("xt", xb_), ("rstd_s", xb_), ("nmr_s", xb_)], writes=[("xt", xb_)])

            def stage_b(t):
                    p = 128 if t < 9 else 16
                    xb_ = t % 3
                    tok0 = t * 128
                    ja = max(0, tok0 - TAU_M)
                    jb = min(NM, tok0 + p - TAU_M)
                    for g4 in range(4):
                        b = nb()
                        eng_ = "act" if (g4 + t) % 2 == 0 else "dve"
                        for k in range(4):
                            c = 4 * g4 + k
                            P.op("pe", TR(ps[b][:, k * 128:k * 128 + p], xt[xb_][0:p, c * 128:(c + 1) * 128], ident[0:p, 0:p]),
                                 reads=[("xt", xb_), "ident"], writes=[("ps", b)])
                        for k in range(4):
                            c = 4 * g4 + k
                            src = ps[b][:, k * 128:k * 128 + p]
                            if eng_ == "act":
                                P.op("act", ACTF(h0bf[:, c, tok0:tok0 + p], src, AF.Identity,
                                                 bias=cc(C_LNIN_B + c), scale=cc(C_LNIN_G + c)),
                                     reads=[("ps", b), "cst"], writes=[("h0bf", c, t)])
                            else:
                                P.op("dve", TS(h0bf[:, c, tok0:tok0 + p], src, cc(C_LNIN_G + c), cc(C_LNIN_B + c), ALU.mult, ALU.add),
                                     reads=[("ps", b), "cst"], writes=[("h0bf", c, t)])
                            if jb > ja:
                                o0 = TAU_M + ja - tok0
                                src2 = ps[b][:, k * 128 + o0:k * 128 + o0 + (jb - ja)]
                                if eng_ == "act":
                                    P.op("act", ACTF(h0f[:, c, ja:jb], src2, AF.Identity, bias=cc(C_LNIN_B + c), scale=cc(C_LNIN_G + c)),
                                         reads=[("ps", b), "cst"], writes=[("h0f", c)])
                                else:
                                    P.op("dve", TS(h0f[:, c, ja:jb], src2, cc(C_LNIN_G + c), cc(C_LNIN_B + c), ALU.mult, ALU.add),
                                         reads=[("ps", b), "cst"], writes=[("h0f", c)])

            cont = u in CONT
            tiles = list(range(2, 9)) if cont else list(range(NTILE))
            for i_ in range(len(tiles) + 1):
                if i_ < len(tiles):
                    stage_a(tiles[i_])
                if i_ >= 1:
                    stage_b(tiles[i_ - 1])
            P.fence("Y")
            h0all = [("h0bf", c, t) for c in range(DC) for t in tiles]
            check_stop("P1a")

            P.cur_tag = "P1b"
            ev = [0]

            def evac(out, in_, reads, writes):
                ev[0] += 1
                if ev[0] % 2:
                    P.op("act", ACP(out, in_), reads=reads, writes=writes)
                else:
                    P.op("dve", CP(out, in_), reads=reads, writes=writes)

            if cont:
                for tt in range(5):
                    P.op("pool", CP(KT[:, :, 128 * tt:128 * tt + 128], KT[:, :, 128 * (tt + 4):128 * (tt + 4) + 128]),
                         reads=[("KT", c) for c in range(8)], writes=[("KT", c) for c in range(8)])
                    P.op("pool", CP(Vt[:, tt, :], Vt[:, tt + 4, :]), reads=[("V", tt + 4, q) for q in range(4)],
                         writes=[("V", tt, q) for q in range(4)])
            for s in range(16):
                sl_, sres = next_slab()
                sv = slab[sl_].rearrange("p (kc n) -> p kc n", kc=16)
                kind = s // 4
                for e_ in range(2):
                    oc = 2 * (s % 4) + e_
                    if kind == 0:
                        pcs = [(TAU_U, 0, 265), (TAU_U + 265, 265, 265)]
                    elif kind == 1:
                        pcs = [(TAU_M, 0, 257), (TAU_M + 257, 257, 257)]
                    elif kind == 2:
                        pcs = [(640, 640, 512)] if cont else [(0, 0, 512), (512, 512, 512), (1024, 1024, 144)]
                    else:
                        pcs = []
                    for (tau0, o0, n) in pcs:
                        b = nb()
                        for kc in range(DC):
                            P.op("pe", MM(ps[b][:, 0:n], sv[:, kc, e_ * 128:(e_ + 1) * 128], h0bf[:, kc, tau0:tau0 + n],
                                          kc == 0, kc == DC - 1),
                                 reads=[sres] + (h0all if kc == 0 else []), writes=[("ps", b)])
                        if kind == 0:
                            evac(Ut[:, oc, o0:o0 + n], ps[b][:, 0:n], [("ps", b)], [("U", oc)])
                        elif kind == 1:
                            evac(QT[:, oc, o0:o0 + n], ps[b][:, 0:n], [("ps", b)], [("QT", oc)])
                        else:
                            evac(KT[:, oc, o0:o0 + n], ps[b][:, 0:n], [("ps", b)], [("KT", oc)])
                if kind == 3:
                    f0 = 256 * (s % 4)
                    for t2 in ((5, 7) if cont else range(0, NTILE, 2)):
                        b = nb()
                        for tt in (t2, t2 + 1):
                            p = 128 if tt < 9 else 16
                            off = 256 * (tt - t2)
                            for kc in range(DC):
                                P.op("pe", MM(ps[b][0:p, off:off + 256], h0bf[:, kc, tt * 128:tt * 128 + p], sv[:, kc, :],
                                              kc == 0, kc == DC - 1),
                                     reads=[sres] + (h0all if kc == 0 else []), writes=[("ps", b)])
                        evac(Vt[:, t2, f0:f0 + 256], ps[b][:, 0:256], [("ps", b)], [("V", t2, s % 4)])
                        p = 128 if t2 + 1 < 9 else 16
                        evac(Vt[0:p, t2 + 1, f0:f0 + 256], ps[b][0:p, 256:512], [("ps", b)], [("V", t2 + 1, s % 4)])
            check_stop("P1b")
            P.fence("A")
            Vall = [("V", t, q) for t in range(NTILE) for q in range(4)]

            P.cur_tag = "P2a"
            P.op("dve", TS(Ut[:, :, 521:530], Ut[:, :, 521:530], fcol(F_FPOST), None, ALU.mult),
                 reads=[("U", c) for c in range(8)] + [FL], writes=[("U", c) for c in range(8)])
            for g in range(4):
                w = 2 ** (g + 1)
                for kc2 in range(2):
                    c = 2 * g + kc2
                    U = Ut[:, c, :]
                    P.op("dve", TT(pt1[:, 1:530], U[:, 0:529], U[:, 1:530], ALU.add), reads=[("U", c)], writes=["pt1"])
                    cur, curname = pt1, "pt1"
                    if g >= 1:
                        P.op("dve", TT(pt2[:, 2:529], pt1[:, 1:528], pt1[:, 3:530], ALU.add), reads=["pt1"], writes=["pt2"])
                        cur, curname = pt2, "pt2"
                    if g >= 2:
                        P.op("dve", TT(pt1[:, 4:527], pt2[:, 2:525], pt2[:, 6:529], ALU.add), reads=["pt2"], writes=["pt1"])
                        cur, curname = pt1, "pt1"
                    if g >= 3:
                        P.op("dve", TT(pt2[:, 8:523], pt1[:, 4:519], pt1[:, 12:527], ALU.add), reads=["pt1"], writes=["pt2"])
                        cur, curname = pt2, "pt2"
                    P.op("dve", TT(cur[:, 513:521], cur[:, 513:521], flg[:, fb + F_CORR + 8 * g: fb + F_CORR + 8 * g + 8], ALU.mult),
                         reads=[curname, FL], writes=[curname])
                    P.op("dve", STT(mT[:, c, :], cur[:, 8:8 + NM], 1.0 / w, U[:, 8:8 + NM], ALU.mult, ALU.subtract),
                         reads=[curname, ("U", c)], writes=[("m", c)])
                for e_ in range(2):
                    oc = 2 * g + e_
                    for half in range(2):
                        b = nb()
                        for kc2 in range(2):
                            P.op("pe", MM(ps[b][:, 0:257], wp[:, g, kc2, e_ * 128:(e_ + 1) * 128],
                                          mT[:, 2 * g + kc2, 257 * half:257 * half + 257], kc2 == 0, kc2 == 1),
                                 reads=[("m", 2 * g + kc2), ("wp", g)], writes=[("ps", b)])
                        P.op("act", ACTF(ypool[:, oc, 257 * half:257 * half + 257], ps[b][:, 0:257], AF.Identity,
                                         scale=cc(C_PSCALE + oc)),
                             reads=[("ps", b), "cst"], writes=[("ypool", oc)])

            check_stop("P2a")
            prefetch_slabs()
            P.cur_tag = "P2b"
            first = True
            for h in range(16):
                hp, pi = h // 2, h % 2
                pr = slice(64 * pi, 64 * pi + 64)
                for it, (kind, j) in enumerate(items_h):
                    col = h * NHALO + it
                    qcol = NM - 1 if kind in ("post", "metapost") else 0
                    if j is None:
                        lhs = KT[pr, hp, 1152:1280]
                        out = ps[7][:, col:col + 1]
                    else:
                        lhs = KT[pr, hp, 128 * j:128 * j + 128]
                        out = ps[7][:, col:col + 1]
                    P.op("pe", MM(out, lhs, QT[pr, hp, qcol:qcol + 1], True, True),
                         reads=[("KT", hp), ("QT", hp)], writes=[("ps", 7)])
            P.op("dve", STT(sbh[:, :], ps[7][:, 0:240], 0.125, cst[:, C_GH:C_GH + 240], ALU.mult, ALU.add),
                 reads=[("ps", 7), "cst"], writes=["sbh"])
            P.op("dve", TT(sbh[:, :], sbh[:, :], flg[:, fb + F_FH: fb + F_FH + 240], ALU.add), reads=["sbh", FL], writes=["sbh"])
            P.op("act", ACTF(pth[:, :], sbh[:, :], AF.Exp), reads=["sbh"], writes=["pth"])

            def load_gp(hp_):
                P.op("sp", DMA(gpb[hp_ % 2].rearrange("p a b -> p (a b)"), gp_d[hp_]), writes=[("gpb", hp_ % 2)],
                     dsem=("gp", hp_ % 2))

            load_gp(0)
            steps = [None] + [j for j in range(9) if plans[j] is not None]
            nst = len(steps)
            sbank = [0, 1, 2]

            def emit_S(hp, i, pi):
                j = steps[i]
                n = 2 * (hp * nst + i) + pi
                h = 2 * hp + pi
                pr = slice(64 * pi, 64 * pi + 64)
                b = sbank[n % 3]
                if j is None:
                    P.op("pe", MM(ps[b][:, 0:512], KT[pr, hp, 1152:1280], QT[pr, hp, 1:513], True, True),
                         reads=[("KT", hp), ("QT", hp)], writes=[("ps", b)])
                    P.op("act", ACTF(ptb[n % 8][:, :], ps[b][:, 0:512], AF.Exp, bias=cc(C_METAB + h), scale=0.125),
                         reads=[("ps", b), "cst"], writes=[("ptb", n % 8)])
                else:
                    ra, rb, segs = plans[j]
                    qa, qb = 64 * ra, 64 * rb + 64
                    n_ = qb - qa
                    k0 = 2 * j - 6
                    gofs = (ra - k0 + 7) * 64
                    P.op("pe", MM(ps[b][:, 0:n_], KT[pr, hp, 128 * j:128 * j + 128], QT[pr, hp, 1 + qa:1 + qb], True, True),
                         reads=[("KT", hp), ("QT", hp)], writes=[("ps", b)])
                    P.op("dve", STT(sbt[n % 3][:, 0:n_], ps[b][:, 0:n_], 0.125, gpb[hp % 2][:, pi, gofs:gofs + n_],
                                    ALU.mult, ALU.add),
                         reads=[("ps", b), ("gpb", hp % 2)], writes=[("sbt", n % 3)])
                    for (r0, r1, combo) in segs:
                        a0, a1 = 64 * r0 - qa, 64 * r1 + 64 - qa
                        P.op("act", ACTF(ptb[n % 8][:, a0:a1], sbt[n % 3][:, a0:a1], AF.Exp, bias=fcol(F_BTAB + combo)),
                             reads=[("sbt", n % 3), FL], writes=[("ptb", n % 8)])

            def emit_PV(hp, i, pi):
                j = steps[i]
                n = 2 * (hp * nst + i) + pi
                h = 2 * hp + pi
                bV, bO = 3 + hp % 2, 5 + hp % 2
                pr = slice(64 * pi, 64 * pi + 64)
                last = i == nst - 1
                if j is None:
                    P.op("pe", MM(ps[bV][pr, 0:512], Vt[:, 9, 64 * h:64 * h + 64], ptb[n % 8][:, :], True, False),
                         reads=[("ptb", n % 8)] + Vall, writes=[("ps", bV)])
                    P.op("pe", MM(ps[bO][pr, 0:512], onesb[:, 0:64], ptb[n % 8][:, :], True, False),
                         reads=[("ptb", n % 8), "ones"], writes=[("ps", bO)])
                else:
                    ra, rb, segs = plans[j]
                    qa, qb = 64 * ra, 64 * rb + 64
                    n_ = qb - qa
                    P.op("pe", MM(ps[bV][pr, qa:qb], Vt[:, j, 64 * h:64 * h + 64], ptb[n % 8][:, 0:n_], False, last),
                         reads=[("ptb", n % 8)], writes=[("ps", bV)])
                    P.op("pe", MM(ps[bO][pr, qa:qb], onesb[:, 0:64], ptb[n % 8][:, 0:n_], False, last),
                         reads=[("ptb", n % 8), "ones"], writes=[("ps", bO)])

            pending = []

            def emit_norm(hp):
                bV, bO = 3 + hp % 2, 5 + hp % 2
                for q4 in range(4):
                    cs_ = slice(128 * q4, 128 * q4 + 128)

                    def piece(cs_=cs_, bV=bV, bO=bO, hp=hp, q4=q4):
                        P.op("dve", lambda e: e.reciprocal(out=pt1[:, cs_], in_=ps[bO][:, cs_]), reads=[("ps", bO)],
                             writes=[("pt1", q4)])
                        P.op("dve", TT(yattn[:, hp, 1 + 128 * q4:1 + 128 * q4 + 128], ps[bV][:, cs_], pt1[:, cs_], ALU.mult),
                             reads=[("ps", bV), ("pt1", q4)], writes=[("yattn", hp)])
                    pending.append(piece)

            allsteps = [(hp, i) for hp in range(8) for i in range(nst)]
            LAG = 2
            for idx in range(len(allsteps) + LAG):
                if idx < len(allsteps):
                    hp, i = allsteps[idx]
                    if i == 0 and hp + 1 < 8:
                        load_gp(hp + 1)
                    emit_S(hp, i, 0)
                    emit_S(hp, i, 1)
                if idx >= LAG:
                    hp2, i2 = allsteps[idx - LAG]
                    emit_PV(hp2, i2, 0)
                    emit_PV(hp2, i2, 1)
                    if i2 == nst - 1:
                        emit_norm(hp2)
                if pending:
                    pending.pop(0)()
            while pending:
                pending.pop(0)()
            for h in range(16):
                hp, pi = h // 2, h % 2
                pr = slice(64 * pi, 64 * pi + 64)
                for which in range(2):
                    its = [(it, kj) for it, kj in enumerate(items_h)
                           if (kj[0] in ("post", "metapost")) == (which == 1)]
                    cV = 256 + which * 8 + hp
                    cO = 288 + which * 8 + hp
                    for pass_ in range(2):
                        for ii, (it, (kind, j)) in enumerate(its):
                            col = h * NHALO + it
                            firstf, lastf = ii == 0, ii == len(its) - 1
                            tj = 9 if j is None else j
                            if pass_ == 0:
                                P.op("pe", MM(ps[7][pr, cV:cV + 1], Vt[:, tj, 64 * h:64 * h + 64], pth[:, col:col + 1], firstf, lastf),
                                     reads=["pth"] + (Vall if ii == 0 else []), writes=[("ps", 7)])
                            else:
                                P.op("pe", MM(ps[7][pr, cO:cO + 1], onesb[:, 0:64], pth[:, col:col + 1], firstf, lastf),
                                     reads=["pth", "ones"], writes=[("ps", 7)])
            P.op("dve", lambda e: e.reciprocal(out=rech[:, :], in_=ps[7][:, 288:304]), reads=[("ps", 7)], writes=["rech"])
            P.op("dve", TT(yattn[:, :, 0], ps[7][:, 256:264], rech[:, 0:8], ALU.mult), reads=[("ps", 7), "rech"],
                 writes=[("yattn", c) for c in range(8)])
            P.op("dve", TT(yattn[:, :, NM - 1], ps[7][:, 264:272], rech[:, 8:16], ALU.mult), reads=[("ps", 7), "rech"],
                 writes=[("yattn", c) for c in range(8)])
            check_stop("P2b")
            P.fence("A")
            P.fence("B")

            P.cur_tag = "P3"
            ln1 = ln_begin([(0, 257), (257, 257)])
            for s in range(8):
                sl_, sres = next_slab()
                sv = slab[sl_].rearrange("p (kc n) -> p kc n", kc=16)
                for e_ in range(2):
                    oc = 2 * s + e_
                    if oc >= 2:
                        ln_chunk(ln1, oc - 2)
                    for half in range(2):
                        b = nb()
                        hs = slice(257 * half, 257 * half + 257)
                        for kc in range(DC):
                            src = ypool[:, kc, hs] if kc < 8 else yattn[:, kc - 8, hs]
                            rd = ("ypool", kc) if kc < 8 else ("yattn", kc - 8)
                            P.op("pe", MM(ps[b][:, 0:257], sv[:, kc, e_ * 128:(e_ + 1) * 128], src, kc == 0, kc == DC - 1),
                                 reads=[sres, rd], writes=[("ps", b)])
                        P.op("dve", STT(h0f[:, oc, hs], h0f[:, oc, hs], ALPHA, ps[b][:, 0:257], ALU.mult, ALU.add),
                             reads=[("ps", b), ("h0f", oc)], writes=[("h0f", oc)])
            check_stop("P3a")
            P.cur_tag = "LN1"
            ln_chunk(ln1, DC - 2)
            ln_chunk(ln1, DC - 1)
            ln_finish(ln1, C_LN1_G, C_LN1_B, True)
            check_stop("P3")
            P.fence("Y")

            P.cur_tag = "P4up"
            for part in range(2):
                P.cur_tag = "P4up"
                j0 = 0 if part == 0 else 22
                jr = range(0, 22) if part == 0 else range(22, NFC)
                def up_post(j, banks4):
                    zi = j % 2
                    for ag in range(2):
                        for half in range(2):
                            b = banks4[2 * ag + half]
                            hs = slice(257 * half, 257 * half + 257)
                            P.op("act", ACTF(zb[zi][:, ag, hs], ps[b][:, 0:257], AF.Identity, bias=cc(C_BUP + ag * NFC + j)),
                                 reads=[("ps", b), "cst"], writes=[("z", zi)])
                    P.op("dve", TS(zb[zi][:, :, NM - 1:NM], zb[zi][:, :, NM - 1:NM], fcol(F_FPOST), None, ALU.mult),
                         reads=[("z", zi), FL], writes=[("z", zi)])
                    outs = [(cab, "ca"), (cgb, "cg")]
                    for ag in range(2):
                        ob, on = outs[ag]
                        cj = ag * NFC + j
                        z = zb[zi][:, ag, :]
                        P.op("act", ACTF(ob[zi][:, :], z[:, 1:513], AF.Identity, bias=cc(C_CB + cj), scale=cc(C_CW + 86 + cj)),
                             reads=[("z", zi), "cst"], writes=[(on, zi)])
                        P.op("dve", STT(ob[zi][:, :], z[:, 0:512], cc(C_CW + cj), ob[zi][:, :], ALU.mult, ALU.add),
                             reads=[("z", zi), (on, zi), "cst"], writes=[(on, zi)])
                        P.op("dve", STT(ob[zi][:, :], z[:, 2:514], cc(C_CW + 172 + cj), ob[zi][:, :], ALU.mult, ALU.add),
                             reads=[("z", zi), (on, zi), "cst"], writes=[(on, zi)])
                    P.op("act", ACTF(ggb[zi][:, :], cgb[zi][:, :], AF.Gelu), reads=[("cg", zi)], writes=[("gg", zi)])
                    P.op("dve", TT(actb[:, j - j0, :], cab[zi][:, :], ggb[zi][:, :], ALU.mult), reads=[("ca", zi), ("gg", zi)],
                         writes=[("act", j - j0)])

                jlist = list(jr)
                if part == 0:
                    grp = []
                    for j in jlist[:2]:
                        sl_, sres = next_slab(NSLAB if j == jlist[0] else NSLAB - 1)
                        sv = slab[sl_].rearrange("p (kc n) -> p kc n", kc=16)
                        grp.append((j, sres, sv, [nb() for _ in range(4)]))
                    for kc in range(DC):
                        for (j, sres, sv, banks4) in grp:
                            for ag in range(2):
                                for half in range(2):
                                    b = banks4[2 * ag + half]
                                    hs = slice(257 * half, 257 * half + 257)
                                    P.op("pe", MM(ps[b][:, 0:257], sv[:, kc, ag * 128:(ag + 1) * 128], h1bf[:, kc, hs], kc == 0, kc == DC - 1),
                                         reads=[sres, ("h1bf", kc)], writes=[("ps", b)])
                    for (j, sres, sv, banks4) in grp:
                        up_post(j, banks4)
                    jlist = jlist[2:]
                for j in jlist:
                    sl_, sres = next_slab()
                    sv = slab[sl_].rearrange("p (kc n) -> p kc n", kc=16)
                    banks4 = []
                    for ag in range(2):
                        for half in range(2):
                            b = nb()
                            banks4.append(b)
                            hs = slice(257 * half, 257 * half + 257)
                            for kc in range(DC):
                                P.op("pe", MM(ps[b][:, 0:257], sv[:, kc, ag * 128:(ag + 1) * 128], h1bf[:, kc, hs], kc == 0, kc == DC - 1),
                                     reads=[sres, ("h1bf", kc)], writes=[("ps", b)])
                    up_post(j, banks4)
                P.cur_tag = "P4dn"
                k0, nk = (0, 22) if part == 0 else (22, 21)
                if part == 1:
                    ln2 = ln_begin([(1, 512)])
                def dn_evac(oc, b):
                    P.op("dve", STT(h0f[:, oc, 1:513], h0f[:, oc, 1:513], ALPHA if part == 0 else 1.0, ps[b][:, 0:512], ALU.mult, ALU.add),
                         reads=[("ps", b), ("h0f", oc)], writes=[("h0f", oc)])

                grp = []
                for oc in range(2):
                    sl_, sres = next_slab(NSLAB if oc == 0 else NSLAB - 1)
                    sv = slab[sl_][:, 0:nk * 128].rearrange("p (kc n) -> p kc n", kc=nk)
                    grp.append((oc, sres, sv, nb()))
                for kk in range(nk):
                    for (oc, sres, sv, b) in grp:
                        P.op("pe", MM(ps[b][:, 0:512], sv[:, kk, :], actb[:, kk, :], kk == 0, kk == nk - 1),
                             reads=[sres, ("act", kk)], writes=[("ps", b)])
                for (oc, sres, sv, b) in grp:
                    dn_evac(oc, b)
                for oc in range(2, DC):
                    if part == 1:
                        ln_chunk(ln2, oc - 2)
                    b = nb()
                    sl_, sres = next_slab()
                    sv = slab[sl_][:, 0:nk * 128].rearrange("p (kc n) -> p kc n", kc=nk)
                    for kk in range(nk):
                        P.op("pe", MM(ps[b][:, 0:512], sv[:, kk, :], actb[:, kk, :], kk == 0, kk == nk - 1),
                             reads=[sres, ("act", kk)], writes=[("ps", b)])
                    dn_evac(oc, b)
            check_stop("P4")
            P.fence("Y")
            P.fence("B")

            P.cur_tag = "P5"
            ln_chunk(ln2, DC - 2)
            ln_chunk(ln2, DC - 1)
            ln_finish(ln2, C_LN2_G, C_LN2_B, False)
            check_stop("P5a")
            for tt in range(4):
                ob_ = tt % 2
                for g4 in range(4):
                    b = nb()
                    for k in range(4):
                        c = 4 * g4 + k
                        P.op("pe", TR(ps[b][:, k * 128:(k + 1) * 128], h0f[:, c, 1 + 128 * tt:1 + 128 * tt + 128], ident[:, :]),
                             reads=[("h0f", c), "ident"], writes=[("ps", b)])
                    evac(otile[ob_][:, 512 * g4:512 * g4 + 512], ps[b][:, :], [("ps", b)], [("otile", ob_, g4)])
                o = P.op("pool", DMA(y_d[u, 128 * tt:128 * tt + 128, :], otile[ob_][:, :]),
                         reads=[("otile", ob_, g4) for g4 in range(4)], dsem=("out", ob_))
                out_ops.append(o)
            P.fence("Y")
            P.fence("A")
          except _Stop:
            break

        finals = {}
        for o in out_ops:
            finals[o.dsem] = o
        P.emit(final_wait_ops=list(finals.values()))
    return nc


def _unit_table():
    units = []
    for core in range(NCORE):
        ps_, run = core // 4, core % 4
        for k in range(2):
            units.append((ps_, 2 * run + k, 8))
        for k in range(4):
            units.append((2 + ps_, 4 * run + k, 16))
    return units


def _status_value(stt, start, end):
    if stt == ST_I:
        return 0.0
    if stt == ST_S:
        return 0.0 if start else -BIG
    if stt == ST_E:
        return 0.0 if end else -BIG
    if stt == ST_NS:
        return -BIG if start else 0.0
    if stt == ST_NE:
        return -BIG if end else 0.0
    return -BIG


def _build_gp(rpb):
    H = rpb.shape[0]
    gp = np.zeros((H, 2, 64, 16, 64), np.float32)
    c = np.arange(64)
    cs = np.clip(c - 8, 0, 48)
    key = np.arange(64)
    ok = (key[:, None] >= cs[None, :]) & (key[:, None] < cs[None, :] + 16)
    dc = np.clip(key[:, None] - c[None, :] + 15, 0, 30)
    for rp in range(2):
        for di in range(16):
            dr = rp - (di - 7) + 7
            if 0 <= dr <= 14:
                vals = rpb[:, dr, :][:, dc]
                gp[:, rp, :, di, :] = np.where(ok[None], vals, np.float32(-BIG))
            else:
                gp[:, rp, :, di, :] = np.where(ok, np.float32(0.0), np.float32(-BIG))[None]
    return gp.reshape(H, 128, 1024)


def _prepare(x_prompt, x_sample, meta_tokens, ln_in_g, ln_in_b, w_in, w_pool, pool_scale, rpb, meta_bias,
             w_out, ln1_g, ln1_b, w_up, b_up, conv_w, conv_b, w_down, ln2_g, ln2_b, cores=None):
    f32 = np.float32
    xs = [np.asarray(x_prompt[0], f32), np.asarray(x_prompt[1], f32), np.asarray(x_sample[0], f32),
          np.asarray(x_sample[1], f32)]
    meta = np.asarray(meta_tokens, f32)
    rpb0 = np.asarray(rpb, f32)[0]
    mb0 = np.asarray(meta_bias, f32)[0]
    units = _unit_table()
    assert len(units) == NCORE * NU

    gp = _build_gp(rpb0)
    gp_pairs = np.ascontiguousarray(gp.reshape(8, 2, 128, 1024).transpose(0, 2, 1, 3).reshape(8, 128, 2048))
    cst = np.zeros((128, NCONST), f32)
    col = lambda v: np.asarray(v, f32).reshape(-1, 128).T
    cst[:, C_LNIN_G:C_LNIN_G + 16] = col(ln_in_g)
    cst[:, C_LNIN_B:C_LNIN_B + 16] = col(ln_in_b)
    cst[:, C_LN1_G:C_LN1_G + 16] = col(ln1_g)
    cst[:, C_LN1_B:C_LN1_B + 16] = col(ln1_b)
    cst[:, C_LN2_G:C_LN2_G + 16] = col(ln2_g)
    cst[:, C_LN2_B:C_LN2_B + 16] = col(ln2_b)
    cst[:, C_PSCALE:C_PSCALE + 8] = col(pool_scale)
    cst[:, C_BUP:C_BUP + 86] = col(b_up)
    cw = np.asarray(conv_w, f32)[0]
    for k in range(3):
        cst[:, C_CW + 86 * k:C_CW + 86 * k + 86] = col(cw[k])
    cst[:, C_CB:C_CB + 86] = col(conv_b)
    cst[:, C_METAB:C_METAB + 16] = -BIG
    cst[0:16, C_METAB:C_METAB + 16] = mb0.T
    items = halo_items()
    for h in range(16):
        for it, (kind, j) in enumerate(items):
            c_ = C_GH + h * NHALO + it
            if j is None:
                cst[:, c_] = -BIG
                cst[0:16, c_] = mb0[h]
            else:
                dl, cq = halo_geom(kind, j)
                cst[:, c_] = gp[h][:, (dl + 7) * 64 + cq]

    def unit_flags(start, end):
        fl = np.zeros((128, NFLAG), f32)
        for stt in range(6):
            for sbb in range(6):
                fl[0:64, F_BTAB + stt * 6 + sbb] = _status_value(stt, start, end)
                fl[64:128, F_BTAB + stt * 6 + sbb] = _status_value(sbb, start, end)
        for h in range(16):
            for it, (kind, j) in enumerate(items):
                if j is None:
                    continue
                k0 = 2 * j - 6
                c_ = F_FH + h * NHALO + it
                fl[0:64, c_] = _status_value(halo_status(kind, k0), start, end)
                fl[64:128, c_] = _status_value(halo_status(kind, k0 + 1), start, end)
        fl[:, F_FPOST] = 0.0 if end else 1.0
        for g in range(4):
            w = 2 ** (g + 1)
            half = w // 2
            for i in range(8):
                tau_own = 504 + i
                cnt = w
                if end and tau_own + half > 512:
                    cnt = 512 - tau_own + half
                fl[:, F_CORR + 8 * g + i] = f32(w) / f32(cnt)
        return fl

    in_maps = []
    for core in (range(NCORE) if cores is None else cores):
        xu = np.zeros((NU, TOK, D), f32)
        flg = np.zeros((NU, 128, NFLAG), f32)
        for k in range(NU):
            s, uu, nun = units[core * NU + k]
            X = xs[s]
            nrows = nun * 8
            r0 = 8 * uu
            for row in range(-6, 12):
                gr = r0 + row
                if 0 <= gr < nrows:
                    xu[k, (row + 6) * 64:(row + 7) * 64] = X[gr * 64:(gr + 1) * 64]
            if uu == 0:
                xu[k, 384 - 16:384] = meta
            xu[k, 1152:1168] = meta
            flg[k] = unit_flags(uu == 0, uu == nun - 1)
        in_maps.append({
            "xu": xu, "w_in": np.ascontiguousarray(np.asarray(w_in, f32)[0]),
            "w_out": np.ascontiguousarray(np.asarray(w_out, f32)[0]),
            "w_up": np.ascontiguousarray(np.asarray(w_up, f32)[0]),
            "w_down": np.ascontiguousarray(np.asarray(w_down, f32)[0]),
            "w_pool": np.ascontiguousarray(np.asarray(w_pool, f32)[0]),
            "gp": gp_pairs, "cst": cst, "flg": flg,
        })

    return in_maps, units


def kernel(**inputs):
    in_maps, units = _prepare(**inputs)
    nc = build_program()
    res = run_bass_kernel_spmd(nc, in_maps, core_ids=list(range(NCORE)))
    return _assemble(res, units)


def _assemble(res, units):
    f32 = np.float32
    yp = np.zeros((2, 4096, D), f32)
    ysm = np.zeros((2, 8192, D), f32)
    outs = [yp[0], yp[1], ysm[0], ysm[1]]
    for core in range(NCORE):
        y = np.asarray(res.results[core]["y"], f32)
        for k in range(NU):
            s, uu, nun = units[core * NU + k]
            outs[s][512 * uu:512 * uu + 512] = y[k]
    return (yp, ysm)
```
